# Optimizing a Trainium2 kernel written in Bass

```python
import math
import jax
import jax.numpy as jnp
from jax import lax
import numpy as np

D_MODEL = 1024
BATCH = 16
SEQ = 2048
DEPTH = 4

GRID_W = 64
CTX_LEN = 256
N_EVEN = (DEPTH + 1) // 2
N_ODD = DEPTH // 2
N_VRES = max(N_ODD - 1, 0)

EPS = 1e-6
MLA_HEADS = 8
MLA_Q_RANK = 256
MLA_KV_RANK = 256
MLA_NOPE = 64
MLA_ROPE = 32
MLA_QK = MLA_NOPE + MLA_ROPE
MLA_V = 64
AXIS_DIM = MLA_ROPE // 2
ROPE_THETA = 10000.0
Q_BLOCK = 128
CONV_CH = 512
CONV_WIDTH = 31
EVEN_SPLITS = [MLA_Q_RANK, MLA_Q_RANK + MLA_KV_RANK, MLA_Q_RANK + MLA_KV_RANK + MLA_ROPE]
EVEN_IN = MLA_Q_RANK + MLA_KV_RANK + MLA_ROPE + 2 * CONV_CH
EVEN_MIX = MLA_HEADS * MLA_V + CONV_CH
RW_HEAD = 64
RW_HEADS = D_MODEL // RW_HEAD
RW_DECAY_LORA = 64
RW_A_LORA = 64
RW_V_LORA = 32
RW_G_LORA = 160
RW_GN_EPS = 6.4e-4
RW_BRANCHES = 6
FFN_HIDDEN = -(-8 * D_MODEL // (3 * 256)) * 256

kernel_name = 'hybrid_mla_conformer_rwkv7_dit'


def rms_norm(x, g):
    xf = x.astype(jnp.float32)
    y = xf * lax.rsqrt(jnp.mean(xf * xf, axis=-1, keepdims=True) + EPS)
    return (y * g.astype(jnp.float32)).astype(x.dtype)


def layer_norm(x, g, b, eps=1e-5):
    xf = x.astype(jnp.float32)
    mu = jnp.mean(xf, axis=-1, keepdims=True)
    var = jnp.mean(jnp.square(xf - mu), axis=-1, keepdims=True)
    y = (xf - mu) * lax.rsqrt(var + eps)
    return (y * g.astype(jnp.float32) + b.astype(jnp.float32)).astype(x.dtype)


def ada_modulation(cond, w, b):
    m = jax.nn.silu(cond) @ w + b
    return [t[:, None, :] for t in jnp.split(m, 6, axis=-1)]


def swiglu(h, w1, w3, w2):
    return (jax.nn.silu(h @ w1) * (h @ w3)) @ w2


def axial_rope_tables(n):
    rows = n // GRID_W
    row = jnp.repeat(jnp.arange(rows, dtype=jnp.float32), GRID_W)
    col = jnp.tile(jnp.arange(GRID_W, dtype=jnp.float32), rows)
    inv = ROPE_THETA ** (-jnp.arange(0, AXIS_DIM, 2, dtype=jnp.float32) / AXIS_DIM)
    ang_r = row[:, None] * inv
    ang_c = col[:, None] * inv
    return (jnp.cos(ang_r), jnp.sin(ang_r), jnp.cos(ang_c), jnp.sin(ang_c))


def rotate(x, cos, sin):
    m = x.shape[-1] // 2
    x1, x2 = x[..., :m], x[..., m:]
    cos = cos[:, None, :].astype(x.dtype)
    sin = sin[:, None, :].astype(x.dtype)
    return jnp.concatenate([x1 * cos - x2 * sin, x2 * cos + x1 * sin], axis=-1)


def rope_latent(t, tabs):
    cr, sr, cc, sc = tabs
    t_nope = t[..., :MLA_NOPE]
    t_row = t[..., MLA_NOPE:MLA_NOPE + AXIS_DIM]
    t_col = t[..., MLA_NOPE + AXIS_DIM:]
    return jnp.concatenate([t_nope, rotate(t_row, cr, sr), rotate(t_col, cc, sc)], axis=-1)


def block_attention(q, k, v):
    b, lq, h, dq = q.shape
    dv = v.shape[-1]
    nb = lq // Q_BLOCK
    qb = jnp.moveaxis(q.reshape(b, nb, Q_BLOCK, h, dq), 1, 0)
    scale = dq ** -0.5

    def one_block(qblk):
        s = jnp.einsum('bqhd,bkhd->bhqk', qblk, k).astype(jnp.float32) * scale
        pr = jax.nn.softmax(s, axis=-1).astype(v.dtype)
        return jnp.einsum('bhqk,bkhd->bqhd', pr, v)

    o = lax.map(one_block, qb)
    return jnp.moveaxis(o, 0, 1).reshape(b, lq, h * dv)


def conformer_conv(u, p):
    a, gate = jnp.split(u, 2, axis=-1)
    y = a * jax.nn.sigmoid(gate)
    y = lax.conv_general_dilated(
        y, p['conv_w'][:, None, :], window_strides=(1,),
        padding=[(CONV_WIDTH // 2, CONV_WIDTH // 2)],
        dimension_numbers=('NWC', 'WIO', 'NWC'),
        feature_group_count=CONV_CH) + p['conv_b']
    y = layer_norm(y, p['conv_ln_g'], p['conv_ln_b'])
    return jax.nn.silu(y)


def even_mixer(a_ctx, a_lat, p, rope, need_ctx_out):
    def mla_q(zq):
        b, n = zq.shape[:2]
        q = (rms_norm(zq, p['q_norm']) @ p['wq_b']).reshape(b, n, MLA_HEADS, MLA_QK)
        return rms_norm(q, p['q_qk'])

    def mla_kv(zkv, zr):
        b, n = zkv.shape[:2]
        kv = (rms_norm(zkv, p['kv_norm']) @ p['wkv_b']).reshape(b, n, MLA_HEADS, MLA_NOPE + MLA_V)
        k_rope = jnp.broadcast_to(zr[:, :, None, :], (b, n, MLA_HEADS, MLA_ROPE))
        k = rms_norm(jnp.concatenate([kv[..., :MLA_NOPE], k_rope], axis=-1), p['k_qk'])
        return k, kv[..., MLA_NOPE:]

    def merge_out(o_attn, zconv):
        return jnp.concatenate([o_attn, conformer_conv(zconv, p)], axis=-1) @ p['w_out']

    zq_c, zkv_c, zr_c, zconv_c = jnp.split(a_ctx @ p['w_in'], EVEN_SPLITS, axis=-1)
    zq_l, zkv_l, zr_l, zconv_l = jnp.split(a_lat @ p['w_in'], EVEN_SPLITS, axis=-1)
    k_c, v_c = mla_kv(zkv_c, zr_c)
    k_l, v_l = mla_kv(zkv_l, zr_l)
    k_l = rope_latent(k_l, rope)
    q_l = rope_latent(mla_q(zq_l), rope)
    o_l = block_attention(q_l, jnp.concatenate([k_c, k_l], axis=1), jnp.concatenate([v_c, v_l], axis=1))
    out_lat = merge_out(o_l, zconv_l)
    out_ctx = merge_out(block_attention(mla_q(zq_c), k_c, v_c), zconv_c) if need_ctx_out else None
    return out_ctx, out_lat


def rwkv_features(x, p, v_first, need_out):
    b, n, _ = x.shape
    heads = lambda t: t.reshape(b, n, RW_HEADS, RW_HEAD)
    zeros = jnp.zeros_like(x[:, :1])
    d_prev = jnp.concatenate([zeros, x[:, :-1]], axis=1) - x
    d_next = jnp.concatenate([x[:, 1:], zeros], axis=1) - x
    mix = lambda i: x + d_prev * p['mu'][0, i] + d_next * p['mu'][1, i]
    xw, xk, xv, xa = mix(1), mix(2), mix(3), mix(4)
    k = xk @ p['wk']
    v = xv @ p['wv']
    if v_first is not None:
        v = v + (v_first - v) * jax.nn.sigmoid(p['v0'] + (xv @ p['v1']) @ p['v2'])
    kkf = heads(k * p['k_k']).astype(jnp.float32)
    kk = kkf / jnp.maximum(jnp.sqrt(jnp.sum(kkf * kkf, axis=-1, keepdims=True)), 1e-12)
    dirs = []
    for d in range(2):
        w_log = -jax.nn.softplus(-(p['w0'][d] + jnp.tanh(xw @ p['w1'][d]) @ p['w2'][d])) - 0.5
        decay = jnp.exp(-jnp.exp(w_log.astype(jnp.float32)))
        a_rate = jax.nn.sigmoid(p['a0'][d] + (xa @ p['a1'][d]) @ p['a2'][d])
        k_d = k * (1 + (a_rate - 1) * p['k_a'])
        dirs.append((heads(decay), heads(k_d), kk * heads(a_rate).astype(jnp.float32)))
    feats = {'v': v, 'vh': heads(v), 'kk': kk, 'dirs': dirs, 'r': None, 'g': None}
    if need_out:
        feats['r'] = heads(mix(0) @ p['wr'])
        feats['g'] = jax.nn.sigmoid(mix(5) @ p['g1']) @ p['g2']
    return feats


def wkv_scan(s0, decay, k, v, kk, b, r, reverse):
    tm = lambda t: jnp.moveaxis(t.astype(jnp.float32), 1, 0)
    emit = r is not None
    xs = (tm(decay), tm(k), tm(v), tm(kk), tm(b)) + ((tm(r),) if emit else ())

    def step(S, inp):
        w_t, k_t, v_t, kk_t, b_t = inp[:5]
        sa = jnp.einsum('bhvk,bhk->bhv', S, kk_t)
        S = S * w_t[:, :, None, :] - sa[..., None] * b_t[:, :, None, :] + v_t[..., None] * k_t[:, :, None, :]
        y = jnp.einsum('bhvk,bhk->bhv', S, inp[5]) if emit else None
        return S, y

    S, ys = lax.scan(step, s0, xs, reverse=reverse)
    return S, (jnp.moveaxis(ys, 0, 1) if emit else None)


def rwkv_readout(y, f, p):
    b, n = y.shape[:2]
    mu = jnp.mean(y, axis=-1, keepdims=True)
    var = jnp.mean(jnp.square(y - mu), axis=-1, keepdims=True)
    yn = ((y - mu) * lax.rsqrt(var + RW_GN_EPS)).reshape(b, n, D_MODEL)
    yn = yn * p['ln_g'].astype(jnp.float32) + p['ln_b'].astype(jnp.float32)
    rf = f['r'].astype(jnp.float32)
    rk = p['r_k'].astype(jnp.float32)
    kmix = f['dirs'][0][1].astype(jnp.float32) * rk[0] + f['dirs'][1][1].astype(jnp.float32) * rk[1]
    coef = jnp.sum(rf * kmix, axis=-1, keepdims=True)
    bonus = (coef * f['vh'].astype(jnp.float32)).reshape(b, n, D_MODEL)
    out = ((yn + bonus) * f['g'].astype(jnp.float32)).astype(f['g'].dtype)
    return out @ p['wo']


def odd_mixer(a_ctx, a_lat, p, v_first, need_ctx_out):
    vf_ctx, vf_lat = v_first if v_first is not None else (None, None)
    fc = rwkv_features(a_ctx, p, vf_ctx, need_ctx_out)
    fl = rwkv_features(a_lat, p, vf_lat, True)
    s0 = jnp.zeros((a_lat.shape[0], RW_HEADS, RW_HEAD, RW_HEAD), jnp.float32)
    y_ctx, y_lat = [], []
    for d in range(2):
        rev = d == 1
        dec_c, k_c, b_c = fc['dirs'][d]
        dec_l, k_l, b_l = fl['dirs'][d]
        s_ctx, yc = wkv_scan(s0, dec_c, k_c, fc['vh'], fc['kk'], b_c, fc['r'], rev)
        _, yl = wkv_scan(s_ctx, dec_l, k_l, fl['vh'], fl['kk'], b_l, fl['r'], rev)
        y_ctx.append(yc)
        y_lat.append(yl)
    out_lat = rwkv_readout(y_lat[0] + y_lat[1], fl, p)
    out_ctx = rwkv_readout(y_ctx[0] + y_ctx[1], fc, p) if need_ctx_out else None
    new_vf = v_first if v_first is not None else (fc['v'], fl['v'])
    return out_ctx, out_lat, new_vf


def setup_inputs(seed: int = 0) -> dict:
    key = jax.random.key(seed)
    ks = iter(jax.random.split(key, 64))
    nrm = lambda shape, scale: jax.random.normal(next(ks), shape, jnp.float32) * scale
    gain = lambda shape: 1.0 + nrm(shape, 0.02)
    uni = lambda shape, lo, hi: jax.random.uniform(next(ks), shape, jnp.float32, lo, hi)
    D = D_MODEL
    return {
        'x': nrm((BATCH, SEQ, D), 1.0),
        'c': nrm((BATCH, D), 1.0),
        'ctx': nrm((BATCH, CTX_LEN, D), 1.0),
        'c_ctx': nrm((D,), 1.0),
        'ada_w': nrm((DEPTH, D, 6 * D), 0.5 * D ** -0.5),
        'ada_b': nrm((DEPTH, 6 * D), 0.02),
        'norm_mix': gain((DEPTH, D)),
        'norm_ffn': gain((DEPTH, D)),
        'ffn_w1': nrm((DEPTH, D, FFN_HIDDEN), D ** -0.5),
        'ffn_w3': nrm((DEPTH, D, FFN_HIDDEN), D ** -0.5),
        'ffn_w2': nrm((DEPTH, FFN_HIDDEN, D), FFN_HIDDEN ** -0.5),
        'even_w_in': nrm((N_EVEN, D, EVEN_IN), D ** -0.5),
        'mla_q_norm': gain((N_EVEN, MLA_Q_RANK)),
        'mla_wq_b': nrm((N_EVEN, MLA_Q_RANK, MLA_HEADS * MLA_QK), MLA_Q_RANK ** -0.5),
        'mla_kv_norm': gain((N_EVEN, MLA_KV_RANK)),
        'mla_wkv_b': nrm((N_EVEN, MLA_KV_RANK, MLA_HEADS * (MLA_NOPE + MLA_V)), MLA_KV_RANK ** -0.5),
        'mla_q_qk': gain((N_EVEN, MLA_QK)),
        'mla_k_qk': gain((N_EVEN, MLA_QK)),
        'conv_w': nrm((N_EVEN, CONV_WIDTH, CONV_CH), CONV_WIDTH ** -0.5),
        'conv_b': nrm((N_EVEN, CONV_CH), 0.02),
        'conv_ln_g': gain((N_EVEN, CONV_CH)),
        'conv_ln_b': nrm((N_EVEN, CONV_CH), 0.02),
        'even_w_out': nrm((N_EVEN, EVEN_MIX, D), EVEN_MIX ** -0.5),
        'rw_mu': uni((N_ODD, 2, RW_BRANCHES, D), 0.0, 0.5),
        'rw_wr': nrm((N_ODD, D, D), D ** -0.5),
        'rw_wk': nrm((N_ODD, D, D), D ** -0.5),
        'rw_wv': nrm((N_ODD, D, D), D ** -0.5),
        'rw_w0': uni((N_ODD, 2, D), -6.0, -1.0),
        'rw_w1': nrm((N_ODD, 2, D, RW_DECAY_LORA), D ** -0.5),
        'rw_w2': nrm((N_ODD, 2, RW_DECAY_LORA, D), 0.5 * RW_DECAY_LORA ** -0.5),
        'rw_a0': nrm((N_ODD, 2, D), 0.5),
        'rw_a1': nrm((N_ODD, 2, D, RW_A_LORA), D ** -0.5),
        'rw_a2': nrm((N_ODD, 2, RW_A_LORA, D), 0.5 * RW_A_LORA ** -0.5),
        'rw_v0': nrm((N_VRES, D), 0.5),
        'rw_v1': nrm((N_VRES, D, RW_V_LORA), D ** -0.5),
        'rw_v2': nrm((N_VRES, RW_V_LORA, D), 0.5 * RW_V_LORA ** -0.5),
        'rw_k_k': 1.0 + nrm((N_ODD, D), 0.1),
        'rw_k_a': 1.0 + nrm((N_ODD, D), 0.1),
        'rw_r_k': nrm((N_ODD, 2, RW_HEADS, RW_HEAD), 0.1),
        'rw_g1': nrm((N_ODD, D, RW_G_LORA), D ** -0.5),
        'rw_g2': nrm((N_ODD, RW_G_LORA, D), RW_G_LORA ** -0.5),
        'rw_ln_g': gain((N_ODD, D)),
        'rw_ln_b': nrm((N_ODD, D), 0.02),
        'rw_wo': nrm((N_ODD, D, D), D ** -0.5),
    }


def reference(x, c, ctx, c_ctx, ada_w, ada_b, norm_mix, norm_ffn, ffn_w1, ffn_w3, ffn_w2,
              even_w_in, mla_q_norm, mla_wq_b, mla_kv_norm, mla_wkv_b, mla_q_qk, mla_k_qk,
              conv_w, conv_b, conv_ln_g, conv_ln_b, even_w_out,
              rw_mu, rw_wr, rw_wk, rw_wv, rw_w0, rw_w1, rw_w2, rw_a0, rw_a1, rw_a2,
              rw_v0, rw_v1, rw_v2, rw_k_k, rw_k_a, rw_r_k, rw_g1, rw_g2, rw_ln_g, rw_ln_b, rw_wo):
    rope = axial_rope_tables(x.shape[1])
    h_ctx, h_lat = ctx, x
    v_first = None
    for i in range(DEPTH):
        last = i == DEPTH - 1
        j = i // 2
        sh1_l, sc1_l, g1_l, sh2_l, sc2_l, g2_l = ada_modulation(c, ada_w[i], ada_b[i])
        sh1_c, sc1_c, g1_c, sh2_c, sc2_c, g2_c = ada_modulation(c_ctx[None, :], ada_w[i], ada_b[i])
        a_lat = rms_norm(h_lat, norm_mix[i]) * (1 + sc1_l) + sh1_l
        a_ctx = rms_norm(h_ctx, norm_mix[i]) * (1 + sc1_c) + sh1_c
        if i % 2 == 0:
            p = {'w_in': even_w_in[j], 'q_norm': mla_q_norm[j], 'wq_b': mla_wq_b[j],
                 'kv_norm': mla_kv_norm[j], 'wkv_b': mla_wkv_b[j], 'q_qk': mla_q_qk[j],
                 'k_qk': mla_k_qk[j], 'conv_w': conv_w[j], 'conv_b': conv_b[j],
                 'conv_ln_g': conv_ln_g[j], 'conv_ln_b': conv_ln_b[j], 'w_out': even_w_out[j]}
            o_ctx, o_lat = even_mixer(a_ctx, a_lat, p, rope, not last)
        else:
            p = {'mu': rw_mu[j], 'wr': rw_wr[j], 'wk': rw_wk[j], 'wv': rw_wv[j],
                 'w0': rw_w0[j], 'w1': rw_w1[j], 'w2': rw_w2[j],
                 'a0': rw_a0[j], 'a1': rw_a1[j], 'a2': rw_a2[j],
                 'k_k': rw_k_k[j], 'k_a': rw_k_a[j], 'r_k': rw_r_k[j],
                 'g1': rw_g1[j], 'g2': rw_g2[j], 'ln_g': rw_ln_g[j], 'ln_b': rw_ln_b[j], 'wo': rw_wo[j]}
            if j > 0:
                p['v0'] = rw_v0[j - 1]
                p['v1'] = rw_v1[j - 1]
                p['v2'] = rw_v2[j - 1]
            o_ctx, o_lat, v_first = odd_mixer(a_ctx, a_lat, p, v_first, not last)
        h_lat = h_lat + g1_l * o_lat
        f_lat = rms_norm(h_lat, norm_ffn[i]) * (1 + sc2_l) + sh2_l
        h_lat = h_lat + g2_l * swiglu(f_lat, ffn_w1[i], ffn_w3[i], ffn_w2[i])
        if not last:
            h_ctx = h_ctx + g1_c * o_ctx
            f_ctx = rms_norm(h_ctx, norm_ffn[i]) * (1 + sc2_c) + sh2_c
            h_ctx = h_ctx + g2_c * swiglu(f_ctx, ffn_w1[i], ffn_w3[i], ffn_w2[i])
    return h_lat
```

```python
import math
from contextlib import ExitStack
import numpy as np
import concourse.bass as bass
import concourse.mybir as mybir
from concourse.bass_utils import run_bass_kernel_spmd

F32 = mybir.dt.float32
BF16 = mybir.dt.bfloat16
ALU = mybir.AluOpType
AF = mybir.ActivationFunctionType
AX = mybir.AxisListType

D = 1024
TC = 256
TL = 2048
TT = TC + TL
NTILE = TT // 128
EPS = 1e-6
DEPTH = 4
FH = 2816
NHC = FH // 128
EV_IN = 1568
GW = 64

class Track:
    __slots__ = ("w", "r", "name", "excl")

    def __init__(self, name=""):
        self.w = []
        self.r = []
        self.name = name
        self.excl = False


class T:
    def __init__(self, h, name=""):
        self.h = h
        self.tr = Track(name)

    def __getitem__(self, idx):
        return self.h[idx]


def _trk(x):
    if isinstance(x, Track):
        return x
    return x.tr


class FW:
    SEM_ROT = 24000

    def __init__(self, nc):
        self.nc = nc
        self.eng = {"pe": nc.tensor, "act": nc.scalar, "dve": nc.vector, "pool": nc.gpsimd, "sp": nc.sync}
        self.sems = {}
        self.cnt = {}
        self.seen = {e: {} for e in self.eng}
        self.nsem = 0
        for e in ("pe", "act", "dve", "pool"):
            self._new_sem(e)
        self.dma_sems = {}
        self.dma_rr = {}
        for q in ("sp", "act", "pool"):
            self.dma_sems[q] = [[self._alloc_sem("dma_%s_%d" % (q, i)), 0] for i in range(4 if q == "sp" else 2)]
            self.dma_rr[q] = 0
        self.ninst = {e: 0 for e in self.eng}
        self.pe_delay = False
        self.dly = None

    def _alloc_sem(self, name):
        self.nsem += 1
        cm = self.nc.semaphore(name)
        return cm.__enter__()

    def _new_sem(self, e):
        self.sems[e] = self._alloc_sem("s_%s_%d" % (e, self.nsem))
        self.cnt[e] = 0

    def _wait(self, e, ticket):
        sem, val, src = ticket
        key = id(sem)
        if self.seen[e].get(key, 0) >= val:
            return
        self.eng[e].wait_ge(sem, val)
        self.seen[e][key] = val
        self.ninst[e] += 1
        if self.pe_delay and src == "pe" and self.dly is not None:
            if e == "dve":
                self.eng[e].memset(self.dly[0][:, 0:256], 0.0)
            elif e == "act":
                self.eng[e].activation(out=self.dly[1][:, 0:192], in_=self.dly[2][:, 0:192], func=AF.Copy)

    def _deps(self, e, reads, writes, is_dma=False, disjoint=False):
        deps = []
        for r in reads:
            t = _trk(r)
            deps.extend(t.w)
            if t.excl:
                deps.extend(t.r)
        for w in writes:
            t = _trk(w)
            if not (disjoint and all(x[2] == "dma" for x in t.w)):
                deps.extend(t.w)
            deps.extend(t.r)
        for d in deps:
            if d[2] == e and e == "pe" and not is_dma:
                continue
            self._wait(e, d)

    @staticmethod
    def _compact(lst):
        best = {}
        for tk in lst:
            k = id(tk[0])
            if k not in best or best[k][1] < tk[1]:
                best[k] = tk
        return list(best.values())

    def _commit(self, ticket, reads, writes, disjoint=False):
        for w in writes:
            t = _trk(w)
            if disjoint and all(x[2] == "dma" for x in t.w) and not t.r:
                t.w = self._compact(t.w + [ticket])
            else:
                t.w = [ticket]
            t.r = []
        for r in reads:
            t = _trk(r)
            if ticket in t.w:
                continue
            t.r.append(ticket)
            if len(t.r) > 48:
                t.r = self._compact(t.r)

    def op(self, e, fn, reads=(), writes=()):
        self._deps(e, reads, writes)
        if self.cnt[e] >= self.SEM_ROT:
            self._new_sem(e)
        ins = fn(self.eng[e])
        self.cnt[e] += 1
        ins.then_inc(self.sems[e], 1)
        ticket = (self.sems[e], self.cnt[e], e)
        self._commit(ticket, reads, writes)
        self.ninst[e] += 1
        return ticket

    def dma(self, q, out, in_, reads=(), writes=(), disjoint=False, **kw):
        slot = self.dma_sems[q][self.dma_rr[q]]
        self.dma_rr[q] = (self.dma_rr[q] + 1) % len(self.dma_sems[q])
        sem, cnt = slot
        if cnt > 0:
            self._wait(q, (sem, cnt, "dma"))
        self._deps(q, reads, writes, is_dma=True, disjoint=disjoint)
        ins = self.eng[q].dma_start(out=out, in_=in_, allow_slow_non_contiguous=True, **kw)
        slot[1] = cnt + 16
        ins.then_inc(sem, 16)
        ticket = (sem, slot[1], "dma")
        self._commit(ticket, reads, writes, disjoint=disjoint)
        self.ninst[q] += 1
        return ticket

    def barrier(self):
        tickets = []
        for e in ("pe", "act", "dve", "pool"):
            if self.cnt[e] > 0:
                tickets.append((self.sems[e], self.cnt[e], e))
        for q in self.dma_sems:
            for sem, cnt in self.dma_sems[q]:
                if cnt > 0:
                    tickets.append((sem, cnt, "dma"))
        for e in self.eng:
            for tk in tickets:
                if tk[2] == e and e == "pe":
                    continue
                self._wait(e, tk)

    def finish(self, out_tracks):
        for t in out_tracks:
            t = _trk(t)
            for x in t.w:
                self._wait("sp", x)
        self.barrier()


def vec_layout():
    lay = {}
    off = 0

    def add(name, n):
        nonlocal off
        lay[name] = (off, n)
        off += n
    for i in range(DEPTH):
        add("nmix%d" % i, 8)
        add("nffn%d" % i, 8)
    for j in range(2):
        add("qn%d" % j, 2)
        add("kvn%d" % j, 2)
        add("convw%d" % j, 4 * 31)
        add("convb%d" % j, 4)
        add("clng%d" % j, 4)
        add("clnb%d" % j, 4)
        add("mu%d" % j, 2 * 6 * 8)
    return lay, off


W_NAMES = ["ada_w", "ada_b", "ffn_w1", "ffn_w3", "ffn_w2", "even_w_in", "mla_wq_b", "mla_wkv_b", "mla_q_qk",
           "mla_k_qk", "even_w_out", "rw_wr", "rw_wk", "rw_wv", "rw_w0", "rw_w1", "rw_w2", "rw_a0", "rw_a1",
           "rw_a2", "rw_v0", "rw_v1", "rw_v2", "rw_k_k", "rw_k_a", "rw_r_k", "rw_g1", "rw_g2", "rw_ln_g",
           "rw_ln_b", "rw_wo"]
W_SHAPES = {
    "ada_w": [4, 1024, 6144], "ada_b": [4, 6144], "ffn_w1": [4, 1024, 2816], "ffn_w3": [4, 1024, 2816],
    "ffn_w2": [4, 2816, 1024], "even_w_in": [2, 1024, 1568], "mla_wq_b": [2, 256, 768], "mla_wkv_b": [2, 256, 1024],
    "mla_q_qk": [2, 96], "mla_k_qk": [2, 96], "even_w_out": [2, 1024, 1024], "rw_wr": [2, 1024, 1024],
    "rw_wk": [2, 1024, 1024], "rw_wv": [2, 1024, 1024], "rw_w0": [2, 2, 1024], "rw_w1": [2, 2, 1024, 64],
    "rw_w2": [2, 2, 64, 1024], "rw_a0": [2, 2, 1024], "rw_a1": [2, 2, 1024, 64], "rw_a2": [2, 2, 64, 1024],
    "rw_v0": [1, 1024], "rw_v1": [1, 1024, 32], "rw_v2": [1, 32, 1024], "rw_k_k": [2, 1024], "rw_k_a": [2, 1024],
    "rw_r_k": [2, 2, 1024], "rw_g1": [2, 1024, 160], "rw_g2": [2, 160, 1024], "rw_ln_g": [2, 1024],
    "rw_ln_b": [2, 1024], "rw_wo": [2, 1024, 1024],
}


class Builder:
    def __init__(self, nlayers=DEPTH, stop_after=None):
        self.nlayers = nlayers
        import os
        self.stop_after = os.environ.get('STOP_AFTER')
        nc = bass.Bass("TRN2", target_bir_lowering=False)
        self.nc = nc
        self.fw = FW(nc)
        self.uid = 0
        self.lay, self.nv = vec_layout()
        inp = lambda name, shape, dt=F32: nc.dram_tensor(name, shape, dt, kind="ExternalInput").ap()
        self.hin = inp("hin", [2, TT, D])
        self.condT = inp("condT", [128, 8, 3])
        self.cst = inp("cst", [128, 13 * 128])
        self.rope = inp("rope", [TT, 64])
        self.vecs = inp("vecs", [128, self.nv])
        self.W = {n: inp(n, W_SHAPES[n]) for n in W_NAMES}
        self.hin2 = inp("hin2", [2, TT, D]) if os.environ.get("DBG_RELOAD") else None
        self.hout = nc.dram_tensor("hout", [2, TT, D], F32, kind="ExternalOutput").ap()
        scr = lambda name, shape, dt=F32: nc.dram_tensor(name, shape, dt).ap()
        self.hbuf = self.hout
        self.qT_d = scr("qT_d", [2, 8, 96, TT], BF16)
        self.kT_d = scr("kT_d", [2, 8, 96, TT], BF16)
        self.V_d = scr("V_d", [2, 8, 128, NTILE, 68], BF16)
        self.YW = 15 + TC + 15 + 15 + TL + 15
        self.yT_d = scr("yT_d", [2, 4, 128, self.YW], F32)
        self.mixT_d = scr("mixT_d", [2, 4, 128, TT], BF16)
        self.AW = 1 + TC + 1 + 1 + TL + 1
        self.aT_d = scr("aT_d", [2, 8, 128, self.AW], F32)
        names = ["r", "v", "a", "g", "lw0", "lw1", "kd0", "kd1", "b0", "b1", "y0", "y1", "vf"]
        if os.environ.get("DBG_S"):
            self.S = {n: nc.dram_tensor("s_" + n, [2, TT, D], F32, kind="ExternalOutput").ap() for n in names}
        else:
            self.S = {n: scr("s_" + n, [2, TT, D], F32) for n in names}
        self.dtrk = {}
        self.ph = None
        self.PP = [nc.alloc_psum_tensor("pp%d" % i, [128, 1024], F32) for i in range(4)]
        self.BT = [Track("bank%d" % k) for k in range(8)]
        for t in self.BT:
            t.excl = True

    def dt(self, *key):
        if key not in self.dtrk:
            self.dtrk[key] = Track(str(key))
        return self.dtrk[key]

    def A(self, name, shape, dt, stack=None):
        self.uid += 1
        st = stack if stack is not None else self.ph
        h = st.enter_context(self.nc.sbuf_tensor("%s_%d" % (name, self.uid), shape, dt))
        return T(h, name)

    def bank(self, k):
        return self.PP[k // 2][:, (k % 2) * 512:(k % 2 + 1) * 512]

    def bank2(self, m):
        return self.PP[m][:, :]

    def vcol(self, name, i=0, n=1):
        o, _ = self.lay[name]
        return self.vec[:, o + i:o + i + n]

    def rstd(self, ss_ap, out_ap, n, eps, trk):
        fw = self.fw
        fw.op("dve", lambda e: e.tensor_scalar(out=out_ap, in0=ss_ap, scalar1=1.0 / n, scalar2=eps, op0=ALU.mult, op1=ALU.add),
              reads=[trk], writes=[trk])
        fw.op("act", lambda e: e.activation(out=out_ap, in_=out_ap, func=AF.Sqrt), reads=[trk], writes=[trk])
        fw.op("dve", lambda e: e.reciprocal(out=out_ap, in_=out_ap), reads=[trk], writes=[trk])

    def load_w(self, dst, src, KC, N, eng="pool", rows=128, col0=0):
        fw = self.fw
        SW = 1408
        for kc in range(KC):
            for n0 in range(0, N, SW):
                n1 = min(N, n0 + SW)
                st = self.stg[self.stg_i % 2]
                self.stg_i += 1
                fw.dma("sp", st[0:rows, 0:n1 - n0], src[kc * rows:(kc + 1) * rows, n0:n1], writes=[st])
                if eng == "act":
                    fw.op(eng, lambda e: e.activation(out=dst[0:rows, kc, col0 + n0:col0 + n1], in_=st[0:rows, 0:n1 - n0], func=AF.Copy), reads=[st], writes=[dst])
                else:
                    fw.op(eng, lambda e: e.tensor_copy(out=dst[0:rows, kc, col0 + n0:col0 + n1], in_=st[0:rows, 0:n1 - n0]), reads=[st], writes=[dst])

    def norm_T(self, hs, scl, shf, j, dst_ap, dst_trk, dt):
        fw = self.fw
        st, junk = self.n_st, self.n_junk
        fw.op("act", lambda e: e.activation(out=junk[:], in_=hs[:], func=AF.Square, accum_out=st[:, 0:1]),
              reads=[hs], writes=[junk, st])
        self.rstd(st[:, 0:1], st[:, 1:2], D, EPS, st)
        if dt == BF16:
            xn = self.n_xnb
            pv = self.bank(0).bitcast(BF16).rearrange("p (a b) -> p a b", a=8)
            ident = self.identb
            ptr = [self.BT[0]]
        else:
            xn = self.n_xnf
            pv = self.bank2(0).rearrange("p (a b) -> p a b", a=8)
            ident = self.identf
            ptr = [self.BT[0], self.BT[1]]
        fw.op("dve", lambda e: e.tensor_scalar(out=xn[:], in0=hs[:], scalar1=st[:, 1:2], scalar2=None, op0=ALU.mult),
              reads=[hs, st], writes=[xn])
        for kc in range(8):
            fw.op("pe", lambda e: e.transpose(out=pv[:, kc, :], in_=xn[:, kc * 128:(kc + 1) * 128], identity=ident[:]),
                  reads=[xn, ident], writes=ptr)
        fw.op("dve", lambda e: e.tensor_tensor(out=dst_ap, in0=pv, in1=scl[:, :, j:j + 1].to_broadcast([128, 8, 128]), op=ALU.mult),
              reads=ptr + [scl], writes=[dst_trk])
        fw.op("dve", lambda e: e.tensor_tensor(out=dst_ap, in0=dst_ap, in1=shf[:, :, j:j + 1].to_broadcast([128, 8, 128]), op=ALU.add),
              reads=[shf, dst_trk], writes=[dst_trk])

    def resid_update(self, b, tile, yT, c0, hs=None):
        fw = self.fw
        if hs is None:
            hs = self.r_hs[tile % 2]
            fw.dma("sp", hs[:], self.hbuf[b, tile * 128:(tile + 1) * 128, :], reads=[self.dt("h", b, tile)], writes=[hs])
        pv = self.bank2(3)
        ptr = [self.BT[6], self.BT[7]]
        for fc in range(8):
            fw.op("pe", lambda e: e.transpose(out=pv[:, fc * 128:(fc + 1) * 128], in_=yT[:, fc, c0:c0 + 128], identity=self.identf[:]),
                  reads=[yT, self.identf], writes=ptr)
        fw.op("dve", lambda e: e.tensor_tensor(out=hs[:], in0=hs[:], in1=pv, op=ALU.add), reads=ptr + [hs], writes=[hs])
        fw.dma("pool", self.hbuf[b, tile * 128:(tile + 1) * 128, :], hs[:], reads=[hs], writes=[self.dt("h", b, tile)])

    def setup(self):
        fw, nc = self.fw, self.nc
        self.gs = ExitStack()
        G = lambda name, shape, dt: self.A(name, shape, dt, stack=self.gs)
        self.cstf = G("cstf", [128, 13 * 128], F32)
        fw.dma("sp", self.cstf[:], self.cst, writes=[self.cstf])
        self.identf = T(self.cstf[:, 0:128]); self.identf.tr = self.cstf.tr
        self.identb = G("identb", [128, 128], BF16)
        fw.op("dve", lambda e: e.tensor_copy(out=self.identb[:], in_=self.cstf[:, 0:128]), reads=[self.cstf], writes=[self.identb])
        self.vec = G("vec", [128, self.nv], F32)
        fw.dma("sp", self.vec[:], self.vecs, writes=[self.vec])
        self.scT = G("scT", [128, 8, 3], F32)
        fw.dma("sp", self.scT[:], self.condT, writes=[self.scT])
        fw.op("act", lambda e: e.activation(out=self.scT[:], in_=self.scT[:], func=AF.Silu), reads=[self.scT], writes=[self.scT])
        self.modT = G("modT", [128, 48, 3], F32)
        self.scl1 = G("scl1", [128, 8, 3], F32)
        self.scl2 = G("scl2", [128, 8, 3], F32)
        self.n_st = G("n_st", [128, 4], F32)
        self.n_junk = G("n_junk", [128, 1024], BF16)
        self.n_xnb = G("n_xnb", [128, 1024], BF16)
        self.stg = [G("stg%d" % i, [128, 1408], F32) for i in range(2)]
        self.stg_i = 0
        self.zero = G("zero", [128, 64], F32)
        fw.op("dve", lambda e: e.memset(self.zero[:], 0.0), writes=[self.zero])
        dl0 = G("dly0", [128, 256], F32)
        fw.op("dve", lambda e: e.memset(dl0[:], 0.0), writes=[dl0])
        fw.op("dve", lambda e: e.memset(self.n_junk[:], 0.0), writes=[self.n_junk])
        fw.dly = [dl0, self.n_xnb, self.n_junk]
        for b in range(2):
            for t in range(NTILE):
                fw.dma("sp" if t % 2 == 0 else "pool", self.hbuf[b, t * 128:(t + 1) * 128, :], self.hin[b, t * 128:(t + 1) * 128, :],
                       writes=[self.dt("h", b, t)])
        for b in range(2):
            for c in range(4):
                for o in (0, 15 + TC, 30 + TC, 30 + TC + 15 + TL):
                    fw.dma("pool", self.yT_d[b, c, :, o:o + 15], self.zero[:, 0:15], reads=[self.zero], writes=[self.dt("ypad")])
            for c in range(8):
                for o in (0, 1 + TC, 2 + TC, 3 + TC + TL):
                    fw.dma("pool", self.aT_d[b, c, :, o:o + 1], self.zero[:, 0:1], reads=[self.zero], writes=[self.dt("apad")])

    def reload(self):
        for b in range(2):
            for t in range(NTILE):
                self.fw.dma("sp", self.hbuf[b, t * 128:(t + 1) * 128, :], self.hin2[b, t * 128:(t + 1) * 128, :],
                            reads=[self.dt("h", b, t)], writes=[self.dt("h", b, t)])
        self.fw.barrier()

    def modulation(self, i):
        fw, nc = self.fw, self.nc
        fw.pe_delay = True
        with ExitStack() as ph:
            self.ph = ph
            mrow = self.A("mrow", [3, 6144], F32)
            adab = self.A("adab", [3, 6144], F32)
            wst = [self.A("wst%d" % k, [128, 8, 512], F32) for k in range(2)]
            fw.dma("pool", adab[:], self.W["ada_b"][i].partition_broadcast(3), writes=[adab])
            pm = self.bank(2)
            for cb in range(12):
                st = wst[cb % 2]
                fw.dma("sp", st[:], self.W["ada_w"][i, :, cb * 512:(cb + 1) * 512].rearrange("(kc p) n -> p kc n", p=128), writes=[st])
                for kc in range(8):
                    fw.op("pe", lambda e: e.matmul(pm[0:3, :], lhsT=self.scT[:, kc, :], rhs=st[:, kc, :], start=(kc == 0), stop=(kc == 7)),
                          reads=[self.scT, st], writes=[self.BT[2]])
                fw.op("dve", lambda e: e.tensor_tensor(out=mrow[:, cb * 512:(cb + 1) * 512], in0=pm[0:3, :], in1=adab[:, cb * 512:(cb + 1) * 512], op=ALU.add),
                      reads=[self.BT[2], adab], writes=[mrow])
            pt = self.bank(3)[:, 0:192].rearrange("p (c k) -> p c k", k=4)
            for c in range(48):
                fw.op("pe", lambda e: e.transpose(out=pt[:, c, 0:3], in_=mrow[0:3, c * 128:(c + 1) * 128], identity=self.identf[0:3, 0:3]),
                      reads=[mrow, self.identf], writes=[self.BT[3]])
            fw.op("dve", lambda e: e.tensor_copy(out=self.modT[:], in_=pt[:, :, 0:3]), reads=[self.BT[3]], writes=[self.modT])
            for (scl, vname, c0) in ((self.scl1, "nmix%d" % i, 8), (self.scl2, "nffn%d" % i, 32)):
                fw.op("dve", lambda e: e.tensor_scalar(out=scl[:], in0=self.modT[:, c0:c0 + 8, :], scalar1=1.0, scalar2=None, op0=ALU.add),
                      reads=[self.modT], writes=[scl])
                fw.op("dve", lambda e: e.tensor_tensor(out=scl[:], in0=scl[:], in1=self.vcol(vname, 0, 8).unsqueeze(2).to_broadcast([128, 8, 3]), op=ALU.mult),
                      reads=[scl, self.vec], writes=[scl])
            fw.barrier()
        fw.pe_delay = False
        self.shf1 = T(self.modT[:, 0:8, :]); self.shf1.tr = self.modT.tr
        self.gate1 = T(self.modT[:, 16:24, :]); self.gate1.tr = self.modT.tr
        self.shf2 = T(self.modT[:, 24:32, :]); self.shf2.tr = self.modT.tr
        self.gate2 = T(self.modT[:, 40:48, :]); self.gate2.tr = self.modT.tr

    def ffn(self, i, last):
        fw, nc = self.fw, self.nc
        with ExitStack() as ph:
            self.ph = ph
            w1b = self.A("w1b", [128, 8, FH], BF16)
            w3b = self.A("w3b", [128, 8, FH], BF16)
            w2b = self.A("w2b", [128, NHC, D], BF16)
            self.load_w(w1b, self.W["ffn_w1"][i], 8, FH, "pool")
            self.load_w(w3b, self.W["ffn_w3"][i], 8, FH, "act")
            self.load_w(w2b, self.W["ffn_w2"][i], NHC, D, "pool")
            TB = 256
            fT = self.A("fT", [128, 8, TB], BF16)
            gT = self.A("gT", [128, NHC, TB], BF16)
            yT = self.A("yT", [128, 8, TB], F32)
            sl = [self.A("sl%d" % k, [128, TB], F32) for k in range(2)]
            hs2 = [self.A("hs2_%d" % k, [128, 1024], F32) for k in range(2)]
            for b in range(2):
                for blk in range(TT // TB):
                    if last and blk == 0:
                        continue
                    j = 2 if blk == 0 else b
                    for tt in range(2):
                        tile = blk * 2 + tt
                        fw.dma("sp", hs2[tt][:], self.hbuf[b, tile * 128:(tile + 1) * 128, :], reads=[self.dt("h", b, tile)], writes=[hs2[tt]])
                        self.norm_T(hs2[tt], self.scl2, self.shf2, j, fT[:, :, tt * 128:(tt + 1) * 128], fT, BF16)
                    for hc in range(NHC):
                        k1 = 2 + (hc % 2)
                        k3 = 4 + (hc % 2)
                        p1 = self.bank(k1)[:, 0:TB]
                        p3 = self.bank(k3)[:, 0:TB]
                        for kc in range(8):
                            fw.op("pe", lambda e: e.matmul(p1, lhsT=w1b[:, kc, hc * 128:(hc + 1) * 128], rhs=fT[:, kc, :], start=(kc == 0), stop=(kc == 7)),
                                  reads=[w1b, fT], writes=[self.BT[k1]])
                        for kc in range(8):
                            fw.op("pe", lambda e: e.matmul(p3, lhsT=w3b[:, kc, hc * 128:(hc + 1) * 128], rhs=fT[:, kc, :], start=(kc == 0), stop=(kc == 7)),
                                  reads=[w3b, fT], writes=[self.BT[k3]])
                        s = sl[hc % 2]
                        fw.op("act", lambda e: e.activation(out=s[:], in_=p1, func=AF.Silu), reads=[self.BT[k1]], writes=[s])
                        fw.op("dve", lambda e: e.tensor_tensor(out=gT[:, hc, :], in0=s[:], in1=p3, op=ALU.mult),
                              reads=[s, self.BT[k3]], writes=[gT])
                    for fc in range(8):
                        ky = 2 + (fc % 2)
                        py = self.bank(ky)[:, 0:TB]
                        for hc in range(NHC):
                            fw.op("pe", lambda e: e.matmul(py, lhsT=w2b[:, hc, fc * 128:(fc + 1) * 128], rhs=gT[:, hc, :], start=(hc == 0), stop=(hc == NHC - 1)),
                                  reads=[w2b, gT], writes=[self.BT[ky]])
                        fw.op("dve", lambda e: e.tensor_scalar(out=yT[:, fc, :], in0=py, scalar1=self.gate2[:, fc, j:j + 1], scalar2=None, op0=ALU.mult),
                              reads=[self.BT[ky], self.gate2], writes=[yT])
                    for tt in range(2):
                        self.resid_update(b, blk * 2 + tt, yT, tt * 128, hs=hs2[tt])
            fw.barrier()

    def rope_apply(self, xf, rp, sw, t1, eng="dve"):
        fw = self.fw
        for (d0, s0) in ((0, 72), (8, 64), (16, 88), (24, 80)):
            fw.op("pool", lambda e: e.tensor_copy(out=sw[:, :, d0:d0 + 8], in_=xf[:, :, s0:s0 + 8]), reads=[xf], writes=[sw])
        fw.op("dve", lambda e: e.tensor_tensor(out=t1[:], in0=xf[:, :, 64:96], in1=rp[:, 0:32].unsqueeze(1).to_broadcast([128, 8, 32]), op=ALU.mult),
              reads=[xf, rp], writes=[t1])
        fw.op("dve", lambda e: e.tensor_tensor(out=sw[:], in0=sw[:], in1=rp[:, 32:64].unsqueeze(1).to_broadcast([128, 8, 32]), op=ALU.mult),
              reads=[sw, rp], writes=[sw])
        fw.op("dve", lambda e: e.tensor_tensor(out=xf[:, :, 64:96], in0=t1[:], in1=sw[:], op=ALU.add), reads=[t1, sw], writes=[xf])

    def even_mixer(self, i, last):
        fw, nc = self.fw, self.nc
        j = i // 2
        W = self.W
        A = self.A
        BT = self.BT
        with ExitStack() as ph:
            self.ph = ph
            winb = A("winb", [128, 8, EV_IN], BF16)
            self.load_w(winb, W["even_w_in"][j], 8, EV_IN, "pool")
            wqb = A("wqb", [128, 2, 768], BF16)
            self.load_w(wqb, W["mla_wq_b"][j], 2, 768, "act")
            wkvb = A("wkvb", [128, 2, 1024], BF16)
            self.load_w(wkvb, W["mla_wkv_b"][j], 2, 1024, "act")
            qg = A("qg", [128, 96], F32)
            kg = A("kg", [128, 96], F32)
            fw.dma("sp", qg[:], W["mla_q_qk"][j].partition_broadcast(128), writes=[qg])
            fw.dma("sp", kg[:], W["mla_k_qk"][j].partition_broadcast(128), writes=[kg])
            fw.op("dve", lambda e: e.tensor_scalar(out=qg[:], in0=qg[:], scalar1=float(96 ** -0.5), scalar2=None, op0=ALU.mult), reads=[qg], writes=[qg])
            hs = [A("hs%d" % k, [128, 1024], F32) for k in range(2)]
            aT = A("aT", [128, 8, 128], BF16)
            zs = A("zs", [128, 544], F32)
            zn = A("zn", [128, 512], BF16)
            znT = A("znT", [128, 4, 128], BF16)
            st = A("st", [128, 32], F32)
            qf = A("qf", [128, 8, 96], F32)
            kf = A("kf", [128, 8, 96], F32)
            sq = A("sq", [128, 8, 96], F32)
            sw = A("sw", [128, 8, 32], F32)
            t1 = A("t1", [128, 8, 32], F32)
            rp = A("rp", [128, 64], F32)
            qb = A("qb", [128, 8, 96], BF16)
            kb = A("kb", [128, 8, 96], BF16)
            qTt = A("qTt", [96, 8, 128], BF16)
            kTt = A("kTt", [96, 8, 128], BF16)
            Vt = A("Vt", [128, 8, 68], BF16)
            sg = A("sg", [128, 4, 128], F32)
            yt = A("yt", [128, 4, 128], F32)
            fw.op("dve", lambda e: e.memset(Vt[:, :, 64:65], 1.0), writes=[Vt])
            for b in range(2):
                for tile in range(NTILE):
                    jc = 2 if tile < 2 else b
                    h = hs[tile % 2]
                    fw.dma("sp", h[:], self.hbuf[b, tile * 128:(tile + 1) * 128, :], reads=[self.dt("h", b, tile)], writes=[h])
                    fw.dma("sp", rp[:], self.rope[tile * 128:(tile + 1) * 128, :], writes=[rp])
                    self.norm_T(h, self.scl1, self.shf1, jc, aT[:], aT, BF16)
                    for (n0, n1, bk) in ((0, 512, 2), (512, 544, 3)):
                        for kc in range(8):
                            fw.op("pe", lambda e: e.matmul(self.bank(bk)[:, 0:n1 - n0], lhsT=aT[:, kc, :], rhs=winb[:, kc, n0:n1], start=(kc == 0), stop=(kc == 7)),
                                  reads=[aT, winb], writes=[BT[bk]])
                    fw.op("act", lambda e: e.activation(out=zs[:, 0:512], in_=self.bank(2), func=AF.Copy), reads=[BT[2]], writes=[zs])
                    fw.op("act", lambda e: e.activation(out=zs[:, 512:544], in_=self.bank(3)[:, 0:32], func=AF.Copy), reads=[BT[3]], writes=[zs])
                    for c in range(2):
                        fw.op("act", lambda e: e.activation(out=sq[:, 0:2, :].rearrange("p a b -> p (a b)")[:, 0:256] if False else self.n_junk[:, 0:256], in_=zs[:, c * 256:(c + 1) * 256], func=AF.Square, accum_out=st[:, c:c + 1]),
                              reads=[zs], writes=[self.n_junk, st])
                    self.rstd(st[:, 0:2], st[:, 2:4], 256, EPS, st)
                    for c in range(2):
                        fw.op("dve", lambda e: e.tensor_scalar(out=zn[:, c * 256:(c + 1) * 256], in0=zs[:, c * 256:(c + 1) * 256], scalar1=st[:, 2 + c:3 + c], scalar2=None, op0=ALU.mult),
                              reads=[zs, st], writes=[zn])
                    pzT = self.bank(0).bitcast(BF16).rearrange("p (a b) -> p a b", a=8)
                    for c in range(4):
                        fw.op("pe", lambda e: e.transpose(out=pzT[:, c, :], in_=zn[:, c * 128:(c + 1) * 128], identity=self.identb[:]),
                              reads=[zn, self.identb], writes=[BT[0]])
                    for (c0, vn) in ((0, "qn%d" % j), (2, "kvn%d" % j)):
                        fw.op("dve", lambda e: e.tensor_tensor(out=znT[:, c0:c0 + 2, :], in0=pzT[:, c0:c0 + 2, :], in1=self.vcol(vn, 0, 2).unsqueeze(2).to_broadcast([128, 2, 128]), op=ALU.mult),
                              reads=[BT[0], self.vec], writes=[znT])
                    for (n0, n1, bk) in ((0, 512, 4), (512, 768, 5)):
                        for kc in range(2):
                            fw.op("pe", lambda e: e.matmul(self.bank(bk)[:, 0:n1 - n0], lhsT=znT[:, kc, :], rhs=wqb[:, kc, n0:n1], start=(kc == 0), stop=(kc == 1)),
                                  reads=[znT, wqb], writes=[BT[bk]])
                    qff = qf[:].rearrange("p a b -> p (a b)")
                    fw.op("act", lambda e: e.activation(out=qff[:, 0:512], in_=self.bank(4), func=AF.Copy), reads=[BT[4]], writes=[qf])
                    fw.op("act", lambda e: e.activation(out=qff[:, 512:768], in_=self.bank(5)[:, 0:256], func=AF.Copy), reads=[BT[5]], writes=[qf])
                    for (n0, bk) in ((0, 6), (512, 7)):
                        for kc in range(2):
                            fw.op("pe", lambda e: e.matmul(self.bank(bk), lhsT=znT[:, 2 + kc, :], rhs=wkvb[:, kc, n0:n0 + 512], start=(kc == 0), stop=(kc == 1)),
                                  reads=[znT, wkvb], writes=[BT[bk]])
                    kvv = self.bank2(3).rearrange("p (a b) -> p a b", a=8)
                    fw.op("dve", lambda e: e.tensor_copy(out=kf[:, :, 0:64], in_=kvv[:, :, 0:64]), reads=[BT[6], BT[7]], writes=[kf])
                    fw.op("pool", lambda e: e.tensor_copy(out=kf[:, :, 64:96], in_=zs[:, 512:544].unsqueeze(1).to_broadcast([128, 8, 32])), reads=[zs, kf], writes=[kf])
                    fw.op("act", lambda e: e.activation(out=Vt[:, :, 0:64], in_=kvv[:, :, 64:128], func=AF.Copy), reads=[BT[6], BT[7]], writes=[Vt])
                    fw.dma("pool", self.V_d[b, :, :, tile, :].rearrange("h p e -> p h e"), Vt[:], reads=[Vt], writes=[self.dt("V", b)], disjoint=True)
                    for (xf, gain, xb, xTt, dst, nm) in ((qf, qg, qb, qTt, self.qT_d, "q"), (kf, kg, kb, kTt, self.kT_d, "k")):
                        fw.op("dve", lambda e: e.tensor_tensor(out=sq[:], in0=xf[:], in1=xf[:], op=ALU.mult), reads=[xf], writes=[sq])
                        fw.op("dve", lambda e: e.tensor_reduce(out=st[:, 8:16], in_=sq[:], axis=AX.X, op=ALU.add), reads=[sq], writes=[st])
                        self.rstd(st[:, 8:16], st[:, 16:24], 96, EPS, st)
                        fw.op("dve", lambda e: e.tensor_tensor(out=xf[:], in0=xf[:], in1=st[:, 16:24].unsqueeze(2).to_broadcast([128, 8, 96]), op=ALU.mult),
                              reads=[xf, st], writes=[xf])
                        fw.op("dve", lambda e: e.tensor_tensor(out=xf[:], in0=xf[:], in1=gain[:].unsqueeze(1).to_broadcast([128, 8, 96]), op=ALU.mult),
                              reads=[xf, gain], writes=[xf])
                        self.rope_apply(xf, rp, sw, t1)
                        fw.op("act", lambda e: e.activation(out=xb[:], in_=xf[:], func=AF.Copy), reads=[xf], writes=[xb])
                        pT = self.bank(1).bitcast(BF16).rearrange("p (a b) -> p a b", a=8)
                        for hh in range(8):
                            fw.op("pe", lambda e: e.transpose(out=pT[0:96, hh, :], in_=xb[:, hh, :], identity=self.identb[:]),
                                  reads=[xb, self.identb], writes=[BT[1]])
                        fw.op("dve", lambda e: e.tensor_copy(out=xTt[:], in_=pT[0:96, :, :]), reads=[BT[1]], writes=[xTt])
                        fw.dma("pool", dst[b, :, :, tile * 128:(tile + 1) * 128].rearrange("h p t -> p h t"), xTt[:], reads=[xTt], writes=[self.dt(nm, b)], disjoint=True)
                    pc = self.bank2(2).rearrange("p (a b) -> p a b", a=8)
                    for cc in range(8):
                        for kc in range(8):
                            fw.op("pe", lambda e: e.matmul(pc[:, cc, :], lhsT=winb[:, kc, 544 + cc * 128:544 + (cc + 1) * 128], rhs=aT[:, kc, :], start=(kc == 0), stop=(kc == 7)),
                                  reads=[winb, aT], writes=[BT[4 + cc // 4]])
                    fw.op("act", lambda e: e.activation(out=sg[:], in_=pc[:, 4:8, :], func=AF.Sigmoid), reads=[BT[5]], writes=[sg])
                    fw.op("dve", lambda e: e.tensor_tensor(out=yt[:], in0=pc[:, 0:4, :], in1=sg[:], op=ALU.mult), reads=[BT[4], sg], writes=[yt])
                    off = 15 + tile * 128 if tile < 2 else (15 + TC + 15 + 15 + (tile - 2) * 128)
                    fw.dma("pool", self.yT_d[b, :, :, off:off + 128].rearrange("c p t -> p c t"), yt[:], reads=[yt], writes=[self.dt("y", b)], disjoint=True)
            fw.barrier()
        if self.stop_after == "E1":
            return
        fw.pe_delay = True
        with ExitStack() as ph:
            self.ph = ph
            yin = A("yin", [128, 4, 542], F32)
            acc = A("acc", [128, 4, 512], F32)
            acc_tr = [Track("acc%d" % c) for c in range(4)]
            sq2 = A("sq2", [128, 4, 512], F32)
            mean = A("mean", [128, 512], F32)
            msq = A("msq", [128, 512], F32)
            rs = A("rs", [128, 512], F32)
            cvb = A("cvb", [128, 4, 512], BF16)
            ones = self.cstf[:, 5 * 128:6 * 128]
            cw0, _ = self.lay["convw%d" % j]
            for b in range(2):
                for (c0, Tb, tok0) in [(0, 256, 0)] + [(286 + k * 512, 512, 256 + k * 512) for k in range(4)]:
                    fw.dma("sp", yin[:, :, 0:Tb + 30], self.yT_d[b, :, :, c0:c0 + Tb + 30].rearrange("c p t -> p c t"),
                           reads=[self.dt("y", b), self.dt("ypad")], writes=[yin])
                    for cc in range(4):
                        eng = "dve"
                        wcol = lambda tap: self.vec[:, cw0 + cc * 31 + tap:cw0 + cc * 31 + tap + 1]
                        fw.op(eng, lambda e: e.tensor_scalar(out=acc[:, cc, 0:Tb], in0=yin[:, cc, 0:Tb], scalar1=wcol(0), scalar2=self.vcol("convb%d" % j, cc, 1), op0=ALU.mult, op1=ALU.add),
                              reads=[yin, self.vec], writes=[acc_tr[cc]])
                        for tap in range(1, 31):
                            fw.op(eng, lambda e: e.scalar_tensor_tensor(out=acc[:, cc, 0:Tb], in0=yin[:, cc, tap:tap + Tb], scalar=wcol(tap), in1=acc[:, cc, 0:Tb], op0=ALU.mult, op1=ALU.add),
                                  reads=[yin, acc_tr[cc]], writes=[acc_tr[cc]])
                    fw.op("act", lambda e: e.activation(out=sq2[:, :, 0:Tb], in_=acc[:, :, 0:Tb], func=AF.Square), reads=acc_tr, writes=[sq2])
                    for (src, strk, bk) in ((acc, acc_tr, 2), (sq2, [sq2], 3)):
                        for cc in range(4):
                            fw.op("pe", lambda e: e.matmul(self.bank(bk)[:, 0:Tb], lhsT=ones, rhs=src[:, cc, 0:Tb], start=(cc == 0), stop=(cc == 3)),
                                  reads=list(strk) + [self.cstf], writes=[BT[bk]])
                    fw.op("act", lambda e: e.activation(out=mean[:, 0:Tb], in_=self.bank(2)[:, 0:Tb], func=AF.Copy, scale=1.0 / 512), reads=[BT[2]], writes=[mean])
                    fw.op("dve", lambda e: e.tensor_tensor(out=msq[:, 0:Tb], in0=mean[:, 0:Tb], in1=mean[:, 0:Tb], op=ALU.mult), reads=[mean], writes=[msq])
                    fw.op("dve", lambda e: e.scalar_tensor_tensor(out=rs[:, 0:Tb], in0=self.bank(3)[:, 0:Tb], scalar=1.0 / 512, in1=msq[:, 0:Tb], op0=ALU.mult, op1=ALU.subtract),
                          reads=[BT[3], msq], writes=[rs])
                    self.rstd(rs[:, 0:Tb], rs[:, 0:Tb], 1.0, 1e-5, rs)
                    for cc in range(4):
                        fw.op("dve", lambda e: e.tensor_tensor(out=sq2[:, cc, 0:Tb], in0=acc[:, cc, 0:Tb], in1=mean[:, 0:Tb], op=ALU.subtract),
                              reads=[acc_tr[cc], mean, sq2], writes=[sq2])
                        fw.op("dve", lambda e: e.tensor_tensor(out=sq2[:, cc, 0:Tb], in0=sq2[:, cc, 0:Tb], in1=rs[:, 0:Tb], op=ALU.mult),
                              reads=[rs, sq2], writes=[sq2])
                        fw.op("act", lambda e: e.activation(out=cvb[:, cc, 0:Tb], in_=sq2[:, cc, 0:Tb], func=AF.Silu,
                                                            scale=self.vcol("clng%d" % j, cc, 1), bias=self.vcol("clnb%d" % j, cc, 1)),
                              reads=[sq2, self.vec], writes=[cvb])
                    fw.dma("pool", self.mixT_d[b, :, :, tok0:tok0 + Tb].rearrange("c p t -> p c t"), cvb[:, :, 0:Tb], reads=[cvb], writes=[self.dt("mixc", b)], disjoint=True)
            fw.barrier()
        fw.pe_delay = False
        if self.stop_after == "E2":
            return
        with ExitStack() as ph:
            self.ph = ph
            woutb = A("woutb", [128, 8, 1024], BF16)
            self.load_w(woutb, W["even_w_out"][j], 8, 1024, "pool")
            kTh = [A("kTh%d" % k, [96, TT], BF16) for k in range(2)]
            qTh = [A("qTh%d" % k, [96, TT], BF16) for k in range(2)]
            Vh = [A("Vh%d" % k, [128, NTILE, 68], BF16) for k in range(2)]
            E = [A("E%d" % k, [128, 512], BF16) for k in range(2)]
            oall = A("oall", [128, NTILE, 512], BF16)
            rden = A("rden", [128, 4], F32)
            mixA = A("mixA", [128, 4, 512], BF16)
            mixC = A("mixC", [128, 4, 512], BF16)
            yT = A("yT", [128, 8, 512], F32)
            hs = [A("hs%d" % k, [128, 1024], F32) for k in range(2)]
            for b in range(2):
                for hh in range(8):
                    kt_, qt_, vt_ = kTh[hh % 2], qTh[hh % 2], Vh[hh % 2]
                    fw.dma("sp", kt_[:], self.kT_d[b, hh], reads=[self.dt("k", b)], writes=[kt_])
                    fw.dma("sp", qt_[:], self.qT_d[b, hh], reads=[self.dt("q", b)], writes=[qt_])
                    fw.dma("sp", vt_[:], self.V_d[b, hh], reads=[self.dt("V", b)], writes=[vt_])
                    for (q0, QW, nkt, tile0) in [(0, 256, 2, 0)] + [(256 + k * 512, 512, NTILE, 2 + 4 * k) for k in range(4)]:
                        nqt = QW // 128
                        for kt in range(nkt):
                            bk = kt % 2
                            Ek = E[kt % 2]
                            fw.op("pe", lambda e: e.matmul(self.bank(bk)[:, 0:QW], lhsT=kt_[:, kt * 128:(kt + 1) * 128], rhs=qt_[:, q0:q0 + QW], start=True, stop=True),
                                  reads=[kt_, qt_], writes=[BT[bk]])
                            fw.op("act", lambda e: e.activation(out=Ek[:, 0:QW], in_=self.bank(bk)[:, 0:QW], func=AF.Exp), reads=[BT[bk]], writes=[Ek])
                            for qt in range(nqt):
                                fw.op("pe", lambda e: e.matmul(self.bank(2 + qt)[:, 0:65], lhsT=Ek[:, qt * 128:(qt + 1) * 128], rhs=vt_[:, kt, 0:65], start=(kt == 0), stop=(kt == nkt - 1)),
                                      reads=[Ek, vt_], writes=[BT[2 + qt]])
                        for qt in range(nqt):
                            fw.op("dve", lambda e: e.reciprocal(out=rden[:, qt:qt + 1], in_=self.bank(2 + qt)[:, 64:65]), reads=[BT[2 + qt], rden], writes=[rden])
                            fw.op("dve", lambda e: e.tensor_scalar(out=oall[:, tile0 + qt, hh * 64:(hh + 1) * 64], in0=self.bank(2 + qt)[:, 0:64], scalar1=rden[:, qt:qt + 1], scalar2=None, op0=ALU.mult),
                                  reads=[BT[2 + qt], rden], writes=[oall])
                for (tok0, Tb) in [(0, 256)] + [(256 + k * 512, 512) for k in range(4)]:
                    jc = 2 if tok0 == 0 else b
                    fw.dma("sp", mixC[:, :, 0:Tb], self.mixT_d[b, :, :, tok0:tok0 + Tb].rearrange("c p t -> p c t"), reads=[self.dt("mixc", b)], writes=[mixC])
                    for tt in range(Tb // 128):
                        tile = tok0 // 128 + tt
                        pT = self.bank(0).bitcast(BF16).rearrange("p (a b) -> p a b", a=8)
                        for c in range(4):
                            fw.op("pe", lambda e: e.transpose(out=pT[:, c, :], in_=oall[:, tile, c * 128:(c + 1) * 128], identity=self.identb[:]),
                                  reads=[oall, self.identb], writes=[BT[0]])
                        fw.op("dve", lambda e: e.tensor_copy(out=mixA[:, :, tt * 128:(tt + 1) * 128], in_=pT[:, 0:4, :]), reads=[BT[0]], writes=[mixA])
                    for fc in range(8):
                        bk = 2 + fc % 2
                        for kc in range(8):
                            src = mixA if kc < 4 else mixC
                            fw.op("pe", lambda e: e.matmul(self.bank(bk)[:, 0:Tb], lhsT=woutb[:, kc, fc * 128:(fc + 1) * 128], rhs=src[:, kc % 4, 0:Tb], start=(kc == 0), stop=(kc == 7)),
                                  reads=[woutb, src], writes=[BT[bk]])
                        fw.op("dve", lambda e: e.tensor_scalar(out=yT[:, fc, 0:Tb], in0=self.bank(bk)[:, 0:Tb], scalar1=self.gate1[:, fc, jc:jc + 1], scalar2=None, op0=ALU.mult),
                              reads=[BT[bk], self.gate1], writes=[yT])
                    for tt in range(Tb // 128):
                        tile = tok0 // 128 + tt
                        h = hs[tile % 2]
                        fw.dma("sp", h[:], self.hbuf[b, tile * 128:(tile + 1) * 128, :], reads=[self.dt("h", b, tile)], writes=[h])
                        self.resid_update(b, tile, yT, tt * 128, hs=h)
            fw.barrier()

    def odd_mixer(self, i, last):
        fw, nc = self.fw, self.nc
        j = i // 2
        W = self.W
        A = self.A
        BT = self.BT
        S = self.S
        NEG_E = -math.exp(-0.5)
        seq_off = lambda tile: (1 + tile * 128) if tile < 2 else (1 + TC + 1 + 1 + (tile - 2) * 128)
        with ExitStack() as ph:
            self.ph = ph
            self.n_xnf = A("n_xnf", [128, 1024], F32)
            hs = [A("hs%d" % k, [128, 1024], F32) for k in range(2)]
            aTf = [A("aTf%d" % k, [128, 8, 128], F32) for k in range(2)]
            for b in range(2):
                for tile in range(NTILE):
                    jc = 2 if tile < 2 else b
                    h = hs[tile % 2]
                    at = aTf[tile % 2]
                    fw.dma("sp", h[:], self.hbuf[b, tile * 128:(tile + 1) * 128, :], reads=[self.dt("h", b, tile)], writes=[h])
                    self.norm_T(h, self.scl1, self.shf1, jc, at[:], at, F32)
                    off = seq_off(tile)
                    fw.dma("pool", self.aT_d[b, :, :, off:off + 128].rearrange("c p t -> p c t"), at[:], reads=[at], writes=[self.dt("aT", b)], disjoint=True)
            fw.barrier()
        with ExitStack() as ph:
            self.ph = ph
            wrb = A("wrb", [128, 8, 1024], BF16)
            wkb = A("wkb", [128, 8, 1024], BF16)
            wvb = A("wvb", [128, 8, 1024], BF16)
            self.load_w(wrb, W["rw_wr"][j], 8, 1024, "pool")
            self.load_w(wkb, W["rw_wk"][j], 8, 1024, "act")
            self.load_w(wvb, W["rw_wv"][j], 8, 1024, "pool")
            w1c = A("w1c", [128, 8, 128], BF16)
            a1c = A("a1c", [128, 8, 128], BF16)
            for d in range(2):
                self.load_w(w1c, W["rw_w1"][j, d], 8, 64, "act", col0=d * 64)
                self.load_w(a1c, W["rw_a1"][j, d], 8, 64, "act", col0=d * 64)
            g1b = A("g1b", [128, 8, 160], BF16)
            self.load_w(g1b, W["rw_g1"][j], 8, 160, "act")
            w2s = A("w2s", [128, 1, 1024], BF16)
            a2s = A("a2s", [128, 1, 1024], BF16)
            self.load_w(w2s, W["rw_w2"][j].rearrange("d r n -> (d r) n"), 1, 1024, "pool")
            self.load_w(a2s, W["rw_a2"][j].rearrange("d r n -> (d r) n"), 1, 1024, "pool")
            g2a = A("g2a", [128, 1, 1024], BF16)
            g2b = A("g2b", [32, 1, 1024], BF16)
            self.load_w(g2a, W["rw_g2"][j, 0:128], 1, 1024, "pool")
            self.load_w(g2b, W["rw_g2"][j, 128:160], 1, 1024, "pool", rows=32)
            vres = j > 0
            if vres:
                v1b = A("v1b", [128, 8, 32], BF16)
                self.load_w(v1b, W["rw_v1"][0], 8, 32, "act")
                v2b = A("v2b", [32, 1, 1024], BF16)
                self.load_w(v2b, W["rw_v2"][0], 1, 1024, "pool", rows=32)
            bc = {}
            srcs = {"w0_0": W["rw_w0"][j, 0], "w0_1": W["rw_w0"][j, 1], "a0_0": W["rw_a0"][j, 0], "a0_1": W["rw_a0"][j, 1],
                    "k_k": W["rw_k_k"][j], "k_a": W["rw_k_a"][j]}
            if vres:
                srcs["v0"] = W["rw_v0"][0]
            for nm, src in srcs.items():
                bc[nm] = A("bc_" + nm, [128, 1024], F32)
                fw.dma("sp", bc[nm][:], src.partition_broadcast(128), writes=[bc[nm]])
            bc["omka"] = A("bc_omka", [128, 1024], F32)
            fw.op("dve", lambda e: e.tensor_scalar(out=bc["omka"][:], in0=bc["k_a"][:], scalar1=-1.0, scalar2=1.0, op0=ALU.mult, op1=ALU.add),
                  reads=[bc["k_a"]], writes=[bc["omka"]])
            mu0, _ = self.lay["mu%d" % j]
            muc = A("muc", [128, 6, 8], F32)
            mv = self.vec[:, mu0:mu0 + 96].rearrange("p (d i k) -> p d i k", d=2, i=6)
            fw.op("dve", lambda e: e.tensor_tensor(out=muc[:], in0=mv[:, 0], in1=mv[:, 1], op=ALU.add), reads=[self.vec], writes=[muc])
            fw.op("dve", lambda e: e.tensor_scalar(out=muc[:], in0=muc[:], scalar1=-1.0, scalar2=1.0, op0=ALU.mult, op1=ALU.add), reads=[muc], writes=[muc])
            aTin = A("aTin", [128, 8, 130], F32)
            xm = [A("xm%d" % k, [128, 8, 128], BF16) for k in range(6)]
            tmix = A("tmix", [128, 4, 128], F32)
            tmix_tr = [Track("tmix%d" % k) for k in range(4)]
            xm_tr = [[Track("xm%d_%d" % (m_, k)) for k in range(8)] for m_ in range(6)]
            for m_ in range(6):
                xm[m_].kct = xm_tr[m_]
            l1 = A("l1", [128, 512], BF16)
            l1T = A("l1T", [128, 5, 128], BF16)
            WK = {n: A("W" + n, [128, 1024], F32) for n in ("r", "k", "v", "a", "t", "u", "b", "c", "g", "f")}
            st = A("st", [128, 64], F32)

            def proj(x, wb, pair):
                for half in range(2):
                    bk = 2 * pair + half
                    for kc in range(8):
                        fw.op("pe", lambda e: e.matmul(self.bank(bk), lhsT=x[:, kc, :], rhs=wb[:, kc, half * 512:(half + 1) * 512], start=(kc == 0), stop=(kc == 7)),
                              reads=[x.kct[kc], wb], writes=[BT[bk]])

            def proj2(lhs_list, pair):
                for half in range(2):
                    bk = 2 * pair + half
                    n = len(lhs_list)
                    for q, (lt, ltr, rt, rsl) in enumerate(lhs_list):
                        fw.op("pe", lambda e: e.matmul(self.bank(bk), lhsT=lt, rhs=rt[rsl, 0, half * 512:(half + 1) * 512], start=(q == 0), stop=(q == n - 1)),
                              reads=[ltr, rt], writes=[BT[bk]])

            for b in range(2):
                for tile in range(NTILE):
                    rows = slice(tile * 128, (tile + 1) * 128)
                    off = seq_off(tile)
                    fw.dma("sp", aTin[:], self.aT_d[b, :, :, off - 1:off + 129].rearrange("c p t -> p c t"),
                           reads=[self.dt("aT", b), self.dt("apad")], writes=[aTin])
                    for m in range(6):
                        for kc in range(8):
                            tk = tmix[:, (m * 8 + kc) % 4, :]
                            ttr = tmix_tr[(m * 8 + kc) % 4]
                            fw.op("dve", lambda e: e.tensor_scalar(out=tk, in0=aTin[:, kc, 1:129], scalar1=muc[:, m, kc:kc + 1], scalar2=None, op0=ALU.mult),
                                  reads=[aTin, muc], writes=[ttr])
                            fw.op("dve", lambda e: e.scalar_tensor_tensor(out=tk, in0=aTin[:, kc, 0:128], scalar=mv[:, 0, m, kc:kc + 1], in1=tk, op0=ALU.mult, op1=ALU.add),
                                  reads=[aTin, self.vec, ttr], writes=[ttr])
                            fw.op("dve", lambda e: e.scalar_tensor_tensor(out=xm[m][:, kc, :], in0=aTin[:, kc, 2:130], scalar=mv[:, 1, m, kc:kc + 1], in1=tk, op0=ALU.mult, op1=ALU.add),
                                  reads=[aTin, self.vec, ttr], writes=[xm_tr[m][kc]])
                    xr, xw, xk, xv, xa, xg = xm
                    p0 = self.bank(0)
                    for (x, wb, c0, n) in ((xw, w1c, 0, 128), (xa, a1c, 128, 128), (xg, g1b, 256, 160)) + (((xv, v1b, 416, 32),) if vres else ()):
                        for kc in range(8):
                            fw.op("pe", lambda e: e.matmul(p0[:, c0:c0 + n], lhsT=x[:, kc, :], rhs=wb[:, kc, :], start=(kc == 0), stop=(kc == 7)),
                                  reads=[x.kct[kc], wb], writes=[BT[0]])
                    fw.op("act", lambda e: e.activation(out=l1[:, 0:128], in_=p0[:, 0:128], func=AF.Tanh), reads=[BT[0]], writes=[l1])
                    fw.op("act", lambda e: e.activation(out=l1[:, 128:256], in_=p0[:, 128:256], func=AF.Copy), reads=[BT[0]], writes=[l1])
                    fw.op("act", lambda e: e.activation(out=l1[:, 256:416], in_=p0[:, 256:416], func=AF.Sigmoid), reads=[BT[0]], writes=[l1])
                    if vres:
                        fw.op("act", lambda e: e.activation(out=l1[:, 416:448], in_=p0[:, 416:448], func=AF.Copy), reads=[BT[0]], writes=[l1])
                    pT = self.bank(1).bitcast(BF16).rearrange("p (a b) -> p a b", a=8)
                    for (q, c0, n) in ((0, 0, 128), (1, 128, 128), (2, 256, 128), (3, 384, 32)) + (((4, 416, 32),) if vres else ()):
                        fw.op("pe", lambda e: e.transpose(out=pT[0:n, q, :], in_=l1[:, c0:c0 + n], identity=self.identb[:]),
                              reads=[l1, self.identb], writes=[BT[1]])
                    fw.op("dve", lambda e: e.tensor_copy(out=l1T[:, 0:3, :], in_=pT[:, 0:3, :]), reads=[BT[1]], writes=[l1T])
                    fw.op("dve", lambda e: e.tensor_copy(out=l1T[0:32, 3:5, :], in_=pT[0:32, 3:5, :]), reads=[BT[1], l1T], writes=[l1T])
                    proj(xr, wrb, 1)
                    fw.op("act", lambda e: e.activation(out=WK["r"][:], in_=self.bank2(1), func=AF.Copy), reads=[BT[2], BT[3]], writes=[WK["r"]])
                    fw.dma("sp", S["r"][b, rows, :], WK["r"][:], reads=[WK["r"]], writes=[self.dt("s_r", b)], disjoint=True)
                    proj(xk, wkb, 2)
                    fw.op("act", lambda e: e.activation(out=WK["k"][:], in_=self.bank2(2), func=AF.Copy), reads=[BT[4], BT[5]], writes=[WK["k"]])
                    proj(xv, wvb, 3)
                    fw.op("act", lambda e: e.activation(out=WK["v"][:], in_=self.bank2(3), func=AF.Copy), reads=[BT[6], BT[7]], writes=[WK["v"]])
                    if vres:
                        proj2([(l1T[0:32, 4, :], l1T, v2b, slice(0, 32))], 1)
                        fw.op("dve", lambda e: e.tensor_tensor(out=WK["t"][:], in0=self.bank2(1), in1=bc["v0"][:], op=ALU.add), reads=[BT[2], BT[3], bc["v0"]], writes=[WK["t"]])
                        fw.op("act", lambda e: e.activation(out=WK["t"][:], in_=WK["t"][:], func=AF.Sigmoid), reads=[WK["t"]], writes=[WK["t"]])
                        fw.dma("sp", WK["f"][:], S["vf"][b, rows, :], reads=[self.dt("s_vf", b)], writes=[WK["f"]])
                        fw.op("dve", lambda e: e.tensor_tensor(out=WK["f"][:], in0=WK["f"][:], in1=WK["v"][:], op=ALU.subtract), reads=[WK["f"], WK["v"]], writes=[WK["f"]])
                        fw.op("dve", lambda e: e.tensor_tensor(out=WK["f"][:], in0=WK["f"][:], in1=WK["t"][:], op=ALU.mult), reads=[WK["f"], WK["t"]], writes=[WK["f"]])
                        fw.op("dve", lambda e: e.tensor_tensor(out=WK["v"][:], in0=WK["v"][:], in1=WK["f"][:], op=ALU.add), reads=[WK["f"], WK["v"]], writes=[WK["v"]])
                    else:
                        fw.dma("sp", S["vf"][b, rows, :], WK["v"][:], reads=[WK["v"]], writes=[self.dt("s_vf", b)], disjoint=True)
                    fw.dma("sp", S["v"][b, rows, :], WK["v"][:], reads=[WK["v"]], writes=[self.dt("s_v", b)], disjoint=True)
                    fw.op("pool", lambda e: e.tensor_tensor(out=WK["a"][:], in0=WK["k"][:], in1=bc["k_k"][:], op=ALU.mult), reads=[WK["k"], bc["k_k"]], writes=[WK["a"]])
                    fw.op("pool", lambda e: e.tensor_tensor(out=WK["t"][:], in0=WK["a"][:], in1=WK["a"][:], op=ALU.mult), reads=[WK["a"]], writes=[WK["t"]])
                    fw.op("dve", lambda e: e.tensor_reduce(out=st[:, 0:16], in_=WK["t"][:].rearrange("p (h k) -> p h k", h=16), axis=AX.X, op=ALU.add), reads=[WK["t"]], writes=[st])
                    fw.op("act", lambda e: e.activation(out=st[:, 0:16], in_=st[:, 0:16], func=AF.Sqrt), reads=[st], writes=[st])
                    fw.op("dve", lambda e: e.tensor_scalar(out=st[:, 0:16], in0=st[:, 0:16], scalar1=1e-12, scalar2=None, op0=ALU.max), reads=[st], writes=[st])
                    fw.op("dve", lambda e: e.reciprocal(out=st[:, 0:16], in_=st[:, 0:16]), reads=[st], writes=[st])
                    fw.op("dve", lambda e: e.tensor_tensor(out=WK["a"][:].rearrange("p (h k) -> p h k", h=16), in0=WK["a"][:].rearrange("p (h k) -> p h k", h=16),
                                                           in1=st[:, 0:16].unsqueeze(2).to_broadcast([128, 16, 64]), op=ALU.mult), reads=[WK["a"], st], writes=[WK["a"]])
                    fw.dma("sp", S["a"][b, rows, :], WK["a"][:], reads=[WK["a"]], writes=[self.dt("s_a", b)], disjoint=True)
                    for d in range(2):
                        dsl = slice(d * 64, (d + 1) * 64)
                        proj2([(l1T[dsl, 0, :], l1T, w2s, dsl)], 1)
                        fw.op("dve", lambda e: e.tensor_tensor(out=WK["t"][:], in0=self.bank2(1), in1=bc["w0_%d" % d][:], op=ALU.add), reads=[BT[2], BT[3], bc["w0_%d" % d]], writes=[WK["t"]])
                        fw.op("act", lambda e: e.activation(out=WK["t"][:], in_=WK["t"][:], func=AF.Sigmoid), reads=[WK["t"]], writes=[WK["t"]])
                        fw.op("act", lambda e: e.activation(out=WK["t"][:], in_=WK["t"][:], func=AF.Copy, scale=NEG_E), reads=[WK["t"]], writes=[WK["t"]])
                        fw.dma("sp", S["lw%d" % d][b, rows, :], WK["t"][:], reads=[WK["t"]], writes=[self.dt("s_lw%d" % d, b)], disjoint=True)
                        proj2([(l1T[dsl, 1, :], l1T, a2s, dsl)], 2)
                        fw.op("dve", lambda e: e.tensor_tensor(out=WK["u"][:], in0=self.bank2(2), in1=bc["a0_%d" % d][:], op=ALU.add), reads=[BT[4], BT[5], bc["a0_%d" % d]], writes=[WK["u"]])
                        fw.op("act", lambda e: e.activation(out=WK["u"][:], in_=WK["u"][:], func=AF.Sigmoid), reads=[WK["u"]], writes=[WK["u"]])
                        fw.op("pool", lambda e: e.tensor_tensor(out=WK["b"][:], in0=WK["a"][:], in1=WK["u"][:], op=ALU.mult), reads=[WK["a"], WK["u"]], writes=[WK["b"]])
                        fw.dma("sp", S["b%d" % d][b, rows, :], WK["b"][:], reads=[WK["b"]], writes=[self.dt("s_b%d" % d, b)], disjoint=True)
                        fw.op("pool", lambda e: e.tensor_tensor(out=WK["c"][:], in0=WK["u"][:], in1=bc["k_a"][:], op=ALU.mult), reads=[WK["u"], bc["k_a"]], writes=[WK["c"]])
                        fw.op("pool", lambda e: e.tensor_tensor(out=WK["c"][:], in0=WK["c"][:], in1=bc["omka"][:], op=ALU.add), reads=[WK["c"], bc["omka"]], writes=[WK["c"]])
                        fw.op("pool", lambda e: e.tensor_tensor(out=WK["c"][:], in0=WK["c"][:], in1=WK["k"][:], op=ALU.mult), reads=[WK["c"], WK["k"]], writes=[WK["c"]])
                        fw.dma("sp", S["kd%d" % d][b, rows, :], WK["c"][:], reads=[WK["c"]], writes=[self.dt("s_kd%d" % d, b)], disjoint=True)
                    proj2([(l1T[:, 2, :], l1T, g2a, slice(0, 128)), (l1T[0:32, 3, :], l1T, g2b, slice(0, 32))], 3)
                    fw.op("act", lambda e: e.activation(out=WK["g"][:], in_=self.bank2(3), func=AF.Copy), reads=[BT[6], BT[7]], writes=[WK["g"]])
                    fw.dma("sp", S["g"][b, rows, :], WK["g"][:], reads=[WK["g"]], writes=[self.dt("s_g", b)], disjoint=True)
            fw.barrier()
        if self.stop_after == "O2":
            return
        self.scan(i)
        if self.stop_after == "O3":
            return
        with ExitStack() as ph:
            self.ph = ph
            wob = A("wob", [128, 8, 1024], BF16)
            self.load_w(wob, W["rw_wo"][j], 8, 1024, "pool")
            bc = {}
            for nm, src in (("rk0", W["rw_r_k"][j, 0]), ("rk1", W["rw_r_k"][j, 1]), ("lng", W["rw_ln_g"][j]), ("lnb", W["rw_ln_b"][j])):
                bc[nm] = A("bc_" + nm, [128, 1024], F32)
                fw.dma("sp", bc[nm][:], src.partition_broadcast(128), writes=[bc[nm]])
            L2 = [{n: A("L%d" % k + n, [128, 1024], F32) for n in ("y0", "y1", "r", "kd0", "kd1", "v", "g")} for k in range(2)]
            tt_ = A("tt", [128, 1024], F32)
            ob = A("ob", [128, 1024], BF16)
            oT = A("oT", [128, 8, 128], BF16)
            yT = A("yT", [128, 8, 128], F32)
            st = A("st", [128, 64], F32)
            hs = [A("hs%d" % k, [128, 1024], F32) for k in range(2)]
            hv = lambda t: t[:].rearrange("p (h k) -> p h k", h=16)
            b16 = lambda ap: ap.unsqueeze(2).to_broadcast([128, 16, 64])
            for b in range(2):
                for tile in range(NTILE):
                    if last and tile < 2:
                        continue
                    jc = 2 if tile < 2 else b
                    rows = slice(tile * 128, (tile + 1) * 128)
                    L = L2[tile % 2]
                    for n in L:
                        fw.dma("sp", L[n][:], S[n][b, rows, :], reads=[self.dt("s_" + n, b)], writes=[L[n]])
                    y = L["y0"]
                    fw.op("pool", lambda e: e.tensor_tensor(out=y[:], in0=y[:], in1=L["y1"][:], op=ALU.add), reads=[y, L["y1"]], writes=[y])
                    fw.op("dve", lambda e: e.tensor_reduce(out=st[:, 0:16], in_=hv(y), axis=AX.X, op=ALU.add), reads=[y], writes=[st])
                    fw.op("dve", lambda e: e.tensor_scalar(out=st[:, 0:16], in0=st[:, 0:16], scalar1=1.0 / 64, scalar2=None, op0=ALU.mult), reads=[st], writes=[st])
                    fw.op("dve", lambda e: e.tensor_tensor(out=hv(y), in0=hv(y), in1=b16(st[:, 0:16]), op=ALU.subtract), reads=[y, st], writes=[y])
                    fw.op("pool", lambda e: e.tensor_tensor(out=tt_[:], in0=y[:], in1=y[:], op=ALU.mult), reads=[y], writes=[tt_])
                    fw.op("dve", lambda e: e.tensor_reduce(out=st[:, 16:32], in_=hv(tt_), axis=AX.X, op=ALU.add), reads=[tt_], writes=[st])
                    self.rstd(st[:, 16:32], st[:, 32:48], 64, 6.4e-4, st)
                    fw.op("dve", lambda e: e.tensor_tensor(out=hv(y), in0=hv(y), in1=b16(st[:, 32:48]), op=ALU.mult), reads=[y, st], writes=[y])
                    fw.op("pool", lambda e: e.tensor_tensor(out=y[:], in0=y[:], in1=bc["lng"][:], op=ALU.mult), reads=[y, bc["lng"]], writes=[y])
                    fw.op("pool", lambda e: e.tensor_tensor(out=y[:], in0=y[:], in1=bc["lnb"][:], op=ALU.add), reads=[y, bc["lnb"]], writes=[y])
                    fw.op("pool", lambda e: e.tensor_tensor(out=L["kd0"][:], in0=L["kd0"][:], in1=bc["rk0"][:], op=ALU.mult), reads=[L["kd0"], bc["rk0"]], writes=[L["kd0"]])
                    fw.op("pool", lambda e: e.tensor_tensor(out=L["kd1"][:], in0=L["kd1"][:], in1=bc["rk1"][:], op=ALU.mult), reads=[L["kd1"], bc["rk1"]], writes=[L["kd1"]])
                    fw.op("dve", lambda e: e.tensor_tensor(out=L["kd0"][:], in0=L["kd0"][:], in1=L["kd1"][:], op=ALU.add), reads=[L["kd0"], L["kd1"]], writes=[L["kd0"]])
                    fw.op("dve", lambda e: e.tensor_tensor(out=L["kd0"][:], in0=L["kd0"][:], in1=L["r"][:], op=ALU.mult), reads=[L["kd0"], L["r"]], writes=[L["kd0"]])
                    fw.op("dve", lambda e: e.tensor_reduce(out=st[:, 48:64], in_=hv(L["kd0"]), axis=AX.X, op=ALU.add), reads=[L["kd0"]], writes=[st])
                    fw.op("dve", lambda e: e.tensor_tensor(out=hv(L["v"]), in0=hv(L["v"]), in1=b16(st[:, 48:64]), op=ALU.mult), reads=[L["v"], st], writes=[L["v"]])
                    fw.op("pool", lambda e: e.tensor_tensor(out=y[:], in0=y[:], in1=L["v"][:], op=ALU.add), reads=[y, L["v"]], writes=[y])
                    fw.op("dve", lambda e: e.tensor_tensor(out=ob[:], in0=y[:], in1=L["g"][:], op=ALU.mult), reads=[y, L["g"]], writes=[ob])
                    pT = self.bank(0).bitcast(BF16).rearrange("p (a b) -> p a b", a=8)
                    for c in range(8):
                        fw.op("pe", lambda e: e.transpose(out=pT[:, c, :], in_=ob[:, c * 128:(c + 1) * 128], identity=self.identb[:]),
                              reads=[ob, self.identb], writes=[BT[0]])
                    fw.op("act", lambda e: e.activation(out=oT[:], in_=pT, func=AF.Copy), reads=[BT[0]], writes=[oT])
                    for fc in range(8):
                        bk = 2 + fc % 2
                        for kc in range(8):
                            fw.op("pe", lambda e: e.matmul(self.bank(bk)[:, 0:128], lhsT=wob[:, kc, fc * 128:(fc + 1) * 128], rhs=oT[:, kc, :], start=(kc == 0), stop=(kc == 7)),
                                  reads=[wob, oT], writes=[BT[bk]])
                        fw.op("dve", lambda e: e.tensor_scalar(out=yT[:, fc, :], in0=self.bank(bk)[:, 0:128], scalar1=self.gate1[:, fc, jc:jc + 1], scalar2=None, op0=ALU.mult),
                              reads=[BT[bk], self.gate1], writes=[yT])
                    h = hs[tile % 2]
                    fw.dma("sp", h[:], self.hbuf[b, rows, :], reads=[self.dt("h", b, tile)], writes=[h])
                    self.resid_update(b, tile, yT, 0, hs=h)
            fw.barrier()

    def scan(self, i):
        fw, nc = self.fw, self.nc
        A = self.A
        BT = self.BT
        S = self.S
        fw.pe_delay = True
        with ExitStack() as ph:
            self.ph = ph
            Ld = {n: A("S" + n, [128, 1024], F32) for n in ("lw", "kd", "b", "a", "v", "r")}
            cumS = A("cumS", [128, 1024], F32)
            Et = A("Et", [128, 1024], F32)
            Tt = A("Tt", [128, 1024], F32)
            El = {n: A("E" + n, [128, 1024], F32) for n in ("At", "Rt", "Kt", "Bt", "Kh", "Bh")}
            FT = {n: [A("F%s%d" % (n, par), [128, 8, 128], F32) for par in range(2)] for n in ("At", "Rt", "Kt", "Bt")}
            for n in FT:
                for par in range(2):
                    fw.op("pool", lambda e: e.memset(FT[n][par][:], 0.0), writes=[FT[n][par]])
            PC = A("PC", [128, 8], F32)
            STp = [A("ST%d" % par, [128, 8, 64], F32) for par in range(2)]
            LNAMES = ("LabT", "Lab", "LakT", "MrkT", "MrbT", "pA", "pAT", "pB", "pBT")
            Yt = A("Yt", [128, 1024], F32)
            Ytr = [Track("Yt0"), Track("Yt1")]
            Cs = [A("Cs%d" % k, [128, 4, 128], F32) for k in range(2)]
            bmask = lambda lev: self.cstf[:, (6 + lev) * 128:(7 + lev) * 128]

            def alias(t):
                v = T(t[:, 0:1024].rearrange("p (a b) -> p a b", a=8), "alias")
                v.tr = t.tr
                return v
            GR = []
            GR.append({"Lm": {n: A("M0" + n, [128, 8, 128], F32) for n in LNAMES}, "CTm": A("CTm0", [128, 8, 128], F32), "M1": A("M1_0", [128, 8, 128], F32),
                       "Xs": [A("X0_%d" % k, [128, 8, 64], F32) for k in range(2)]})
            lm1 = {}
            extra = [alias(self.stg[0]), alias(self.stg[1]), alias(Tt)]
            for q, n in enumerate(LNAMES):
                lm1[n] = A("M1" + n, [128, 8, 128], F32) if q < 6 else extra[q - 6]
            GR.append({"Lm": lm1, "CTm": alias(cumS), "M1": alias(Et), "Xs": [A("X1_%d" % k, [128, 8, 64], F32) for k in range(2)]})
            cst = self.cstf
            ident = self.identf
            m_lt, m_le, m_gt, m_ge, ones = (cst[:, k * 128:(k + 1) * 128] for k in (1, 2, 3, 4, 5))
            mb = lambda m: m.unsqueeze(1).to_broadcast([128, 8, 128])
            for b in range(2):
                for d in range(2):
                    for par in range(2):
                        fw.op("dve", lambda e: e.memset(STp[par][:], 0.0), reads=[STp[par]], writes=[STp[par]])
                    if d == 0:
                        order = list(range(NTILE))
                        tri = m_le
                        mT_strict, mT_incl, m_strict = m_lt, m_le, m_gt
                    else:
                        order = [1, 0] + list(range(NTILE - 1, 1, -1))
                        tri = m_ge
                        mT_strict, mT_incl, m_strict = m_gt, m_ge, m_lt
                    for ch in order:
                        rows = slice(ch * 128, (ch + 1) * 128)
                        for n, src in (("lw", "lw%d" % d), ("kd", "kd%d" % d), ("b", "b%d" % d), ("a", "a"), ("r", "r"), ("v", "v")):
                            fw.dma("sp", Ld[n][:], S[src][b, rows, :], reads=[self.dt("s_" + src, b)], writes=[Ld[n]])
                        lw = Ld["lw"]
                        for half in range(2):
                            cs = slice(half * 512, (half + 1) * 512)
                            fw.op("pe", lambda e: e.matmul(self.bank(half), lhsT=tri, rhs=lw[:, cs], start=True, stop=True), reads=[cst, lw], writes=[BT[half]])
                            fw.op("pe", lambda e: e.matmul(self.bank(2 + half), lhsT=ones, rhs=lw[:, cs], start=True, stop=True), reads=[cst, lw], writes=[BT[2 + half]])
                        for hp in range(8):
                            fw.op("pe", lambda e: e.matmul(self.bank(4)[:, 2 * hp:2 * hp + 2], lhsT=lw[:, hp * 128:(hp + 1) * 128], rhs=ones[:, 0:2], start=True, stop=True),
                                  reads=[cst, lw], writes=[BT[4]])
                        fw.op("act", lambda e: e.activation(out=PC[:], in_=self.bank(4)[:, 0:16].rearrange("p (a b) -> p a b", b=2)[:, :, 0], func=AF.Exp), reads=[BT[4]], writes=[PC])
                        fw.op("act", lambda e: e.activation(out=cumS[:], in_=self.bank2(0), func=AF.Copy), reads=[BT[0], BT[1]], writes=[cumS])
                        fw.op("act", lambda e: e.activation(out=Et[:], in_=cumS[:], func=AF.Exp), reads=[cumS], writes=[Et])
                        fw.op("pool", lambda e: e.tensor_tensor(out=El["Rt"][:], in0=Ld["r"][:], in1=Et[:], op=ALU.mult), reads=[Ld["r"], Et], writes=[El["Rt"]])
                        fw.op("dve", lambda e: e.tensor_tensor(out=Tt[:], in0=cumS[:], in1=lw[:], op=ALU.subtract), reads=[cumS, lw], writes=[Tt])
                        fw.op("act", lambda e: e.activation(out=Tt[:], in_=Tt[:], func=AF.Exp), reads=[Tt], writes=[Tt])
                        fw.op("pool", lambda e: e.tensor_tensor(out=El["At"][:], in0=Ld["a"][:], in1=Tt[:], op=ALU.mult), reads=[Ld["a"], Tt], writes=[El["At"]])
                        fw.op("act", lambda e: e.activation(out=Et[:], in_=cumS[:], func=AF.Exp, scale=-1.0), reads=[cumS], writes=[Et])
                        fw.op("pool", lambda e: e.tensor_tensor(out=El["Kt"][:], in0=Ld["kd"][:], in1=Et[:], op=ALU.mult), reads=[Ld["kd"], Et], writes=[El["Kt"]])
                        fw.op("dve", lambda e: e.tensor_tensor(out=El["Bt"][:], in0=Ld["b"][:], in1=Et[:], op=ALU.mult), reads=[Ld["b"], Et], writes=[El["Bt"]])
                        fw.op("dve", lambda e: e.tensor_tensor(out=Tt[:], in0=self.bank2(1), in1=cumS[:], op=ALU.subtract), reads=[BT[2], BT[3], cumS, Tt], writes=[Tt])
                        fw.op("act", lambda e: e.activation(out=Tt[:], in_=Tt[:], func=AF.Exp), reads=[Tt], writes=[Tt])
                        fw.op("pool", lambda e: e.tensor_tensor(out=El["Kh"][:], in0=Ld["kd"][:], in1=Tt[:], op=ALU.mult), reads=[Ld["kd"], Tt], writes=[El["Kh"]])
                        fw.op("dve", lambda e: e.scalar_tensor_tensor(out=El["Bh"][:], in0=Ld["b"][:], scalar=-1.0, in1=Tt[:], op0=ALU.mult, op1=ALU.mult),
                              reads=[Ld["b"], Tt], writes=[El["Bh"]])
                        for q, n in enumerate(("At", "Rt", "Kt", "Bt")):
                            pair = 2 + (q % 2)
                            pv = self.bank2(pair).rearrange("p (a b) -> p a b", a=8)
                            for hp in range(8):
                                fw.op("pe", lambda e: e.transpose(out=pv[:, hp, :], in_=El[n][:, hp * 128:(hp + 1) * 128], identity=ident[:]),
                                      reads=[El[n], ident], writes=[BT[2 * pair], BT[2 * pair + 1]])
                            for par in range(2):
                                ps = slice(par * 64, (par + 1) * 64)
                                if par == 0:
                                    fw.op("act", lambda e: e.activation(out=FT[n][par][ps], in_=pv[ps], func=AF.Copy), reads=[BT[2 * pair], BT[2 * pair + 1]], writes=[FT[n][par]])
                                else:
                                    fw.op("dve", lambda e: e.tensor_copy(out=FT[n][par][ps], in_=pv[ps]), reads=[BT[2 * pair], BT[2 * pair + 1]], writes=[FT[n][par]])
                        def group_gen(g):
                            R = GR[g]
                            Lm, CTm, M1, Xs_, Csg = R["Lm"], R["CTm"], R["M1"], R["Xs"], Cs[g]
                            pA, pB = 2 * g, 2 * g + 1
                            kb = 4 * g

                            def hd(hl):
                                h = g * 8 + hl
                                return h, h // 2, h % 2

                            def scores(dst, lname, rname, pair, mask, neg=False):
                                pv = self.bank2(pair).rearrange("p (a b) -> p a b", a=8)
                                trs = [BT[2 * pair], BT[2 * pair + 1]]
                                for hl in range(8):
                                    h, hp, rs = hd(hl)
                                    fw.op("pe", lambda e: e.matmul(pv[:, hl, :], lhsT=FT[lname][rs][:, hp, :], rhs=FT[rname][rs][:, hp, :], start=True, stop=True),
                                          reads=[FT[lname][rs], FT[rname][rs]], writes=trs)
                                if neg:
                                    fw.op("dve", lambda e: e.scalar_tensor_tensor(out=dst[:], in0=pv, scalar=-1.0, in1=mb(mask), op0=ALU.mult, op1=ALU.mult),
                                          reads=trs + [cst], writes=[dst])
                                else:
                                    fw.op("dve", lambda e: e.tensor_tensor(out=dst[:], in0=pv, in1=mb(mask), op=ALU.mult), reads=trs + [cst], writes=[dst])
                            scores(Lm["LabT"], "Bt", "At", pA, mT_strict)
                            scores(Lm["Lab"], "At", "Bt", pB, m_strict)
                            yield
                            Tm, TTm = Lm["pA"], Lm["pAT"]
                            for (dst, src) in ((Tm, Lm["Lab"]), (TTm, Lm["LabT"])):
                                fw.op("dve", lambda e: e.scalar_tensor_tensor(out=dst[:], in0=src[:], scalar=-1.0, in1=mb(bmask(0)), op0=ALU.mult, op1=ALU.mult),
                                      reads=[src, cst], writes=[dst])
                                fw.op("dve", lambda e: e.tensor_tensor(out=dst[:], in0=dst[:], in1=mb(ident[:]), op=ALU.add), reads=[dst, cst], writes=[dst])
                            scores(Lm["LakT"], "Kt", "At", pA, mT_strict)
                            yield
                            scores(Lm["MrkT"], "Kt", "Rt", pB, mT_incl)
                            scores(Lm["MrbT"], "Bt", "Rt", pA, mT_incl, neg=True)
                            yield
                            bx = self.bank(kb + 2).rearrange("p (a b) -> p a b", a=8)
                            for hl in range(8):
                                h, hp, rs = hd(hl)
                                fw.op("pe", lambda e: e.matmul(bx[:, hl, :], lhsT=FT["At"][rs][:, hp, :], rhs=STp[rs][:, hp, :], start=True, stop=False),
                                      reads=[FT["At"][rs], STp[rs]], writes=[BT[kb + 2]])
                                fw.op("pe", lambda e: e.matmul(bx[:, hl, :], lhsT=Lm["LakT"][:, hl, :], rhs=Ld["v"][:, h * 64:(h + 1) * 64], start=False, stop=True),
                                      reads=[Lm["LakT"], Ld["v"]], writes=[BT[kb + 2]])
                            cur, nxt = Xs_[0], Xs_[1]
                            fw.op("act", lambda e: e.activation(out=cur[:], in_=bx, func=AF.Copy), reads=[BT[kb + 2]], writes=[cur])
                            spare = [(Lm["pA"], Lm["pAT"]), (Lm["pB"], Lm["pBT"])]
                            for lev in range(1, 7):
                                nT, nTT = spare[lev % 2]
                                fw.op("pool", lambda e: e.tensor_tensor(out=CTm[:], in0=Lm["LabT"][:], in1=mb(bmask(lev)), op=ALU.mult), reads=[Lm["LabT"], cst], writes=[CTm])
                                p1 = self.bank2(pA).rearrange("p (a b) -> p a b", a=8)
                                t1 = [BT[2 * pA], BT[2 * pA + 1]]
                                for hl in range(8):
                                    fw.op("pe", lambda e: e.matmul(p1[:, hl, :], lhsT=CTm[:, hl, :], rhs=Tm[:, hl, :], start=True, stop=True),
                                          reads=[CTm, Tm], writes=t1)
                                yield
                                fw.op("act", lambda e: e.activation(out=M1[:], in_=p1, func=AF.Copy), reads=t1, writes=[M1])
                                prt = self.bank2(pB).rearrange("p (a b) -> p a b", a=8)
                                t2 = [BT[2 * pB], BT[2 * pB + 1]]
                                for hl in range(8):
                                    fw.op("pe", lambda e: e.matmul(prt[:, hl, :], lhsT=M1[:, hl, :], rhs=TTm[:, hl, :], start=True, stop=True),
                                          reads=[M1, TTm], writes=t2)
                                if lev < 6:
                                    for hl in range(8):
                                        fw.op("pe", lambda e: e.matmul(p1[:, hl, :], lhsT=TTm[:, hl, :], rhs=M1[:, hl, :], start=True, stop=True),
                                              reads=[M1, TTm], writes=t1)
                                yield
                                fw.op("dve", lambda e: e.tensor_tensor(out=nTT[:], in0=TTm[:], in1=prt, op=ALU.subtract), reads=[TTm] + t2, writes=[nTT])
                                if lev < 6:
                                    fw.op("dve", lambda e: e.tensor_tensor(out=nT[:], in0=Tm[:], in1=p1, op=ALU.subtract), reads=[Tm] + t1, writes=[nT])
                                Tm, TTm = nT, nTT
                            ba = self.bank(kb + 3).rearrange("p (a b) -> p a b", a=8)
                            for hl in range(8):
                                fw.op("pe", lambda e: e.matmul(ba[:, hl, :], lhsT=TTm[:, hl, :], rhs=cur[:, hl, :], start=True, stop=True),
                                      reads=[TTm, cur], writes=[BT[kb + 3]])
                            yield
                            fw.op("act", lambda e: e.activation(out=nxt[:], in_=ba, func=AF.Copy), reads=[BT[kb + 3]], writes=[nxt])
                            U = nxt
                            by = self.bank(kb + 2).rearrange("p (a b) -> p a b", a=8)
                            for hl in range(8):
                                h, hp, rs = hd(hl)
                                fw.op("pe", lambda e: e.matmul(by[:, hl, :], lhsT=FT["Rt"][rs][:, hp, :], rhs=STp[rs][:, hp, :], start=True, stop=False),
                                      reads=[FT["Rt"][rs], STp[rs]], writes=[BT[kb + 2]])
                                fw.op("pe", lambda e: e.matmul(by[:, hl, :], lhsT=Lm["MrkT"][:, hl, :], rhs=Ld["v"][:, h * 64:(h + 1) * 64], start=False, stop=False),
                                      reads=[Lm["MrkT"], Ld["v"]], writes=[BT[kb + 2]])
                                fw.op("pe", lambda e: e.matmul(by[:, hl, :], lhsT=Lm["MrbT"][:, hl, :], rhs=U[:, hl, :], start=False, stop=True),
                                      reads=[Lm["MrbT"], U], writes=[BT[kb + 2]])
                            fw.op("act", lambda e: e.activation(out=Yt[:, g * 512:(g + 1) * 512], in_=self.bank(kb + 2), func=AF.Copy), reads=[BT[kb + 2]], writes=[Ytr[g]])
                            bs = self.bank(kb).rearrange("p (a b) -> p a b", a=4)
                            for hq in range(4):
                                hp = g * 4 + hq
                                fw.op("pe", lambda e: e.matmul(bs[:, hq, :], lhsT=El["Kh"][:, hp * 128:(hp + 1) * 128], rhs=Ld["v"][:, hp * 128:(hp + 1) * 128], start=True, stop=False),
                                      reads=[El["Kh"], Ld["v"]], writes=[BT[kb]])
                                fw.op("pe", lambda e: e.matmul(bs[:, hq, :], lhsT=El["Bh"][:, hp * 128:(hp + 1) * 128], rhs=U[:, 2 * hq:2 * hq + 2, :].rearrange("p a b -> p (a b)"), start=False, stop=True),
                                      reads=[El["Bh"], U], writes=[BT[kb]])
                            fw.op("act", lambda e: e.activation(out=Csg[:], in_=bs, func=AF.Copy), reads=[BT[kb]], writes=[Csg])

                        gens = [group_gen(0), group_gen(1)]
                        while gens:
                            for gen in list(gens):
                                try:
                                    next(gen)
                                except StopIteration:
                                    gens.remove(gen)
                        for par in range(2):
                            ps = slice(par * 64, (par + 1) * 64)
                            fw.op("dve", lambda e: e.tensor_tensor(out=STp[par][ps], in0=STp[par][ps], in1=PC[ps].unsqueeze(2).to_broadcast([64, 8, 64]), op=ALU.mult),
                                  reads=[STp[par], PC], writes=[STp[par]])
                            for g in range(2):
                                fw.op("dve", lambda e: e.tensor_tensor(out=STp[par][ps, g * 4:(g + 1) * 4, :], in0=STp[par][ps, g * 4:(g + 1) * 4, :], in1=Cs[g][ps, :, par * 64:(par + 1) * 64], op=ALU.add),
                                      reads=[STp[par], Cs[g]], writes=[STp[par]])
                        fw.dma("pool", S["y%d" % d][b, rows, :], Yt[:], reads=Ytr, writes=[self.dt("s_y%d" % d, b)], disjoint=True)
            fw.barrier()
        fw.pe_delay = False


def full_plan():
    plan = []
    for i in range(DEPTH):
        plan.append(("mod", i))
        plan.append(("even" if i % 2 == 0 else "odd", i))
        plan.append(("ffn", i))
    return plan


def build_program(plan):
    B = Builder()
    B.setup()
    for step, i in plan:
        last = i == DEPTH - 1
        if step == "mod":
            B.modulation(i)
        elif step == "ffn":
            B.ffn(i, last)
        elif step == "even":
            B.even_mixer(i, last)
        elif step == "odd":
            B.odd_mixer(i, last)
        elif step == "reload":
            B.reload()
    B.fw.finish([B.dt("h", b, t) for b in range(2) for t in range(NTILE)])
    return B


def fm(v):
    v = np.asarray(v, np.float32).reshape(-1, 128)
    return np.ascontiguousarray(v.T)


def host_consts(inputs):
    lay, nv = vec_layout()
    vecs = np.zeros((128, nv), np.float32)

    def put(name, arr):
        o, n = lay[name]
        assert arr.shape == (128, n), (name, arr.shape, n)
        vecs[:, o:o + n] = arr
    for i in range(DEPTH):
        put("nmix%d" % i, fm(inputs["norm_mix"][i]))
        put("nffn%d" % i, fm(inputs["norm_ffn"][i]))
    for j in range(2):
        put("qn%d" % j, fm(inputs["mla_q_norm"][j]))
        put("kvn%d" % j, fm(inputs["mla_kv_norm"][j]))
        cw = np.asarray(inputs["conv_w"][j], np.float32)
        put("convw%d" % j, np.ascontiguousarray(cw.reshape(31, 4, 128).transpose(2, 1, 0).reshape(128, 4 * 31)))
        put("convb%d" % j, fm(inputs["conv_b"][j]))
        put("clng%d" % j, fm(inputs["conv_ln_g"][j]))
        put("clnb%d" % j, fm(inputs["conv_ln_b"][j]))
        mu = np.asarray(inputs["rw_mu"][j], np.float32)
        put("mu%d" % j, np.ascontiguousarray(mu.reshape(12, 8, 128).transpose(2, 0, 1).reshape(128, 96)))
    s = np.arange(128)[:, None]
    t = np.arange(128)[None, :]
    bms = []
    for lev in range(7):
        n = 1 << lev
        bms.append(((s // (2 * n)) == (t // (2 * n))) & ((s // n) != (t // n)))
    cst = np.concatenate([np.eye(128), s < t, s <= t, s > t, s >= t, np.ones((128, 128))] + bms, axis=1).astype(np.float32)
    half = 4
    inv = 10000.0 ** (-np.arange(0, 16, 2, dtype=np.float32) / 16.0)
    tl = np.arange(TL)
    row = (tl // GW).astype(np.float32)[:, None] * inv
    col = (tl % GW).astype(np.float32)[:, None] * inv
    cr, sr, cc, sc = np.cos(row), np.sin(row), np.cos(col), np.sin(col)
    cos = np.concatenate([cr, cr, cc, cc], axis=1)
    sin = np.concatenate([-sr, sr, -sc, sc], axis=1)
    rope = np.zeros((TT, 64), np.float32)
    rope[:TC, 0:32] = 1.0
    rope[TC:, 0:32] = cos
    rope[TC:, 32:64] = sin
    return vecs, cst, rope


def core_inputs(inputs, core, vecs, cst, rope, hin_override=None):
    b0 = 2 * core
    x = np.asarray(inputs["x"], np.float32)
    ctx = np.asarray(inputs["ctx"], np.float32)
    if hin_override is not None:
        hin = hin_override
    else:
        hin = np.concatenate([ctx[b0:b0 + 2], x[b0:b0 + 2]], axis=1)
    cond = np.stack([inputs["c"][b0], inputs["c"][b0 + 1], inputs["c_ctx"]]).astype(np.float32)
    condT = np.ascontiguousarray(cond.reshape(3, 8, 128).transpose(2, 1, 0))
    m = {"hin": np.ascontiguousarray(hin), "condT": condT, "cst": cst, "rope": rope, "vecs": vecs}
    for n in W_NAMES:
        m[n] = np.ascontiguousarray(np.asarray(inputs[n], np.float32).reshape(W_SHAPES[n]))
    return m


def kernel(**inputs):
    vecs, cst, rope = host_consts(inputs)
    B = build_program(full_plan())
    in_maps = [core_inputs(inputs, c, vecs, cst, rope) for c in range(8)]
    res = run_bass_kernel_spmd(B.nc, in_maps, core_ids=list(range(8)))
    outs = [r["hout"][:, TC:, :] for r in res.results]
    return np.ascontiguousarray(np.concatenate(outs, axis=0).astype(np.float32))
```

```python
import math
from contextlib import ExitStack
import numpy as np
import concourse.bass as bass
import concourse.mybir as mybir
from concourse.bass_utils import run_bass_kernel_spmd

F32 = mybir.dt.float32
BF16 = mybir.dt.bfloat16
ALU = mybir.AluOpType
AF = mybir.ActivationFunctionType
AX = mybir.AxisListType

D = 1024
TC = 256
TL = 2048
TT = TC + TL
NTILE = TT // 128
EPS = 1e-6
DEPTH = 4
FH = 2816
NHC = FH // 128
EV_IN = 1568
GW = 64

class Track:
    __slots__ = ("w", "r", "name", "excl")

    def __init__(self, name=""):
        self.w = []
        self.r = []
        self.name = name
        self.excl = False


class T:
    def __init__(self, h, name=""):
        self.h = h
        self.tr = Track(name)

    def __getitem__(self, idx):
        return self.h[idx]


def _trk(x):
    if isinstance(x, Track):
        return x
    return x.tr


class FW:
    SEM_ROT = 24000

    def __init__(self, nc):
        self.nc = nc
        self.eng = {"pe": nc.tensor, "act": nc.scalar, "dve": nc.vector, "pool": nc.gpsimd, "sp": nc.sync}
        self.sems = {}
        self.cnt = {}
        self.seen = {e: {} for e in self.eng}
        self.nsem = 0
        for e in ("pe", "act", "dve", "pool"):
            self._new_sem(e)
        self.dma_sems = {}
        self.dma_rr = {}
        for q in ("sp", "act", "pool"):
            self.dma_sems[q] = [[self._alloc_sem("dma_%s_%d" % (q, i)), 0] for i in range(4 if q == "sp" else 2)]
            self.dma_rr[q] = 0
        self.ninst = {e: 0 for e in self.eng}
        self.pe_delay = False
        self.dly = None

    def _alloc_sem(self, name):
        self.nsem += 1
        cm = self.nc.semaphore(name)
        return cm.__enter__()

    def _new_sem(self, e):
        self.sems[e] = self._alloc_sem("s_%s_%d" % (e, self.nsem))
        self.cnt[e] = 0

    def _wait(self, e, ticket):
        sem, val, src = ticket
        key = id(sem)
        if self.seen[e].get(key, 0) >= val:
            return
        self.eng[e].wait_ge(sem, val)
        self.seen[e][key] = val
        self.ninst[e] += 1
        if self.pe_delay and src == "pe" and self.dly is not None:
            if e == "dve":
                self.eng[e].memset(self.dly[0][:, 0:256], 0.0)
            elif e == "act":
                self.eng[e].activation(out=self.dly[1][:, 0:192], in_=self.dly[2][:, 0:192], func=AF.Copy)

    def _deps(self, e, reads, writes, is_dma=False, disjoint=False):
        deps = []
        for r in reads:
            t = _trk(r)
            deps.extend(t.w)
            if t.excl:
                deps.extend(t.r)
        for w in writes:
            t = _trk(w)
            if not (disjoint and all(x[2] == "dma" for x in t.w)):
                deps.extend(t.w)
            deps.extend(t.r)
        for d in deps:
            if d[2] == e and e == "pe" and not is_dma:
                continue
            self._wait(e, d)

    @staticmethod
    def _compact(lst):
        best = {}
        for tk in lst:
            k = id(tk[0])
            if k not in best or best[k][1] < tk[1]:
                best[k] = tk
        return list(best.values())

    def _commit(self, ticket, reads, writes, disjoint=False):
        for w in writes:
            t = _trk(w)
            if disjoint and all(x[2] == "dma" for x in t.w) and not t.r:
                t.w = self._compact(t.w + [ticket])
            else:
                t.w = [ticket]
            t.r = []
        for r in reads:
            t = _trk(r)
            if ticket in t.w:
                continue
            t.r.append(ticket)
            if len(t.r) > 48:
                t.r = self._compact(t.r)

    def op(self, e, fn, reads=(), writes=()):
        self._deps(e, reads, writes)
        if self.cnt[e] >= self.SEM_ROT:
            self._new_sem(e)
        ins = fn(self.eng[e])
        self.cnt[e] += 1
        ins.then_inc(self.sems[e], 1)
        ticket = (self.sems[e], self.cnt[e], e)
        self._commit(ticket, reads, writes)
        self.ninst[e] += 1
        return ticket

    def dma(self, q, out, in_, reads=(), writes=(), disjoint=False, **kw):
        slot = self.dma_sems[q][self.dma_rr[q]]
        self.dma_rr[q] = (self.dma_rr[q] + 1) % len(self.dma_sems[q])
        sem, cnt = slot
        if cnt > 0:
            self._wait(q, (sem, cnt, "dma"))
        self._deps(q, reads, writes, is_dma=True, disjoint=disjoint)
        ins = self.eng[q].dma_start(out=out, in_=in_, allow_slow_non_contiguous=True, **kw)
        slot[1] = cnt + 16
        ins.then_inc(sem, 16)
        ticket = (sem, slot[1], "dma")
        self._commit(ticket, reads, writes, disjoint=disjoint)
        self.ninst[q] += 1
        return ticket

    def barrier(self):
        tickets = []
        for e in ("pe", "act", "dve", "pool"):
            if self.cnt[e] > 0:
                tickets.append((self.sems[e], self.cnt[e], e))
        for q in self.dma_sems:
            for sem, cnt in self.dma_sems[q]:
                if cnt > 0:
                    tickets.append((sem, cnt, "dma"))
        for e in self.eng:
            for tk in tickets:
                if tk[2] == e and e == "pe":
                    continue
                self._wait(e, tk)

    def finish(self, out_tracks):
        for t in out_tracks:
            t = _trk(t)
            for x in t.w:
                self._wait("sp", x)
        self.barrier()


def vec_layout():
    lay = {}
    off = 0

    def add(name, n):
        nonlocal off
        lay[name] = (off, n)
        off += n
    for i in range(DEPTH):
        add("nmix%d" % i, 8)
        add("nffn%d" % i, 8)
    for j in range(2):
        add("qn%d" % j, 2)
        add("kvn%d" % j, 2)
        add("convw%d" % j, 4 * 31)
        add("convb%d" % j, 4)
        add("clng%d" % j, 4)
        add("clnb%d" % j, 4)
        add("mu%d" % j, 2 * 6 * 8)
    return lay, off


W_NAMES = ["ada_w", "ada_b", "ffn_w1", "ffn_w3", "ffn_w2", "even_w_in", "mla_wq_b", "mla_wkv_b", "mla_q_qk",
           "mla_k_qk", "even_w_out", "rw_wr", "rw_wk", "rw_wv", "rw_w0", "rw_w1", "rw_w2", "rw_a0", "rw_a1",
           "rw_a2", "rw_v0", "rw_v1", "rw_v2", "rw_k_k", "rw_k_a", "rw_r_k", "rw_g1", "rw_g2", "rw_ln_g",
           "rw_ln_b", "rw_wo"]
W_SHAPES = {
    "ada_w": [4, 1024, 6144], "ada_b": [4, 6144], "ffn_w1": [4, 1024, 2816], "ffn_w3": [4, 1024, 2816],
    "ffn_w2": [4, 2816, 1024], "even_w_in": [2, 1024, 1568], "mla_wq_b": [2, 256, 768], "mla_wkv_b": [2, 256, 1024],
    "mla_q_qk": [2, 96], "mla_k_qk": [2, 96], "even_w_out": [2, 1024, 1024], "rw_wr": [2, 1024, 1024],
    "rw_wk": [2, 1024, 1024], "rw_wv": [2, 1024, 1024], "rw_w0": [2, 2, 1024], "rw_w1": [2, 2, 1024, 64],
    "rw_w2": [2, 2, 64, 1024], "rw_a0": [2, 2, 1024], "rw_a1": [2, 2, 1024, 64], "rw_a2": [2, 2, 64, 1024],
    "rw_v0": [1, 1024], "rw_v1": [1, 1024, 32], "rw_v2": [1, 32, 1024], "rw_k_k": [2, 1024], "rw_k_a": [2, 1024],
    "rw_r_k": [2, 2, 1024], "rw_g1": [2, 1024, 160], "rw_g2": [2, 160, 1024], "rw_ln_g": [2, 1024],
    "rw_ln_b": [2, 1024], "rw_wo": [2, 1024, 1024],
}


class Builder:
    def __init__(self, nlayers=DEPTH, stop_after=None):
        self.nlayers = nlayers
        import os
        self.stop_after = os.environ.get('STOP_AFTER')
        nc = bass.Bass("TRN2", target_bir_lowering=False)
        self.nc = nc
        self.fw = FW(nc)
        self.uid = 0
        self.lay, self.nv = vec_layout()
        inp = lambda name, shape, dt=F32: nc.dram_tensor(name, shape, dt, kind="ExternalInput").ap()
        self.hin = inp("hin", [2, TT, D])
        self.condT = inp("condT", [128, 8, 3])
        self.cst = inp("cst", [128, 13 * 128])
        self.rope = inp("rope", [TT, 64])
        self.vecs = inp("vecs", [128, self.nv])
        self.W = {n: inp(n, W_SHAPES[n]) for n in W_NAMES}
        self.hin2 = inp("hin2", [2, TT, D]) if os.environ.get("DBG_RELOAD") else None
        self.hout = nc.dram_tensor("hout", [2, TT, D], F32, kind="ExternalOutput").ap()
        scr = lambda name, shape, dt=F32: nc.dram_tensor(name, shape, dt).ap()
        self.hbuf = self.hout
        self.qT_d = scr("qT_d", [2, 8, 96, TT], BF16)
        self.kT_d = scr("kT_d", [2, 8, 96, TT], BF16)
        self.V_d = scr("V_d", [2, 8, 128, NTILE, 68], BF16)
        self.YW = 15 + TC + 15 + 15 + TL + 15
        self.yT_d = scr("yT_d", [2, 4, 128, self.YW], F32)
        self.mixT_d = scr("mixT_d", [2, 4, 128, TT], BF16)
        self.AW = 1 + TC + 1 + 1 + TL + 1
        self.aT_d = scr("aT_d", [2, 8, 128, self.AW], F32)
        names = ["r", "v", "a", "g", "lw0", "lw1", "kd0", "kd1", "b0", "b1", "y0", "y1", "vf"]
        if os.environ.get("DBG_S"):
            self.S = {n: nc.dram_tensor("s_" + n, [2, TT, D], F32, kind="ExternalOutput").ap() for n in names}
        else:
            self.S = {n: scr("s_" + n, [2, TT, D], F32) for n in names}
        self.dtrk = {}
        self.ph = None
        self.PP = [nc.alloc_psum_tensor("pp%d" % i, [128, 1024], F32) for i in range(4)]
        self.BT = [Track("bank%d" % k) for k in range(8)]
        for t in self.BT:
            t.excl = True

    def dt(self, *key):
        if key not in self.dtrk:
            self.dtrk[key] = Track(str(key))
        return self.dtrk[key]

    def A(self, name, shape, dt, stack=None):
        self.uid += 1
        st = stack if stack is not None else self.ph
        h = st.enter_context(self.nc.sbuf_tensor("%s_%d" % (name, self.uid), shape, dt))
        return T(h, name)

    def bank(self, k):
        return self.PP[k // 2][:, (k % 2) * 512:(k % 2 + 1) * 512]

    def bank2(self, m):
        return self.PP[m][:, :]

    def vcol(self, name, i=0, n=1):
        o, _ = self.lay[name]
        return self.vec[:, o + i:o + i + n]

    def rstd(self, ss_ap, out_ap, n, eps, trk):
        fw = self.fw
        fw.op("dve", lambda e: e.tensor_scalar(out=out_ap, in0=ss_ap, scalar1=1.0 / n, scalar2=eps, op0=ALU.mult, op1=ALU.add),
              reads=[trk], writes=[trk])
        fw.op("act", lambda e: e.activation(out=out_ap, in_=out_ap, func=AF.Sqrt), reads=[trk], writes=[trk])
        fw.op("dve", lambda e: e.reciprocal(out=out_ap, in_=out_ap), reads=[trk], writes=[trk])

    def load_w(self, dst, src, KC, N, eng="pool", rows=128, col0=0):
        fw = self.fw
        SW = 1408
        for kc in range(KC):
            for n0 in range(0, N, SW):
                n1 = min(N, n0 + SW)
                st = self.stg[self.stg_i % 2]
                self.stg_i += 1
                fw.dma("sp", st[0:rows, 0:n1 - n0], src[kc * rows:(kc + 1) * rows, n0:n1], writes=[st])
                if eng == "act":
                    fw.op(eng, lambda e: e.activation(out=dst[0:rows, kc, col0 + n0:col0 + n1], in_=st[0:rows, 0:n1 - n0], func=AF.Copy), reads=[st], writes=[dst])
                else:
                    fw.op(eng, lambda e: e.tensor_copy(out=dst[0:rows, kc, col0 + n0:col0 + n1], in_=st[0:rows, 0:n1 - n0]), reads=[st], writes=[dst])

    def norm_T(self, hs, scl, shf, j, dst_ap, dst_trk, dt):
        fw = self.fw
        st, junk = self.n_st, self.n_junk
        fw.op("act", lambda e: e.activation(out=junk[:], in_=hs[:], func=AF.Square, accum_out=st[:, 0:1]),
              reads=[hs], writes=[junk, st])
        self.rstd(st[:, 0:1], st[:, 1:2], D, EPS, st)
        if dt == BF16:
            xn = self.n_xnb
            pv = self.bank(0).bitcast(BF16).rearrange("p (a b) -> p a b", a=8)
            ident = self.identb
            ptr = [self.BT[0]]
        else:
            xn = self.n_xnf
            pv = self.bank2(0).rearrange("p (a b) -> p a b", a=8)
            ident = self.identf
            ptr = [self.BT[0], self.BT[1]]
        fw.op("dve", lambda e: e.tensor_scalar(out=xn[:], in0=hs[:], scalar1=st[:, 1:2], scalar2=None, op0=ALU.mult),
              reads=[hs, st], writes=[xn])
        for kc in range(8):
            fw.op("pe", lambda e: e.transpose(out=pv[:, kc, :], in_=xn[:, kc * 128:(kc + 1) * 128], identity=ident[:]),
                  reads=[xn, ident], writes=ptr)
        fw.op("dve", lambda e: e.tensor_tensor(out=dst_ap, in0=pv, in1=scl[:, :, j:j + 1].to_broadcast([128, 8, 128]), op=ALU.mult),
              reads=ptr + [scl], writes=[dst_trk])
        fw.op("dve", lambda e: e.tensor_tensor(out=dst_ap, in0=dst_ap, in1=shf[:, :, j:j + 1].to_broadcast([128, 8, 128]), op=ALU.add),
              reads=[shf, dst_trk], writes=[dst_trk])

    def resid_update(self, b, tile, yT, c0, hs=None):
        fw = self.fw
        if hs is None:
            hs = self.r_hs[tile % 2]
            fw.dma("sp", hs[:], self.hbuf[b, tile * 128:(tile + 1) * 128, :], reads=[self.dt("h", b, tile)], writes=[hs])
        pv = self.bank2(3)
        ptr = [self.BT[6], self.BT[7]]
        for fc in range(8):
            fw.op("pe", lambda e: e.transpose(out=pv[:, fc * 128:(fc + 1) * 128], in_=yT[:, fc, c0:c0 + 128], identity=self.identf[:]),
                  reads=[yT, self.identf], writes=ptr)
        fw.op("dve", lambda e: e.tensor_tensor(out=hs[:], in0=hs[:], in1=pv, op=ALU.add), reads=ptr + [hs], writes=[hs])
        fw.dma("pool", self.hbuf[b, tile * 128:(tile + 1) * 128, :], hs[:], reads=[hs], writes=[self.dt("h", b, tile)])

    def setup(self):
        fw, nc = self.fw, self.nc
        self.gs = ExitStack()
        G = lambda name, shape, dt: self.A(name, shape, dt, stack=self.gs)
        self.cstf = G("cstf", [128, 13 * 128], F32)
        fw.dma("sp", self.cstf[:], self.cst, writes=[self.cstf])
        self.identf = T(self.cstf[:, 0:128]); self.identf.tr = self.cstf.tr
        self.identb = G("identb", [128, 128], BF16)
        fw.op("dve", lambda e: e.tensor_copy(out=self.identb[:], in_=self.cstf[:, 0:128]), reads=[self.cstf], writes=[self.identb])
        self.vec = G("vec", [128, self.nv], F32)
        fw.dma("sp", self.vec[:], self.vecs, writes=[self.vec])
        self.scT = G("scT", [128, 8, 3], F32)
        fw.dma("sp", self.scT[:], self.condT, writes=[self.scT])
        fw.op("act", lambda e: e.activation(out=self.scT[:], in_=self.scT[:], func=AF.Silu), reads=[self.scT], writes=[self.scT])
        self.modT = G("modT", [128, 48, 3], F32)
        self.scl1 = G("scl1", [128, 8, 3], F32)
        self.scl2 = G("scl2", [128, 8, 3], F32)
        self.n_st = G("n_st", [128, 4], F32)
        self.n_junk = G("n_junk", [128, 1024], BF16)
        self.n_xnb = G("n_xnb", [128, 1024], BF16)
        self.stg = [G("stg%d" % i, [128, 1408], F32) for i in range(2)]
        self.stg_i = 0
        self.zero = G("zero", [128, 64], F32)
        fw.op("dve", lambda e: e.memset(self.zero[:], 0.0), writes=[self.zero])
        dl0 = G("dly0", [128, 256], F32)
        fw.op("dve", lambda e: e.memset(dl0[:], 0.0), writes=[dl0])
        fw.op("dve", lambda e: e.memset(self.n_junk[:], 0.0), writes=[self.n_junk])
        fw.dly = None
        for b in range(2):
            for t in range(NTILE):
                fw.dma("sp" if t % 2 == 0 else "pool", self.hbuf[b, t * 128:(t + 1) * 128, :], self.hin[b, t * 128:(t + 1) * 128, :],
                       writes=[self.dt("h", b, t)])
        for b in range(2):
            for c in range(4):
                for o in (0, 15 + TC, 30 + TC, 30 + TC + 15 + TL):
                    fw.dma("pool", self.yT_d[b, c, :, o:o + 15], self.zero[:, 0:15], reads=[self.zero], writes=[self.dt("ypad")])
            for c in range(8):
                for o in (0, 1 + TC, 2 + TC, 3 + TC + TL):
                    fw.dma("pool", self.aT_d[b, c, :, o:o + 1], self.zero[:, 0:1], reads=[self.zero], writes=[self.dt("apad")])

    def reload(self):
        for b in range(2):
            for t in range(NTILE):
                self.fw.dma("sp", self.hbuf[b, t * 128:(t + 1) * 128, :], self.hin2[b, t * 128:(t + 1) * 128, :],
                            reads=[self.dt("h", b, t)], writes=[self.dt("h", b, t)])
        self.fw.barrier()

    def modulation(self, i):
        fw, nc = self.fw, self.nc
        fw.pe_delay = True
        with ExitStack() as ph:
            self.ph = ph
            mrow = self.A("mrow", [3, 6144], F32)
            adab = self.A("adab", [3, 6144], F32)
            wst = [self.A("wst%d" % k, [128, 8, 512], F32) for k in range(2)]
            fw.dma("pool", adab[:], self.W["ada_b"][i].partition_broadcast(3), writes=[adab])
            pm = self.bank(2)
            for cb in range(12):
                st = wst[cb % 2]
                fw.dma("sp", st[:], self.W["ada_w"][i, :, cb * 512:(cb + 1) * 512].rearrange("(kc p) n -> p kc n", p=128), writes=[st])
                for kc in range(8):
                    fw.op("pe", lambda e: e.matmul(pm[0:3, :], lhsT=self.scT[:, kc, :], rhs=st[:, kc, :], start=(kc == 0), stop=(kc == 7)),
                          reads=[self.scT, st], writes=[self.BT[2]])
                fw.op("dve", lambda e: e.tensor_tensor(out=mrow[:, cb * 512:(cb + 1) * 512], in0=pm[0:3, :], in1=adab[:, cb * 512:(cb + 1) * 512], op=ALU.add),
                      reads=[self.BT[2], adab], writes=[mrow])
            pt = self.bank(3)[:, 0:192].rearrange("p (c k) -> p c k", k=4)
            for c in range(48):
                fw.op("pe", lambda e: e.transpose(out=pt[:, c, 0:3], in_=mrow[0:3, c * 128:(c + 1) * 128], identity=self.identf[0:3, 0:3]),
                      reads=[mrow, self.identf], writes=[self.BT[3]])
            fw.op("dve", lambda e: e.tensor_copy(out=self.modT[:], in_=pt[:, :, 0:3]), reads=[self.BT[3]], writes=[self.modT])
            for (scl, vname, c0) in ((self.scl1, "nmix%d" % i, 8), (self.scl2, "nffn%d" % i, 32)):
                fw.op("dve", lambda e: e.tensor_scalar(out=scl[:], in0=self.modT[:, c0:c0 + 8, :], scalar1=1.0, scalar2=None, op0=ALU.add),
                      reads=[self.modT], writes=[scl])
                fw.op("dve", lambda e: e.tensor_tensor(out=scl[:], in0=scl[:], in1=self.vcol(vname, 0, 8).unsqueeze(2).to_broadcast([128, 8, 3]), op=ALU.mult),
                      reads=[scl, self.vec], writes=[scl])
            fw.barrier()
        fw.pe_delay = False
        self.shf1 = T(self.modT[:, 0:8, :]); self.shf1.tr = self.modT.tr
        self.gate1 = T(self.modT[:, 16:24, :]); self.gate1.tr = self.modT.tr
        self.shf2 = T(self.modT[:, 24:32, :]); self.shf2.tr = self.modT.tr
        self.gate2 = T(self.modT[:, 40:48, :]); self.gate2.tr = self.modT.tr

    def ffn(self, i, last):
        fw, nc = self.fw, self.nc
        with ExitStack() as ph:
            self.ph = ph
            w1b = self.A("w1b", [128, 8, FH], BF16)
            w3b = self.A("w3b", [128, 8, FH], BF16)
            w2b = self.A("w2b", [128, NHC, D], BF16)
            self.load_w(w1b, self.W["ffn_w1"][i], 8, FH, "pool")
            self.load_w(w3b, self.W["ffn_w3"][i], 8, FH, "act")
            self.load_w(w2b, self.W["ffn_w2"][i], NHC, D, "pool")
            TB = 256
            fT = self.A("fT", [128, 8, TB], BF16)
            gT = self.A("gT", [128, NHC, TB], BF16)
            yT = self.A("yT", [128, 8, TB], F32)
            sl = [self.A("sl%d" % k, [128, TB], F32) for k in range(2)]
            hs2 = [self.A("hs2_%d" % k, [128, 1024], F32) for k in range(2)]
            for b in range(2):
                for blk in range(TT // TB):
                    if last and blk == 0:
                        continue
                    j = 2 if blk == 0 else b
                    for tt in range(2):
                        tile = blk * 2 + tt
                        fw.dma("sp", hs2[tt][:], self.hbuf[b, tile * 128:(tile + 1) * 128, :], reads=[self.dt("h", b, tile)], writes=[hs2[tt]])
                        self.norm_T(hs2[tt], self.scl2, self.shf2, j, fT[:, :, tt * 128:(tt + 1) * 128], fT, BF16)
                    for hc in range(NHC):
                        k1 = 2 + (hc % 2)
                        k3 = 4 + (hc % 2)
                        p1 = self.bank(k1)[:, 0:TB]
                        p3 = self.bank(k3)[:, 0:TB]
                        for kc in range(8):
                            fw.op("pe", lambda e: e.matmul(p1, lhsT=w1b[:, kc, hc * 128:(hc + 1) * 128], rhs=fT[:, kc, :], start=(kc == 0), stop=(kc == 7)),
                                  reads=[w1b, fT], writes=[self.BT[k1]])
                        for kc in range(8):
                            fw.op("pe", lambda e: e.matmul(p3, lhsT=w3b[:, kc, hc * 128:(hc + 1) * 128], rhs=fT[:, kc, :], start=(kc == 0), stop=(kc == 7)),
                                  reads=[w3b, fT], writes=[self.BT[k3]])
                        s = sl[hc % 2]
                        fw.op("act", lambda e: e.activation(out=s[:], in_=p1, func=AF.Silu), reads=[self.BT[k1]], writes=[s])
                        fw.op("dve", lambda e: e.tensor_tensor(out=gT[:, hc, :], in0=s[:], in1=p3, op=ALU.mult),
                              reads=[s, self.BT[k3]], writes=[gT])
                    for fc in range(8):
                        ky = 2 + (fc % 2)
                        py = self.bank(ky)[:, 0:TB]
                        for hc in range(NHC):
                            fw.op("pe", lambda e: e.matmul(py, lhsT=w2b[:, hc, fc * 128:(fc + 1) * 128], rhs=gT[:, hc, :], start=(hc == 0), stop=(hc == NHC - 1)),
                                  reads=[w2b, gT], writes=[self.BT[ky]])
                        fw.op("dve", lambda e: e.tensor_scalar(out=yT[:, fc, :], in0=py, scalar1=self.gate2[:, fc, j:j + 1], scalar2=None, op0=ALU.mult),
                              reads=[self.BT[ky], self.gate2], writes=[yT])
                    for tt in range(2):
                        self.resid_update(b, blk * 2 + tt, yT, tt * 128, hs=hs2[tt])
            fw.barrier()

    def rope_apply(self, xf, rp, sw, t1, eng="dve"):
        fw = self.fw
        for (d0, s0) in ((0, 72), (8, 64), (16, 88), (24, 80)):
            fw.op("pool", lambda e: e.tensor_copy(out=sw[:, :, d0:d0 + 8], in_=xf[:, :, s0:s0 + 8]), reads=[xf], writes=[sw])
        fw.op("dve", lambda e: e.tensor_tensor(out=t1[:], in0=xf[:, :, 64:96], in1=rp[:, 0:32].unsqueeze(1).to_broadcast([128, 8, 32]), op=ALU.mult),
              reads=[xf, rp], writes=[t1])
        fw.op("dve", lambda e: e.tensor_tensor(out=sw[:], in0=sw[:], in1=rp[:, 32:64].unsqueeze(1).to_broadcast([128, 8, 32]), op=ALU.mult),
              reads=[sw, rp], writes=[sw])
        fw.op("dve", lambda e: e.tensor_tensor(out=xf[:, :, 64:96], in0=t1[:], in1=sw[:], op=ALU.add), reads=[t1, sw], writes=[xf])

    def even_mixer(self, i, last):
        fw, nc = self.fw, self.nc
        j = i // 2
        W = self.W
        A = self.A
        BT = self.BT
        with ExitStack() as ph:
            self.ph = ph
            winb = A("winb", [128, 8, EV_IN], BF16)
            self.load_w(winb, W["even_w_in"][j], 8, EV_IN, "pool")
            wqb = A("wqb", [128, 2, 768], BF16)
            self.load_w(wqb, W["mla_wq_b"][j], 2, 768, "act")
            wkvb = A("wkvb", [128, 2, 1024], BF16)
            self.load_w(wkvb, W["mla_wkv_b"][j], 2, 1024, "act")
            qg = A("qg", [128, 96], F32)
            kg = A("kg", [128, 96], F32)
            fw.dma("sp", qg[:], W["mla_q_qk"][j].partition_broadcast(128), writes=[qg])
            fw.dma("sp", kg[:], W["mla_k_qk"][j].partition_broadcast(128), writes=[kg])
            fw.op("dve", lambda e: e.tensor_scalar(out=qg[:], in0=qg[:], scalar1=float(96 ** -0.5), scalar2=None, op0=ALU.mult), reads=[qg], writes=[qg])
            hs = [A("hs%d" % k, [128, 1024], F32) for k in range(2)]
            aT = A("aT", [128, 8, 128], BF16)
            zs = A("zs", [128, 544], F32)
            zn = A("zn", [128, 512], BF16)
            znT = A("znT", [128, 4, 128], BF16)
            st = A("st", [128, 32], F32)
            qf = A("qf", [128, 8, 96], F32)
            kf = A("kf", [128, 8, 96], F32)
            sq = A("sq", [128, 8, 96], F32)
            sw = A("sw", [128, 8, 32], F32)
            t1 = A("t1", [128, 8, 32], F32)
            rp = A("rp", [128, 64], F32)
            qb = A("qb", [128, 8, 96], BF16)
            kb = A("kb", [128, 8, 96], BF16)
            qTt = A("qTt", [96, 8, 128], BF16)
            kTt = A("kTt", [96, 8, 128], BF16)
            Vt = A("Vt", [128, 8, 68], BF16)
            sg = A("sg", [128, 4, 128], F32)
            yt = A("yt", [128, 4, 128], F32)
            fw.op("dve", lambda e: e.memset(Vt[:], 1.0), writes=[Vt])
            for b in range(2):
                for tile in range(NTILE):
                    jc = 2 if tile < 2 else b
                    h = hs[tile % 2]
                    fw.dma("sp", h[:], self.hbuf[b, tile * 128:(tile + 1) * 128, :], reads=[self.dt("h", b, tile)], writes=[h])
                    fw.dma("sp", rp[:], self.rope[tile * 128:(tile + 1) * 128, :], writes=[rp])
                    self.norm_T(h, self.scl1, self.shf1, jc, aT[:], aT, BF16)
                    for (n0, n1, bk) in ((0, 512, 2), (512, 544, 3)):
                        for kc in range(8):
                            fw.op("pe", lambda e: e.matmul(self.bank(bk)[:, 0:n1 - n0], lhsT=aT[:, kc, :], rhs=winb[:, kc, n0:n1], start=(kc == 0), stop=(kc == 7)),
                                  reads=[aT, winb], writes=[BT[bk]])
                    fw.op("act", lambda e: e.activation(out=zs[:, 0:512], in_=self.bank(2), func=AF.Copy), reads=[BT[2]], writes=[zs])
                    fw.op("act", lambda e: e.activation(out=zs[:, 512:544], in_=self.bank(3)[:, 0:32], func=AF.Copy), reads=[BT[3]], writes=[zs])
                    for c in range(2):
                        fw.op("act", lambda e: e.activation(out=sq[:, 0:2, :].rearrange("p a b -> p (a b)")[:, 0:256] if False else self.n_junk[:, 0:256], in_=zs[:, c * 256:(c + 1) * 256], func=AF.Square, accum_out=st[:, c:c + 1]),
                              reads=[zs], writes=[self.n_junk, st])
                    self.rstd(st[:, 0:2], st[:, 2:4], 256, EPS, st)
                    for c in range(2):
                        fw.op("dve", lambda e: e.tensor_scalar(out=zn[:, c * 256:(c + 1) * 256], in0=zs[:, c * 256:(c + 1) * 256], scalar1=st[:, 2 + c:3 + c], scalar2=None, op0=ALU.mult),
                              reads=[zs, st], writes=[zn])
                    pzT = self.bank(0).bitcast(BF16).rearrange("p (a b) -> p a b", a=8)
                    for c in range(4):
                        fw.op("pe", lambda e: e.transpose(out=pzT[:, c, :], in_=zn[:, c * 128:(c + 1) * 128], identity=self.identb[:]),
                              reads=[zn, self.identb], writes=[BT[0]])
                    for (c0, vn) in ((0, "qn%d" % j), (2, "kvn%d" % j)):
                        fw.op("dve", lambda e: e.tensor_tensor(out=znT[:, c0:c0 + 2, :], in0=pzT[:, c0:c0 + 2, :], in1=self.vcol(vn, 0, 2).unsqueeze(2).to_broadcast([128, 2, 128]), op=ALU.mult),
                              reads=[BT[0], self.vec], writes=[znT])
                    for (n0, n1, bk) in ((0, 512, 4), (512, 768, 5)):
                        for kc in range(2):
                            fw.op("pe", lambda e: e.matmul(self.bank(bk)[:, 0:n1 - n0], lhsT=znT[:, kc, :], rhs=wqb[:, kc, n0:n1], start=(kc == 0), stop=(kc == 1)),
                                  reads=[znT, wqb], writes=[BT[bk]])
                    qff = qf[:].rearrange("p a b -> p (a b)")
                    fw.op("act", lambda e: e.activation(out=qff[:, 0:512], in_=self.bank(4), func=AF.Copy), reads=[BT[4]], writes=[qf])
                    fw.op("act", lambda e: e.activation(out=qff[:, 512:768], in_=self.bank(5)[:, 0:256], func=AF.Copy), reads=[BT[5]], writes=[qf])
                    for (n0, bk) in ((0, 6), (512, 7)):
                        for kc in range(2):
                            fw.op("pe", lambda e: e.matmul(self.bank(bk), lhsT=znT[:, 2 + kc, :], rhs=wkvb[:, kc, n0:n0 + 512], start=(kc == 0), stop=(kc == 1)),
                                  reads=[znT, wkvb], writes=[BT[bk]])
                    kvv = self.bank2(3).rearrange("p (a b) -> p a b", a=8)
                    fw.op("dve", lambda e: e.tensor_copy(out=kf[:, :, 0:64], in_=kvv[:, :, 0:64]), reads=[BT[6], BT[7]], writes=[kf])
                    fw.op("pool", lambda e: e.tensor_copy(out=kf[:, :, 64:96], in_=zs[:, 512:544].unsqueeze(1).to_broadcast([128, 8, 32])), reads=[zs, kf], writes=[kf])
                    fw.op("act", lambda e: e.activation(out=Vt[:, :, 0:64], in_=kvv[:, :, 64:128], func=AF.Copy), reads=[BT[6], BT[7]], writes=[Vt])
                    fw.dma("pool", self.V_d[b, :, :, tile, :].rearrange("h p e -> p h e"), Vt[:], reads=[Vt], writes=[self.dt("V", b)], disjoint=True)
                    for (xf, gain, xb, xTt, dst, nm) in ((qf, qg, qb, qTt, self.qT_d, "q"), (kf, kg, kb, kTt, self.kT_d, "k")):
                        fw.op("dve", lambda e: e.tensor_tensor(out=sq[:], in0=xf[:], in1=xf[:], op=ALU.mult), reads=[xf], writes=[sq])
                        fw.op("dve", lambda e: e.tensor_reduce(out=st[:, 8:16], in_=sq[:], axis=AX.X, op=ALU.add), reads=[sq], writes=[st])
                        self.rstd(st[:, 8:16], st[:, 16:24], 96, EPS, st)
                        fw.op("dve", lambda e: e.tensor_tensor(out=xf[:], in0=xf[:], in1=st[:, 16:24].unsqueeze(2).to_broadcast([128, 8, 96]), op=ALU.mult),
                              reads=[xf, st], writes=[xf])
                        fw.op("dve", lambda e: e.tensor_tensor(out=xf[:], in0=xf[:], in1=gain[:].unsqueeze(1).to_broadcast([128, 8, 96]), op=ALU.mult),
                              reads=[xf, gain], writes=[xf])
                        self.rope_apply(xf, rp, sw, t1)
                        fw.op("act", lambda e: e.activation(out=xb[:], in_=xf[:], func=AF.Copy), reads=[xf], writes=[xb])
                        pT = self.bank(1).bitcast(BF16).rearrange("p (a b) -> p a b", a=8)
                        for hh in range(8):
                            fw.op("pe", lambda e: e.transpose(out=pT[0:96, hh, :], in_=xb[:, hh, :], identity=self.identb[:]),
                                  reads=[xb, self.identb], writes=[BT[1]])
                        fw.op("dve", lambda e: e.tensor_copy(out=xTt[:], in_=pT[0:96, :, :]), reads=[BT[1]], writes=[xTt])
                        fw.dma("pool", dst[b, :, :, tile * 128:(tile + 1) * 128].rearrange("h p t -> p h t"), xTt[:], reads=[xTt], writes=[self.dt(nm, b)], disjoint=True)
                    pc = self.bank2(2).rearrange("p (a b) -> p a b", a=8)
                    for cc in range(8):
                        for kc in range(8):
                            fw.op("pe", lambda e: e.matmul(pc[:, cc, :], lhsT=winb[:, kc, 544 + cc * 128:544 + (cc + 1) * 128], rhs=aT[:, kc, :], start=(kc == 0), stop=(kc == 7)),
                                  reads=[winb, aT], writes=[BT[4 + cc // 4]])
                    fw.op("act", lambda e: e.activation(out=sg[:], in_=pc[:, 4:8, :], func=AF.Sigmoid), reads=[BT[5]], writes=[sg])
                    fw.op("dve", lambda e: e.tensor_tensor(out=yt[:], in0=pc[:, 0:4, :], in1=sg[:], op=ALU.mult), reads=[BT[4], sg], writes=[yt])
                    off = 15 + tile * 128 if tile < 2 else (15 + TC + 15 + 15 + (tile - 2) * 128)
                    fw.dma("pool", self.yT_d[b, :, :, off:off + 128].rearrange("c p t -> p c t"), yt[:], reads=[yt], writes=[self.dt("y", b)], disjoint=True)
            fw.barrier()
        if self.stop_after == "E1":
            return
        fw.pe_delay = True
        with ExitStack() as ph:
            self.ph = ph
            yin = A("yin", [128, 4, 542], F32)
            acc = A("acc", [128, 4, 512], F32)
            acc_tr = [Track("acc%d" % c) for c in range(4)]
            sq2 = A("sq2", [128, 4, 512], F32)
            mean = A("mean", [128, 512], F32)
            msq = A("msq", [128, 512], F32)
            rs = A("rs", [128, 512], F32)
            cvb = A("cvb", [128, 4, 512], BF16)
            ones = self.cstf[:, 5 * 128:6 * 128]
            cw0, _ = self.lay["convw%d" % j]
            for b in range(2):
                for (c0, Tb, tok0) in [(0, 256, 0)] + [(286 + k * 512, 512, 256 + k * 512) for k in range(4)]:
                    fw.dma("sp", yin[:, :, 0:Tb + 30], self.yT_d[b, :, :, c0:c0 + Tb + 30].rearrange("c p t -> p c t"),
                           reads=[self.dt("y", b), self.dt("ypad")], writes=[yin])
                    for cc in range(4):
                        eng = "dve"
                        wcol = lambda tap: self.vec[:, cw0 + cc * 31 + tap:cw0 + cc * 31 + tap + 1]
                        fw.op(eng, lambda e: e.tensor_scalar(out=acc[:, cc, 0:Tb], in0=yin[:, cc, 0:Tb], scalar1=wcol(0), scalar2=self.vcol("convb%d" % j, cc, 1), op0=ALU.mult, op1=ALU.add),
                              reads=[yin, self.vec], writes=[acc_tr[cc]])
                        for tap in range(1, 31):
                            fw.op(eng, lambda e: e.scalar_tensor_tensor(out=acc[:, cc, 0:Tb], in0=yin[:, cc, tap:tap + Tb], scalar=wcol(tap), in1=acc[:, cc, 0:Tb], op0=ALU.mult, op1=ALU.add),
                                  reads=[yin, acc_tr[cc]], writes=[acc_tr[cc]])
                    fw.op("act", lambda e: e.activation(out=sq2[:, :, 0:Tb], in_=acc[:, :, 0:Tb], func=AF.Square), reads=acc_tr, writes=[sq2])
                    for (src, strk, bk) in ((acc, acc_tr, 2), (sq2, [sq2], 3)):
                        for cc in range(4):
                            fw.op("pe", lambda e: e.matmul(self.bank(bk)[:, 0:Tb], lhsT=ones, rhs=src[:, cc, 0:Tb], start=(cc == 0), stop=(cc == 3)),
                                  reads=list(strk) + [self.cstf], writes=[BT[bk]])
                    fw.op("act", lambda e: e.activation(out=mean[:, 0:Tb], in_=self.bank(2)[:, 0:Tb], func=AF.Copy, scale=1.0 / 512), reads=[BT[2]], writes=[mean])
                    fw.op("dve", lambda e: e.tensor_tensor(out=msq[:, 0:Tb], in0=mean[:, 0:Tb], in1=mean[:, 0:Tb], op=ALU.mult), reads=[mean], writes=[msq])
                    fw.op("dve", lambda e: e.scalar_tensor_tensor(out=rs[:, 0:Tb], in0=self.bank(3)[:, 0:Tb], scalar=1.0 / 512, in1=msq[:, 0:Tb], op0=ALU.mult, op1=ALU.subtract),
                          reads=[BT[3], msq], writes=[rs])
                    self.rstd(rs[:, 0:Tb], rs[:, 0:Tb], 1.0, 1e-5, rs)
                    for cc in range(4):
                        fw.op("dve", lambda e: e.tensor_tensor(out=sq2[:, cc, 0:Tb], in0=acc[:, cc, 0:Tb], in1=mean[:, 0:Tb], op=ALU.subtract),
                              reads=[acc_tr[cc], mean, sq2], writes=[sq2])
                        fw.op("dve", lambda e: e.tensor_tensor(out=sq2[:, cc, 0:Tb], in0=sq2[:, cc, 0:Tb], in1=rs[:, 0:Tb], op=ALU.mult),
                              reads=[rs, sq2], writes=[sq2])
                        fw.op("act", lambda e: e.activation(out=cvb[:, cc, 0:Tb], in_=sq2[:, cc, 0:Tb], func=AF.Silu,
                                                            scale=self.vcol("clng%d" % j, cc, 1), bias=self.vcol("clnb%d" % j, cc, 1)),
                              reads=[sq2, self.vec], writes=[cvb])
                    fw.dma("pool", self.mixT_d[b, :, :, tok0:tok0 + Tb].rearrange("c p t -> p c t"), cvb[:, :, 0:Tb], reads=[cvb], writes=[self.dt("mixc", b)], disjoint=True)
            fw.barrier()
        fw.pe_delay = False
        if self.stop_after == "E2":
            return
        with ExitStack() as ph:
            self.ph = ph
            woutb = A("woutb", [128, 8, 1024], BF16)
            self.load_w(woutb, W["even_w_out"][j], 8, 1024, "pool")
            kTh = [A("kTh%d" % k, [96, TT], BF16) for k in range(2)]
            qTh = [A("qTh%d" % k, [96, TT], BF16) for k in range(2)]
            Vh = [A("Vh%d" % k, [128, NTILE, 68], BF16) for k in range(2)]
            E = [A("E%d" % k, [128, 512], BF16) for k in range(2)]
            oall = A("oall", [128, NTILE, 512], BF16)
            rden = A("rden", [128, 4], F32)
            mixA = A("mixA", [128, 4, 512], BF16)
            mixC = A("mixC", [128, 4, 512], BF16)
            yT = A("yT", [128, 8, 512], F32)
            hs = [A("hs%d" % k, [128, 1024], F32) for k in range(2)]
            for b in range(2):
                for hh in range(8):
                    kt_, qt_, vt_ = kTh[hh % 2], qTh[hh % 2], Vh[hh % 2]
                    fw.dma("sp", kt_[:], self.kT_d[b, hh], reads=[self.dt("k", b)], writes=[kt_])
                    fw.dma("sp", qt_[:], self.qT_d[b, hh], reads=[self.dt("q", b)], writes=[qt_])
                    fw.dma("sp", vt_[:], self.V_d[b, hh], reads=[self.dt("V", b)], writes=[vt_])
                    for (q0, QW, nkt, tile0) in [(0, 256, 2, 0)] + [(256 + k * 512, 512, NTILE, 2 + 4 * k) for k in range(4)]:
                        nqt = QW // 128
                        for kt in range(nkt):
                            bk = kt % 2
                            Ek = E[kt % 2]
                            fw.op("pe", lambda e: e.matmul(self.bank(bk)[:, 0:QW], lhsT=kt_[:, kt * 128:(kt + 1) * 128], rhs=qt_[:, q0:q0 + QW], start=True, stop=True),
                                  reads=[kt_, qt_], writes=[BT[bk]])
                            fw.op("act", lambda e: e.activation(out=Ek[:, 0:QW], in_=self.bank(bk)[:, 0:QW], func=AF.Exp), reads=[BT[bk]], writes=[Ek])
                            for qt in range(nqt):
                                fw.op("pe", lambda e: e.matmul(self.bank(2 + qt)[:, 0:65], lhsT=Ek[:, qt * 128:(qt + 1) * 128], rhs=vt_[:, kt, 0:65], start=(kt == 0), stop=(kt == nkt - 1)),
                                      reads=[Ek, vt_], writes=[BT[2 + qt]])
                        for qt in range(nqt):
                            fw.op("dve", lambda e: e.reciprocal(out=rden[:, qt:qt + 1], in_=self.bank(2 + qt)[:, 64:65]), reads=[BT[2 + qt], rden], writes=[rden])
                            fw.op("dve", lambda e: e.tensor_scalar(out=oall[:, tile0 + qt, hh * 64:(hh + 1) * 64], in0=self.bank(2 + qt)[:, 0:64], scalar1=rden[:, qt:qt + 1], scalar2=None, op0=ALU.mult),
                                  reads=[BT[2 + qt], rden], writes=[oall])
                for (tok0, Tb) in [(0, 256)] + [(256 + k * 512, 512) for k in range(4)]:
                    jc = 2 if tok0 == 0 else b
                    fw.dma("sp", mixC[:, :, 0:Tb], self.mixT_d[b, :, :, tok0:tok0 + Tb].rearrange("c p t -> p c t"), reads=[self.dt("mixc", b)], writes=[mixC])
                    for tt in range(Tb // 128):
                        tile = tok0 // 128 + tt
                        pT = self.bank(0).bitcast(BF16).rearrange("p (a b) -> p a b", a=8)
                        for c in range(4):
                            fw.op("pe", lambda e: e.transpose(out=pT[:, c, :], in_=oall[:, tile, c * 128:(c + 1) * 128], identity=self.identb[:]),
                                  reads=[oall, self.identb], writes=[BT[0]])
                        fw.op("dve", lambda e: e.tensor_copy(out=mixA[:, :, tt * 128:(tt + 1) * 128], in_=pT[:, 0:4, :]), reads=[BT[0]], writes=[mixA])
                    for fc in range(8):
                        bk = 2 + fc % 2
                        for kc in range(8):
                            src = mixA if kc < 4 else mixC
                            fw.op("pe", lambda e: e.matmul(self.bank(bk)[:, 0:Tb], lhsT=woutb[:, kc, fc * 128:(fc + 1) * 128], rhs=src[:, kc % 4, 0:Tb], start=(kc == 0), stop=(kc == 7)),
                                  reads=[woutb, src], writes=[BT[bk]])
                        fw.op("dve", lambda e: e.tensor_scalar(out=yT[:, fc, 0:Tb], in0=self.bank(bk)[:, 0:Tb], scalar1=self.gate1[:, fc, jc:jc + 1], scalar2=None, op0=ALU.mult),
                              reads=[BT[bk], self.gate1], writes=[yT])
                    for tt in range(Tb // 128):
                        tile = tok0 // 128 + tt
                        h = hs[tile % 2]
                        fw.dma("sp", h[:], self.hbuf[b, tile * 128:(tile + 1) * 128, :], reads=[self.dt("h", b, tile)], writes=[h])
                        self.resid_update(b, tile, yT, tt * 128, hs=h)
            fw.barrier()

    def odd_mixer(self, i, last):
        fw, nc = self.fw, self.nc
        j = i // 2
        W = self.W
        A = self.A
        BT = self.BT
        S = self.S
        NEG_E = -math.exp(-0.5)
        seq_off = lambda tile: (1 + tile * 128) if tile < 2 else (1 + TC + 1 + 1 + (tile - 2) * 128)
        with ExitStack() as ph:
            self.ph = ph
            self.n_xnf = A("n_xnf", [128, 1024], F32)
            hs = [A("hs%d" % k, [128, 1024], F32) for k in range(2)]
            aTf = [A("aTf%d" % k, [128, 8, 128], F32) for k in range(2)]
            for b in range(2):
                for tile in range(NTILE):
                    jc = 2 if tile < 2 else b
                    h = hs[tile % 2]
                    at = aTf[tile % 2]
                    fw.dma("sp", h[:], self.hbuf[b, tile * 128:(tile + 1) * 128, :], reads=[self.dt("h", b, tile)], writes=[h])
                    self.norm_T(h, self.scl1, self.shf1, jc, at[:], at, F32)
                    off = seq_off(tile)
                    fw.dma("pool", self.aT_d[b, :, :, off:off + 128].rearrange("c p t -> p c t"), at[:], reads=[at], writes=[self.dt("aT", b)], disjoint=True)
            fw.barrier()
        with ExitStack() as ph:
            self.ph = ph
            wrb = A("wrb", [128, 8, 1024], BF16)
            wkb = A("wkb", [128, 8, 1024], BF16)
            wvb = A("wvb", [128, 8, 1024], BF16)
            self.load_w(wrb, W["rw_wr"][j], 8, 1024, "pool")
            self.load_w(wkb, W["rw_wk"][j], 8, 1024, "act")
            self.load_w(wvb, W["rw_wv"][j], 8, 1024, "pool")
            w1c = A("w1c", [128, 8, 128], BF16)
            a1c = A("a1c", [128, 8, 128], BF16)
            for d in range(2):
                self.load_w(w1c, W["rw_w1"][j, d], 8, 64, "act", col0=d * 64)
                self.load_w(a1c, W["rw_a1"][j, d], 8, 64, "act", col0=d * 64)
            g1b = A("g1b", [128, 8, 160], BF16)
            self.load_w(g1b, W["rw_g1"][j], 8, 160, "act")
            w2s = A("w2s", [128, 1, 1024], BF16)
            a2s = A("a2s", [128, 1, 1024], BF16)
            self.load_w(w2s, W["rw_w2"][j].rearrange("d r n -> (d r) n"), 1, 1024, "pool")
            self.load_w(a2s, W["rw_a2"][j].rearrange("d r n -> (d r) n"), 1, 1024, "pool")
            g2a = A("g2a", [128, 1, 1024], BF16)
            g2b = A("g2b", [32, 1, 1024], BF16)
            self.load_w(g2a, W["rw_g2"][j, 0:128], 1, 1024, "pool")
            self.load_w(g2b, W["rw_g2"][j, 128:160], 1, 1024, "pool", rows=32)
            vres = j > 0
            if vres:
                v1b = A("v1b", [128, 8, 32], BF16)
                self.load_w(v1b, W["rw_v1"][0], 8, 32, "act")
                v2b = A("v2b", [32, 1, 1024], BF16)
                self.load_w(v2b, W["rw_v2"][0], 1, 1024, "pool", rows=32)
            bc = {}
            srcs = {"w0_0": W["rw_w0"][j, 0], "w0_1": W["rw_w0"][j, 1], "a0_0": W["rw_a0"][j, 0], "a0_1": W["rw_a0"][j, 1],
                    "k_k": W["rw_k_k"][j], "k_a": W["rw_k_a"][j]}
            if vres:
                srcs["v0"] = W["rw_v0"][0]
            for nm, src in srcs.items():
                bc[nm] = A("bc_" + nm, [128, 1024], F32)
                fw.dma("sp", bc[nm][:], src.partition_broadcast(128), writes=[bc[nm]])
            bc["omka"] = A("bc_omka", [128, 1024], F32)
            fw.op("dve", lambda e: e.tensor_scalar(out=bc["omka"][:], in0=bc["k_a"][:], scalar1=-1.0, scalar2=1.0, op0=ALU.mult, op1=ALU.add),
                  reads=[bc["k_a"]], writes=[bc["omka"]])
            mu0, _ = self.lay["mu%d" % j]
            muc = A("muc", [128, 6, 8], F32)
            mv = self.vec[:, mu0:mu0 + 96].rearrange("p (d i k) -> p d i k", d=2, i=6)
            fw.op("dve", lambda e: e.tensor_tensor(out=muc[:], in0=mv[:, 0], in1=mv[:, 1], op=ALU.add), reads=[self.vec], writes=[muc])
            fw.op("dve", lambda e: e.tensor_scalar(out=muc[:], in0=muc[:], scalar1=-1.0, scalar2=1.0, op0=ALU.mult, op1=ALU.add), reads=[muc], writes=[muc])
            aTin = A("aTin", [128, 8, 130], F32)
            xm = [A("xm%d" % k, [128, 8, 128], BF16) for k in range(6)]
            tmix = A("tmix", [128, 4, 128], F32)
            tmix_tr = [Track("tmix%d" % k) for k in range(4)]
            xm_tr = [[Track("xm%d_%d" % (m_, k)) for k in range(8)] for m_ in range(6)]
            for m_ in range(6):
                xm[m_].kct = xm_tr[m_]
            l1 = A("l1", [128, 512], BF16)
            l1T = A("l1T", [128, 5, 128], BF16)
            WK = {n: A("W" + n, [128, 1024], F32) for n in ("r", "k", "v", "a", "t", "u", "b", "c", "g", "f")}
            st = A("st", [128, 64], F32)

            def proj(x, wb, pair):
                for half in range(2):
                    bk = 2 * pair + half
                    for kc in range(8):
                        fw.op("pe", lambda e: e.matmul(self.bank(bk), lhsT=x[:, kc, :], rhs=wb[:, kc, half * 512:(half + 1) * 512], start=(kc == 0), stop=(kc == 7)),
                              reads=[x.kct[kc], wb], writes=[BT[bk]])

            def proj2(lhs_list, pair):
                for half in range(2):
                    bk = 2 * pair + half
                    n = len(lhs_list)
                    for q, (lt, ltr, rt, rsl) in enumerate(lhs_list):
                        fw.op("pe", lambda e: e.matmul(self.bank(bk), lhsT=lt, rhs=rt[rsl, 0, half * 512:(half + 1) * 512], start=(q == 0), stop=(q == n - 1)),
                              reads=[ltr, rt], writes=[BT[bk]])

            for b in range(2):
                for tile in range(NTILE):
                    rows = slice(tile * 128, (tile + 1) * 128)
                    off = seq_off(tile)
                    fw.dma("sp", aTin[:], self.aT_d[b, :, :, off - 1:off + 129].rearrange("c p t -> p c t"),
                           reads=[self.dt("aT", b), self.dt("apad")], writes=[aTin])
                    for m in range(6):
                        for kc in range(8):
                            tk = tmix[:, (m * 8 + kc) % 4, :]
                            ttr = tmix_tr[(m * 8 + kc) % 4]
                            fw.op("dve", lambda e: e.tensor_scalar(out=tk, in0=aTin[:, kc, 1:129], scalar1=muc[:, m, kc:kc + 1], scalar2=None, op0=ALU.mult),
                                  reads=[aTin, muc], writes=[ttr])
                            fw.op("dve", lambda e: e.scalar_tensor_tensor(out=tk, in0=aTin[:, kc, 0:128], scalar=mv[:, 0, m, kc:kc + 1], in1=tk, op0=ALU.mult, op1=ALU.add),
                                  reads=[aTin, self.vec, ttr], writes=[ttr])
                            fw.op("dve", lambda e: e.scalar_tensor_tensor(out=xm[m][:, kc, :], in0=aTin[:, kc, 2:130], scalar=mv[:, 1, m, kc:kc + 1], in1=tk, op0=ALU.mult, op1=ALU.add),
                                  reads=[aTin, self.vec, ttr], writes=[xm_tr[m][kc]])
                    xr, xw, xk, xv, xa, xg = xm
                    p0 = self.bank(0)
                    for (x, wb, c0, n) in ((xw, w1c, 0, 128), (xa, a1c, 128, 128), (xg, g1b, 256, 160)) + (((xv, v1b, 416, 32),) if vres else ()):
                        for kc in range(8):
                            fw.op("pe", lambda e: e.matmul(p0[:, c0:c0 + n], lhsT=x[:, kc, :], rhs=wb[:, kc, :], start=(kc == 0), stop=(kc == 7)),
                                  reads=[x.kct[kc], wb], writes=[BT[0]])
                    fw.op("act", lambda e: e.activation(out=l1[:, 0:128], in_=p0[:, 0:128], func=AF.Tanh), reads=[BT[0]], writes=[l1])
                    fw.op("act", lambda e: e.activation(out=l1[:, 128:256], in_=p0[:, 128:256], func=AF.Copy), reads=[BT[0]], writes=[l1])
                    fw.op("act", lambda e: e.activation(out=l1[:, 256:416], in_=p0[:, 256:416], func=AF.Sigmoid), reads=[BT[0]], writes=[l1])
                    if vres:
                        fw.op("act", lambda e: e.activation(out=l1[:, 416:448], in_=p0[:, 416:448], func=AF.Copy), reads=[BT[0]], writes=[l1])
                    pT = self.bank(1).bitcast(BF16).rearrange("p (a b) -> p a b", a=8)
                    for (q, c0, n) in ((0, 0, 128), (1, 128, 128), (2, 256, 128), (3, 384, 32)) + (((4, 416, 32),) if vres else ()):
                        fw.op("pe", lambda e: e.transpose(out=pT[0:n, q, :], in_=l1[:, c0:c0 + n], identity=self.identb[:]),
                              reads=[l1, self.identb], writes=[BT[1]])
                    fw.op("dve", lambda e: e.tensor_copy(out=l1T[:, 0:3, :], in_=pT[:, 0:3, :]), reads=[BT[1]], writes=[l1T])
                    nq = 5 if vres else 4
                    fw.op("dve", lambda e: e.tensor_copy(out=l1T[0:32, 3:nq, :], in_=pT[0:32, 3:nq, :]), reads=[BT[1], l1T], writes=[l1T])
                    proj(xr, wrb, 1)
                    fw.op("act", lambda e: e.activation(out=WK["r"][:], in_=self.bank2(1), func=AF.Copy), reads=[BT[2], BT[3]], writes=[WK["r"]])
                    fw.dma("sp", S["r"][b, rows, :], WK["r"][:], reads=[WK["r"]], writes=[self.dt("s_r", b)], disjoint=True)
                    proj(xk, wkb, 2)
                    fw.op("act", lambda e: e.activation(out=WK["k"][:], in_=self.bank2(2), func=AF.Copy), reads=[BT[4], BT[5]], writes=[WK["k"]])
                    proj(xv, wvb, 3)
                    fw.op("act", lambda e: e.activation(out=WK["v"][:], in_=self.bank2(3), func=AF.Copy), reads=[BT[6], BT[7]], writes=[WK["v"]])
                    if vres:
                        proj2([(l1T[0:32, 4, :], l1T, v2b, slice(0, 32))], 1)
                        fw.op("dve", lambda e: e.tensor_tensor(out=WK["t"][:], in0=self.bank2(1), in1=bc["v0"][:], op=ALU.add), reads=[BT[2], BT[3], bc["v0"]], writes=[WK["t"]])
                        fw.op("act", lambda e: e.activation(out=WK["t"][:], in_=WK["t"][:], func=AF.Sigmoid), reads=[WK["t"]], writes=[WK["t"]])
                        fw.dma("sp", WK["f"][:], S["vf"][b, rows, :], reads=[self.dt("s_vf", b)], writes=[WK["f"]])
                        fw.op("dve", lambda e: e.tensor_tensor(out=WK["f"][:], in0=WK["f"][:], in1=WK["v"][:], op=ALU.subtract), reads=[WK["f"], WK["v"]], writes=[WK["f"]])
                        fw.op("dve", lambda e: e.tensor_tensor(out=WK["f"][:], in0=WK["f"][:], in1=WK["t"][:], op=ALU.mult), reads=[WK["f"], WK["t"]], writes=[WK["f"]])
                        fw.op("dve", lambda e: e.tensor_tensor(out=WK["v"][:], in0=WK["v"][:], in1=WK["f"][:], op=ALU.add), reads=[WK["f"], WK["v"]], writes=[WK["v"]])
                    else:
                        fw.dma("sp", S["vf"][b, rows, :], WK["v"][:], reads=[WK["v"]], writes=[self.dt("s_vf", b)], disjoint=True)
                    fw.dma("sp", S["v"][b, rows, :], WK["v"][:], reads=[WK["v"]], writes=[self.dt("s_v", b)], disjoint=True)
                    fw.op("pool", lambda e: e.tensor_tensor(out=WK["a"][:], in0=WK["k"][:], in1=bc["k_k"][:], op=ALU.mult), reads=[WK["k"], bc["k_k"]], writes=[WK["a"]])
                    fw.op("pool", lambda e: e.tensor_tensor(out=WK["t"][:], in0=WK["a"][:], in1=WK["a"][:], op=ALU.mult), reads=[WK["a"]], writes=[WK["t"]])
                    fw.op("dve", lambda e: e.tensor_reduce(out=st[:, 0:16], in_=WK["t"][:].rearrange("p (h k) -> p h k", h=16), axis=AX.X, op=ALU.add), reads=[WK["t"]], writes=[st])
                    fw.op("act", lambda e: e.activation(out=st[:, 0:16], in_=st[:, 0:16], func=AF.Sqrt), reads=[st], writes=[st])
                    fw.op("dve", lambda e: e.tensor_scalar(out=st[:, 0:16], in0=st[:, 0:16], scalar1=1e-12, scalar2=None, op0=ALU.max), reads=[st], writes=[st])
                    fw.op("dve", lambda e: e.reciprocal(out=st[:, 0:16], in_=st[:, 0:16]), reads=[st], writes=[st])
                    fw.op("dve", lambda e: e.tensor_tensor(out=WK["a"][:].rearrange("p (h k) -> p h k", h=16), in0=WK["a"][:].rearrange("p (h k) -> p h k", h=16),
                                                           in1=st[:, 0:16].unsqueeze(2).to_broadcast([128, 16, 64]), op=ALU.mult), reads=[WK["a"], st], writes=[WK["a"]])
                    fw.dma("sp", S["a"][b, rows, :], WK["a"][:], reads=[WK["a"]], writes=[self.dt("s_a", b)], disjoint=True)
                    for d in range(2):
                        dsl = slice(d * 64, (d + 1) * 64)
                        proj2([(l1T[dsl, 0, :], l1T, w2s, dsl)], 1)
                        fw.op("dve", lambda e: e.tensor_tensor(out=WK["t"][:], in0=self.bank2(1), in1=bc["w0_%d" % d][:], op=ALU.add), reads=[BT[2], BT[3], bc["w0_%d" % d]], writes=[WK["t"]])
                        fw.op("act", lambda e: e.activation(out=WK["t"][:], in_=WK["t"][:], func=AF.Sigmoid), reads=[WK["t"]], writes=[WK["t"]])
                        fw.op("act", lambda e: e.activation(out=WK["t"][:], in_=WK["t"][:], func=AF.Copy, scale=NEG_E), reads=[WK["t"]], writes=[WK["t"]])
                        fw.dma("sp", S["lw%d" % d][b, rows, :], WK["t"][:], reads=[WK["t"]], writes=[self.dt("s_lw%d" % d, b)], disjoint=True)
                        proj2([(l1T[dsl, 1, :], l1T, a2s, dsl)], 2)
                        fw.op("dve", lambda e: e.tensor_tensor(out=WK["u"][:], in0=self.bank2(2), in1=bc["a0_%d" % d][:], op=ALU.add), reads=[BT[4], BT[5], bc["a0_%d" % d]], writes=[WK["u"]])
                        fw.op("act", lambda e: e.activation(out=WK["u"][:], in_=WK["u"][:], func=AF.Sigmoid), reads=[WK["u"]], writes=[WK["u"]])
                        fw.op("pool", lambda e: e.tensor_tensor(out=WK["b"][:], in0=WK["a"][:], in1=WK["u"][:], op=ALU.mult), reads=[WK["a"], WK["u"]], writes=[WK["b"]])
                        fw.dma("sp", S["b%d" % d][b, rows, :], WK["b"][:], reads=[WK["b"]], writes=[self.dt("s_b%d" % d, b)], disjoint=True)
                        fw.op("pool", lambda e: e.tensor_tensor(out=WK["c"][:], in0=WK["u"][:], in1=bc["k_a"][:], op=ALU.mult), reads=[WK["u"], bc["k_a"]], writes=[WK["c"]])
                        fw.op("pool", lambda e: e.tensor_tensor(out=WK["c"][:], in0=WK["c"][:], in1=bc["omka"][:], op=ALU.add), reads=[WK["c"], bc["omka"]], writes=[WK["c"]])
                        fw.op("pool", lambda e: e.tensor_tensor(out=WK["c"][:], in0=WK["c"][:], in1=WK["k"][:], op=ALU.mult), reads=[WK["c"], WK["k"]], writes=[WK["c"]])
                        fw.dma("sp", S["kd%d" % d][b, rows, :], WK["c"][:], reads=[WK["c"]], writes=[self.dt("s_kd%d" % d, b)], disjoint=True)
                    proj2([(l1T[:, 2, :], l1T, g2a, slice(0, 128)), (l1T[0:32, 3, :], l1T, g2b, slice(0, 32))], 3)
                    fw.op("act", lambda e: e.activation(out=WK["g"][:], in_=self.bank2(3), func=AF.Copy), reads=[BT[6], BT[7]], writes=[WK["g"]])
                    fw.dma("sp", S["g"][b, rows, :], WK["g"][:], reads=[WK["g"]], writes=[self.dt("s_g", b)], disjoint=True)
            fw.barrier()
        if self.stop_after == "O2":
            return
        self.scan(i)
        if self.stop_after == "O3":
            return
        with ExitStack() as ph:
            self.ph = ph
            wob = A("wob", [128, 8, 1024], BF16)
            self.load_w(wob, W["rw_wo"][j], 8, 1024, "pool")
            bc = {}
            for nm, src in (("rk0", W["rw_r_k"][j, 0]), ("rk1", W["rw_r_k"][j, 1]), ("lng", W["rw_ln_g"][j]), ("lnb", W["rw_ln_b"][j])):
                bc[nm] = A("bc_" + nm, [128, 1024], F32)
                fw.dma("sp", bc[nm][:], src.partition_broadcast(128), writes=[bc[nm]])
            L2 = [{n: A("L%d" % k + n, [128, 1024], F32) for n in ("y0", "y1", "r", "kd0", "kd1", "v", "g")} for k in range(2)]
            tt_ = A("tt", [128, 1024], F32)
            ob = A("ob", [128, 1024], BF16)
            oT = A("oT", [128, 8, 128], BF16)
            yT = A("yT", [128, 8, 128], F32)
            st = A("st", [128, 64], F32)
            hs = [A("hs%d" % k, [128, 1024], F32) for k in range(2)]
            hv = lambda t: t[:].rearrange("p (h k) -> p h k", h=16)
            b16 = lambda ap: ap.unsqueeze(2).to_broadcast([128, 16, 64])
            for b in range(2):
                for tile in range(NTILE):
                    if last and tile < 2:
                        continue
                    jc = 2 if tile < 2 else b
                    rows = slice(tile * 128, (tile + 1) * 128)
                    L = L2[tile % 2]
                    for n in L:
                        fw.dma("sp", L[n][:], S[n][b, rows, :], reads=[self.dt("s_" + n, b)], writes=[L[n]])
                    y = L["y0"]
                    fw.op("pool", lambda e: e.tensor_tensor(out=y[:], in0=y[:], in1=L["y1"][:], op=ALU.add), reads=[y, L["y1"]], writes=[y])
                    fw.op("dve", lambda e: e.tensor_reduce(out=st[:, 0:16], in_=hv(y), axis=AX.X, op=ALU.add), reads=[y], writes=[st])
                    fw.op("dve", lambda e: e.tensor_scalar(out=st[:, 0:16], in0=st[:, 0:16], scalar1=1.0 / 64, scalar2=None, op0=ALU.mult), reads=[st], writes=[st])
                    fw.op("dve", lambda e: e.tensor_tensor(out=hv(y), in0=hv(y), in1=b16(st[:, 0:16]), op=ALU.subtract), reads=[y, st], writes=[y])
                    fw.op("pool", lambda e: e.tensor_tensor(out=tt_[:], in0=y[:], in1=y[:], op=ALU.mult), reads=[y], writes=[tt_])
                    fw.op("dve", lambda e: e.tensor_reduce(out=st[:, 16:32], in_=hv(tt_), axis=AX.X, op=ALU.add), reads=[tt_], writes=[st])
                    self.rstd(st[:, 16:32], st[:, 32:48], 64, 6.4e-4, st)
                    fw.op("dve", lambda e: e.tensor_tensor(out=hv(y), in0=hv(y), in1=b16(st[:, 32:48]), op=ALU.mult), reads=[y, st], writes=[y])
                    fw.op("pool", lambda e: e.tensor_tensor(out=y[:], in0=y[:], in1=bc["lng"][:], op=ALU.mult), reads=[y, bc["lng"]], writes=[y])
                    fw.op("pool", lambda e: e.tensor_tensor(out=y[:], in0=y[:], in1=bc["lnb"][:], op=ALU.add), reads=[y, bc["lnb"]], writes=[y])
                    fw.op("pool", lambda e: e.tensor_tensor(out=L["kd0"][:], in0=L["kd0"][:], in1=bc["rk0"][:], op=ALU.mult), reads=[L["kd0"], bc["rk0"]], writes=[L["kd0"]])
                    fw.op("pool", lambda e: e.tensor_tensor(out=L["kd1"][:], in0=L["kd1"][:], in1=bc["rk1"][:], op=ALU.mult), reads=[L["kd1"], bc["rk1"]], writes=[L["kd1"]])
                    fw.op("dve", lambda e: e.tensor_tensor(out=L["kd0"][:], in0=L["kd0"][:], in1=L["kd1"][:], op=ALU.add), reads=[L["kd0"], L["kd1"]], writes=[L["kd0"]])
                    fw.op("dve", lambda e: e.tensor_tensor(out=L["kd0"][:], in0=L["kd0"][:], in1=L["r"][:], op=ALU.mult), reads=[L["kd0"], L["r"]], writes=[L["kd0"]])
                    fw.op("dve", lambda e: e.tensor_reduce(out=st[:, 48:64], in_=hv(L["kd0"]), axis=AX.X, op=ALU.add), reads=[L["kd0"]], writes=[st])
                    fw.op("dve", lambda e: e.tensor_tensor(out=hv(L["v"]), in0=hv(L["v"]), in1=b16(st[:, 48:64]), op=ALU.mult), reads=[L["v"], st], writes=[L["v"]])
                    fw.op("pool", lambda e: e.tensor_tensor(out=y[:], in0=y[:], in1=L["v"][:], op=ALU.add), reads=[y, L["v"]], writes=[y])
                    fw.op("dve", lambda e: e.tensor_tensor(out=ob[:], in0=y[:], in1=L["g"][:], op=ALU.mult), reads=[y, L["g"]], writes=[ob])
                    pT = self.bank(0).bitcast(BF16).rearrange("p (a b) -> p a b", a=8)
                    for c in range(8):
                        fw.op("pe", lambda e: e.transpose(out=pT[:, c, :], in_=ob[:, c * 128:(c + 1) * 128], identity=self.identb[:]),
                              reads=[ob, self.identb], writes=[BT[0]])
                    fw.op("act", lambda e: e.activation(out=oT[:], in_=pT, func=AF.Copy), reads=[BT[0]], writes=[oT])
                    for fc in range(8):
                        bk = 2 + fc % 2
                        for kc in range(8):
                            fw.op("pe", lambda e: e.matmul(self.bank(bk)[:, 0:128], lhsT=wob[:, kc, fc * 128:(fc + 1) * 128], rhs=oT[:, kc, :], start=(kc == 0), stop=(kc == 7)),
                                  reads=[wob, oT], writes=[BT[bk]])
                        fw.op("dve", lambda e: e.tensor_scalar(out=yT[:, fc, :], in0=self.bank(bk)[:, 0:128], scalar1=self.gate1[:, fc, jc:jc + 1], scalar2=None, op0=ALU.mult),
                              reads=[BT[bk], self.gate1], writes=[yT])
                    h = hs[tile % 2]
                    fw.dma("sp", h[:], self.hbuf[b, rows, :], reads=[self.dt("h", b, tile)], writes=[h])
                    self.resid_update(b, tile, yT, 0, hs=h)
            fw.barrier()

    def scan(self, i):
        fw, nc = self.fw, self.nc
        A = self.A
        BT = self.BT
        S = self.S
        fw.pe_delay = True
        with ExitStack() as ph:
            self.ph = ph
            Ld = {n: A("S" + n, [128, 1024], F32) for n in ("lw", "kd", "b", "a", "v", "r")}
            cumS = A("cumS", [128, 1024], F32)
            Et = A("Et", [128, 1024], F32)
            Tt = A("Tt", [128, 1024], F32)
            El = {n: A("E" + n, [128, 1024], F32) for n in ("At", "Rt", "Kt", "Bt", "Kh", "Bh")}
            FT = {n: [A("F%s%d" % (n, par), [128, 8, 128], F32) for par in range(2)] for n in ("At", "Rt", "Kt", "Bt")}
            for n in FT:
                for par in range(2):
                    fw.op("pool", lambda e: e.memset(FT[n][par][:], 0.0), writes=[FT[n][par]])
            PC = A("PC", [128, 8], F32)
            STp = [A("ST%d" % par, [128, 8, 64], F32) for par in range(2)]
            LNAMES = ("LabT", "Lab", "LakT", "MrkT", "MrbT", "pA", "pAT", "pB", "pBT")
            Yt = A("Yt", [128, 1024], F32)
            Ytr = [Track("Yt0"), Track("Yt1")]
            Cs = [A("Cs%d" % k, [128, 4, 128], F32) for k in range(2)]
            bmask = lambda lev: self.cstf[:, (6 + lev) * 128:(7 + lev) * 128]

            def alias(t):
                v = T(t[:, 0:1024].rearrange("p (a b) -> p a b", a=8), "alias")
                v.tr = t.tr
                return v
            GR = []
            GR.append({"Lm": {n: A("M0" + n, [128, 8, 128], F32) for n in LNAMES}, "CTm": A("CTm0", [128, 8, 128], F32), "M1": A("M1_0", [128, 8, 128], F32),
                       "Xs": [A("X0_%d" % k, [128, 8, 64], F32) for k in range(2)]})
            lm1 = {}
            extra = [alias(self.stg[0]), alias(self.stg[1]), alias(Tt)]
            for q, n in enumerate(LNAMES):
                lm1[n] = A("M1" + n, [128, 8, 128], F32) if q < 6 else extra[q - 6]
            GR.append({"Lm": lm1, "CTm": alias(cumS), "M1": alias(Et), "Xs": [A("X1_%d" % k, [128, 8, 64], F32) for k in range(2)]})
            cst = self.cstf
            ident = self.identf
            m_lt, m_le, m_gt, m_ge, ones = (cst[:, k * 128:(k + 1) * 128] for k in (1, 2, 3, 4, 5))
            mb = lambda m: m.unsqueeze(1).to_broadcast([128, 8, 128])
            for b in range(2):
                for d in range(2):
                    for par in range(2):
                        fw.op("dve", lambda e: e.memset(STp[par][:], 0.0), reads=[STp[par]], writes=[STp[par]])
                    if d == 0:
                        order = list(range(NTILE))
                        tri = m_le
                        mT_strict, mT_incl, m_strict = m_lt, m_le, m_gt
                    else:
                        order = [1, 0] + list(range(NTILE - 1, 1, -1))
                        tri = m_ge
                        mT_strict, mT_incl, m_strict = m_gt, m_ge, m_lt
                    for ch in order:
                        rows = slice(ch * 128, (ch + 1) * 128)
                        for n, src in (("lw", "lw%d" % d), ("kd", "kd%d" % d), ("b", "b%d" % d), ("a", "a"), ("r", "r"), ("v", "v")):
                            fw.dma("sp", Ld[n][:], S[src][b, rows, :], reads=[self.dt("s_" + src, b)], writes=[Ld[n]])
                        lw = Ld["lw"]
                        for half in range(2):
                            cs = slice(half * 512, (half + 1) * 512)
                            fw.op("pe", lambda e: e.matmul(self.bank(half), lhsT=tri, rhs=lw[:, cs], start=True, stop=True), reads=[cst, lw], writes=[BT[half]])
                            fw.op("pe", lambda e: e.matmul(self.bank(2 + half), lhsT=ones, rhs=lw[:, cs], start=True, stop=True), reads=[cst, lw], writes=[BT[2 + half]])
                        for hp in range(8):
                            fw.op("pe", lambda e: e.matmul(self.bank(4)[:, 2 * hp:2 * hp + 2], lhsT=lw[:, hp * 128:(hp + 1) * 128], rhs=ones[:, 0:2], start=True, stop=True),
                                  reads=[cst, lw], writes=[BT[4]])
                        fw.op("act", lambda e: e.activation(out=PC[:], in_=self.bank(4)[:, 0:16].rearrange("p (a b) -> p a b", b=2)[:, :, 0], func=AF.Exp), reads=[BT[4]], writes=[PC])
                        fw.op("act", lambda e: e.activation(out=cumS[:], in_=self.bank2(0), func=AF.Copy), reads=[BT[0], BT[1]], writes=[cumS])
                        fw.op("act", lambda e: e.activation(out=Et[:], in_=cumS[:], func=AF.Exp), reads=[cumS], writes=[Et])
                        fw.op("pool", lambda e: e.tensor_tensor(out=El["Rt"][:], in0=Ld["r"][:], in1=Et[:], op=ALU.mult), reads=[Ld["r"], Et], writes=[El["Rt"]])
                        fw.op("dve", lambda e: e.tensor_tensor(out=Tt[:], in0=cumS[:], in1=lw[:], op=ALU.subtract), reads=[cumS, lw], writes=[Tt])
                        fw.op("act", lambda e: e.activation(out=Tt[:], in_=Tt[:], func=AF.Exp), reads=[Tt], writes=[Tt])
                        fw.op("pool", lambda e: e.tensor_tensor(out=El["At"][:], in0=Ld["a"][:], in1=Tt[:], op=ALU.mult), reads=[Ld["a"], Tt], writes=[El["At"]])
                        fw.op("act", lambda e: e.activation(out=Et[:], in_=cumS[:], func=AF.Exp, scale=-1.0), reads=[cumS], writes=[Et])
                        fw.op("pool", lambda e: e.tensor_tensor(out=El["Kt"][:], in0=Ld["kd"][:], in1=Et[:], op=ALU.mult), reads=[Ld["kd"], Et], writes=[El["Kt"]])
                        fw.op("dve", lambda e: e.tensor_tensor(out=El["Bt"][:], in0=Ld["b"][:], in1=Et[:], op=ALU.mult), reads=[Ld["b"], Et], writes=[El["Bt"]])
                        fw.op("dve", lambda e: e.tensor_tensor(out=Tt[:], in0=self.bank2(1), in1=cumS[:], op=ALU.subtract), reads=[BT[2], BT[3], cumS, Tt], writes=[Tt])
                        fw.op("act", lambda e: e.activation(out=Tt[:], in_=Tt[:], func=AF.Exp), reads=[Tt], writes=[Tt])
                        fw.op("pool", lambda e: e.tensor_tensor(out=El["Kh"][:], in0=Ld["kd"][:], in1=Tt[:], op=ALU.mult), reads=[Ld["kd"], Tt], writes=[El["Kh"]])
                        fw.op("dve", lambda e: e.scalar_tensor_tensor(out=El["Bh"][:], in0=Ld["b"][:], scalar=-1.0, in1=Tt[:], op0=ALU.mult, op1=ALU.mult),
                              reads=[Ld["b"], Tt], writes=[El["Bh"]])
                        for q, n in enumerate(("At", "Rt", "Kt", "Bt")):
                            pair = 2 + (q % 2)
                            pv = self.bank2(pair).rearrange("p (a b) -> p a b", a=8)
                            for hp in range(8):
                                fw.op("pe", lambda e: e.transpose(out=pv[:, hp, :], in_=El[n][:, hp * 128:(hp + 1) * 128], identity=ident[:]),
                                      reads=[El[n], ident], writes=[BT[2 * pair], BT[2 * pair + 1]])
                            for par in range(2):
                                ps = slice(par * 64, (par + 1) * 64)
                                if par == 0:
                                    fw.op("act", lambda e: e.activation(out=FT[n][par][ps], in_=pv[ps], func=AF.Copy), reads=[BT[2 * pair], BT[2 * pair + 1]], writes=[FT[n][par]])
                                else:
                                    fw.op("dve", lambda e: e.tensor_copy(out=FT[n][par][ps], in_=pv[ps]), reads=[BT[2 * pair], BT[2 * pair + 1]], writes=[FT[n][par]])
                        def group_gen(g):
                            R = GR[g]
                            Lm, CTm, M1, Xs_, Csg = R["Lm"], R["CTm"], R["M1"], R["Xs"], Cs[g]
                            pA, pB = 2 * g, 2 * g + 1
                            kb = 4 * g

                            def hd(hl):
                                h = g * 8 + hl
                                return h, h // 2, h % 2

                            def scores(dst, lname, rname, pair, mask, neg=False):
                                pv = self.bank2(pair).rearrange("p (a b) -> p a b", a=8)
                                trs = [BT[2 * pair], BT[2 * pair + 1]]
                                for hl in range(8):
                                    h, hp, rs = hd(hl)
                                    fw.op("pe", lambda e: e.matmul(pv[:, hl, :], lhsT=FT[lname][rs][:, hp, :], rhs=FT[rname][rs][:, hp, :], start=True, stop=True),
                                          reads=[FT[lname][rs], FT[rname][rs]], writes=trs)
                                if neg:
                                    fw.op("dve", lambda e: e.scalar_tensor_tensor(out=dst[:], in0=pv, scalar=-1.0, in1=mb(mask), op0=ALU.mult, op1=ALU.mult),
                                          reads=trs + [cst], writes=[dst])
                                else:
                                    fw.op("dve", lambda e: e.tensor_tensor(out=dst[:], in0=pv, in1=mb(mask), op=ALU.mult), reads=trs + [cst], writes=[dst])
                            scores(Lm["LabT"], "Bt", "At", pA, mT_strict)
                            scores(Lm["Lab"], "At", "Bt", pB, m_strict)
                            yield
                            Tm, TTm = Lm["pA"], Lm["pAT"]
                            for (dst, src) in ((Tm, Lm["Lab"]), (TTm, Lm["LabT"])):
                                fw.op("dve", lambda e: e.scalar_tensor_tensor(out=dst[:], in0=src[:], scalar=-1.0, in1=mb(bmask(0)), op0=ALU.mult, op1=ALU.mult),
                                      reads=[src, cst], writes=[dst])
                                fw.op("dve", lambda e: e.tensor_tensor(out=dst[:], in0=dst[:], in1=mb(ident[:]), op=ALU.add), reads=[dst, cst], writes=[dst])
                            scores(Lm["LakT"], "Kt", "At", pA, mT_strict)
                            yield
                            scores(Lm["MrkT"], "Kt", "Rt", pB, mT_incl)
                            scores(Lm["MrbT"], "Bt", "Rt", pA, mT_incl, neg=True)
                            yield
                            bx = self.bank(kb + 2).rearrange("p (a b) -> p a b", a=8)
                            for hl in range(8):
                                h, hp, rs = hd(hl)
                                fw.op("pe", lambda e: e.matmul(bx[:, hl, :], lhsT=FT["At"][rs][:, hp, :], rhs=STp[rs][:, hp, :], start=True, stop=False),
                                      reads=[FT["At"][rs], STp[rs]], writes=[BT[kb + 2]])
                                fw.op("pe", lambda e: e.matmul(bx[:, hl, :], lhsT=Lm["LakT"][:, hl, :], rhs=Ld["v"][:, h * 64:(h + 1) * 64], start=False, stop=True),
                                      reads=[Lm["LakT"], Ld["v"]], writes=[BT[kb + 2]])
                            cur, nxt = Xs_[0], Xs_[1]
                            fw.op("act", lambda e: e.activation(out=cur[:], in_=bx, func=AF.Copy), reads=[BT[kb + 2]], writes=[cur])
                            spare = [(Lm["pA"], Lm["pAT"]), (Lm["pB"], Lm["pBT"])]
                            for lev in range(1, 7):
                                nT, nTT = spare[lev % 2]
                                fw.op("pool", lambda e: e.tensor_tensor(out=CTm[:], in0=Lm["LabT"][:], in1=mb(bmask(lev)), op=ALU.mult), reads=[Lm["LabT"], cst], writes=[CTm])
                                p1 = self.bank2(pA).rearrange("p (a b) -> p a b", a=8)
                                t1 = [BT[2 * pA], BT[2 * pA + 1]]
                                for hl in range(8):
                                    fw.op("pe", lambda e: e.matmul(p1[:, hl, :], lhsT=CTm[:, hl, :], rhs=Tm[:, hl, :], start=True, stop=True),
                                          reads=[CTm, Tm], writes=t1)
                                yield
                                fw.op("act", lambda e: e.activation(out=M1[:], in_=p1, func=AF.Copy), reads=t1, writes=[M1])
                                prt = self.bank2(pB).rearrange("p (a b) -> p a b", a=8)
                                t2 = [BT[2 * pB], BT[2 * pB + 1]]
                                for hl in range(8):
                                    fw.op("pe", lambda e: e.matmul(prt[:, hl, :], lhsT=M1[:, hl, :], rhs=TTm[:, hl, :], start=True, stop=True),
                                          reads=[M1, TTm], writes=t2)
                                if lev < 6:
                                    for hl in range(8):
                                        fw.op("pe", lambda e: e.matmul(p1[:, hl, :], lhsT=TTm[:, hl, :], rhs=M1[:, hl, :], start=True, stop=True),
                                              reads=[M1, TTm], writes=t1)
                                yield
                                fw.op("dve", lambda e: e.tensor_tensor(out=nTT[:], in0=TTm[:], in1=prt, op=ALU.subtract), reads=[TTm] + t2, writes=[nTT])
                                if lev < 6:
                                    fw.op("dve", lambda e: e.tensor_tensor(out=nT[:], in0=Tm[:], in1=p1, op=ALU.subtract), reads=[Tm] + t1, writes=[nT])
                                Tm, TTm = nT, nTT
                            ba = self.bank(kb + 3).rearrange("p (a b) -> p a b", a=8)
                            for hl in range(8):
                                fw.op("pe", lambda e: e.matmul(ba[:, hl, :], lhsT=TTm[:, hl, :], rhs=cur[:, hl, :], start=True, stop=True),
                                      reads=[TTm, cur], writes=[BT[kb + 3]])
                            yield
                            fw.op("act", lambda e: e.activation(out=nxt[:], in_=ba, func=AF.Copy), reads=[BT[kb + 3]], writes=[nxt])
                            U = nxt
                            by = self.bank(kb + 2).rearrange("p (a b) -> p a b", a=8)
                            for hl in range(8):
                                h, hp, rs = hd(hl)
                                fw.op("pe", lambda e: e.matmul(by[:, hl, :], lhsT=FT["Rt"][rs][:, hp, :], rhs=STp[rs][:, hp, :], start=True, stop=False),
                                      reads=[FT["Rt"][rs], STp[rs]], writes=[BT[kb + 2]])
                                fw.op("pe", lambda e: e.matmul(by[:, hl, :], lhsT=Lm["MrkT"][:, hl, :], rhs=Ld["v"][:, h * 64:(h + 1) * 64], start=False, stop=False),
                                      reads=[Lm["MrkT"], Ld["v"]], writes=[BT[kb + 2]])
                                fw.op("pe", lambda e: e.matmul(by[:, hl, :], lhsT=Lm["MrbT"][:, hl, :], rhs=U[:, hl, :], start=False, stop=True),
                                      reads=[Lm["MrbT"], U], writes=[BT[kb + 2]])
                            fw.op("act", lambda e: e.activation(out=Yt[:, g * 512:(g + 1) * 512], in_=self.bank(kb + 2), func=AF.Copy), reads=[BT[kb + 2]], writes=[Ytr[g]])
                            bs = self.bank(kb).rearrange("p (a b) -> p a b", a=4)
                            for hq in range(4):
                                hp = g * 4 + hq
                                fw.op("pe", lambda e: e.matmul(bs[:, hq, :], lhsT=El["Kh"][:, hp * 128:(hp + 1) * 128], rhs=Ld["v"][:, hp * 128:(hp + 1) * 128], start=True, stop=False),
                                      reads=[El["Kh"], Ld["v"]], writes=[BT[kb]])
                                fw.op("pe", lambda e: e.matmul(bs[:, hq, :], lhsT=El["Bh"][:, hp * 128:(hp + 1) * 128], rhs=U[:, 2 * hq:2 * hq + 2, :].rearrange("p a b -> p (a b)"), start=False, stop=True),
                                      reads=[El["Bh"], U], writes=[BT[kb]])
                            fw.op("act", lambda e: e.activation(out=Csg[:], in_=bs, func=AF.Copy), reads=[BT[kb]], writes=[Csg])

                        gens = [group_gen(0), group_gen(1)]
                        while gens:
                            for gen in list(gens):
                                try:
                                    next(gen)
                                except StopIteration:
                                    gens.remove(gen)
                        for par in range(2):
                            ps = slice(par * 64, (par + 1) * 64)
                            fw.op("dve", lambda e: e.tensor_tensor(out=STp[par][ps], in0=STp[par][ps], in1=PC[ps].unsqueeze(2).to_broadcast([64, 8, 64]), op=ALU.mult),
                                  reads=[STp[par], PC], writes=[STp[par]])
                            for g in range(2):
                                fw.op("dve", lambda e: e.tensor_tensor(out=STp[par][ps, g * 4:(g + 1) * 4, :], in0=STp[par][ps, g * 4:(g + 1) * 4, :], in1=Cs[g][ps, :, par * 64:(par + 1) * 64], op=ALU.add),
                                      reads=[STp[par], Cs[g]], writes=[STp[par]])
                        fw.dma("pool", S["y%d" % d][b, rows, :], Yt[:], reads=Ytr, writes=[self.dt("s_y%d" % d, b)], disjoint=True)
            fw.barrier()
        fw.pe_delay = False


def full_plan():
    plan = []
    for i in range(DEPTH):
        plan.append(("mod", i))
        plan.append(("even" if i % 2 == 0 else "odd", i))
        plan.append(("ffn", i))
    return plan


def build_program(plan):
    B = Builder()
    B.setup()
    for step, i in plan:
        last = i == DEPTH - 1
        if step == "mod":
            B.modulation(i)
        elif step == "ffn":
            B.ffn(i, last)
        elif step == "even":
            B.even_mixer(i, last)
        elif step == "odd":
            B.odd_mixer(i, last)
        elif step == "reload":
            B.reload()
    B.fw.finish([B.dt("h", b, t) for b in range(2) for t in range(NTILE)])
    return B


def fm(v):
    v = np.asarray(v, np.float32).reshape(-1, 128)
    return np.ascontiguousarray(v.T)


def host_consts(inputs):
    lay, nv = vec_layout()
    vecs = np.zeros((128, nv), np.float32)

    def put(name, arr):
        o, n = lay[name]
        assert arr.shape == (128, n), (name, arr.shape, n)
        vecs[:, o:o + n] = arr
    for i in range(DEPTH):
        put("nmix%d" % i, fm(inputs["norm_mix"][i]))
        put("nffn%d" % i, fm(inputs["norm_ffn"][i]))
    for j in range(2):
        put("qn%d" % j, fm(inputs["mla_q_norm"][j]))
        put("kvn%d" % j, fm(inputs["mla_kv_norm"][j]))
        cw = np.asarray(inputs["conv_w"][j], np.float32)
        put("convw%d" % j, np.ascontiguousarray(cw.reshape(31, 4, 128).transpose(2, 1, 0).reshape(128, 4 * 31)))
        put("convb%d" % j, fm(inputs["conv_b"][j]))
        put("clng%d" % j, fm(inputs["conv_ln_g"][j]))
        put("clnb%d" % j, fm(inputs["conv_ln_b"][j]))
        mu = np.asarray(inputs["rw_mu"][j], np.float32)
        put("mu%d" % j, np.ascontiguousarray(mu.reshape(12, 8, 128).transpose(2, 0, 1).reshape(128, 96)))
    s = np.arange(128)[:, None]
    t = np.arange(128)[None, :]
    bms = []
    for lev in range(7):
        n = 1 << lev
        bms.append(((s // (2 * n)) == (t // (2 * n))) & ((s // n) != (t // n)))
    cst = np.concatenate([np.eye(128), s < t, s <= t, s > t, s >= t, np.ones((128, 128))] + bms, axis=1).astype(np.float32)
    half = 4
    inv = 10000.0 ** (-np.arange(0, 16, 2, dtype=np.float32) / 16.0)
    tl = np.arange(TL)
    row = (tl // GW).astype(np.float32)[:, None] * inv
    col = (tl % GW).astype(np.float32)[:, None] * inv
    cr, sr, cc, sc = np.cos(row), np.sin(row), np.cos(col), np.sin(col)
    cos = np.concatenate([cr, cr, cc, cc], axis=1)
    sin = np.concatenate([-sr, sr, -sc, sc], axis=1)
    rope = np.zeros((TT, 64), np.float32)
    rope[:TC, 0:32] = 1.0
    rope[TC:, 0:32] = cos
    rope[TC:, 32:64] = sin
    return vecs, cst, rope


def core_inputs(inputs, core, vecs, cst, rope, hin_override=None):
    b0 = 2 * core
    x = np.asarray(inputs["x"], np.float32)
    ctx = np.asarray(inputs["ctx"], np.float32)
    if hin_override is not None:
        hin = hin_override
    else:
        hin = np.concatenate([ctx[b0:b0 + 2], x[b0:b0 + 2]], axis=1)
    cond = np.stack([inputs["c"][b0], inputs["c"][b0 + 1], inputs["c_ctx"]]).astype(np.float32)
    condT = np.ascontiguousarray(cond.reshape(3, 8, 128).transpose(2, 1, 0))
    m = {"hin": np.ascontiguousarray(hin), "condT": condT, "cst": cst, "rope": rope, "vecs": vecs}
    for n in W_NAMES:
        m[n] = np.ascontiguousarray(np.asarray(inputs[n], np.float32).reshape(W_SHAPES[n]))
    return m


def kernel(**inputs):
    vecs, cst, rope = host_consts(inputs)
    B = build_program(full_plan())
    in_maps = [core_inputs(inputs, c, vecs, cst, rope) for c in range(8)]
    res = run_bass_kernel_spmd(B.nc, in_maps, core_ids=list(range(8)))
    outs = [r["hout"][:, TC:, :] for r in res.results]
    return np.ascontiguousarray(np.concatenate(outs, axis=0).astype(np.float32))
```
